# Optimizing a Trainium2 kernel written in Bass

```python
import jax, jax.numpy as jnp
from jax import lax
import numpy as np

D_MODEL = 1024
BATCH = 8
SEQ = 2048
DEPTH = 1

MEM_LEN = 256
HEAD_DIM = 64
Q_BLOCK = 128
EPS = 1e-6
SB_HEADS = 8
SB_WIDTH = SB_HEADS * HEAD_DIM
DIL_CONFIG = ((128, 1), (512, 4), (2048, 16))
N_DIL_GROUPS = 3
DIL_HEADS_PER_GROUP = 4
DIL_HEADS = N_DIL_GROUPS * DIL_HEADS_PER_GROUP
DIL_WIDTH = DIL_HEADS * HEAD_DIM
DIL_OUT_WIDTH = DIL_HEADS_PER_GROUP * HEAD_DIM
DIL_STREAM_KEYS = 128
ALIBI_MAX_BIAS = 8.0
MEM_HEADS = 4
MEM_HEAD_DIM = 128
MEM_WIDTH = MEM_HEADS * MEM_HEAD_DIM
N_BRANCHES = 3
IN_WIDTH = 3 * SB_WIDTH + 3 * DIL_WIDTH + MEM_WIDTH
IN_SPLITS = (SB_WIDTH, 2 * SB_WIDTH, 3 * SB_WIDTH,
             3 * SB_WIDTH + DIL_WIDTH, 3 * SB_WIDTH + 2 * DIL_WIDTH, 3 * SB_WIDTH + 3 * DIL_WIDTH)
PEER_HEADS = 8
PEER_KEYS = 128
PEER_EXPERTS = PEER_KEYS * PEER_KEYS
PEER_TOPK = 16
PEER_QDIM = 256
PEER_HALF = PEER_QDIM // 2
PEER_CHUNK = 128

kernel_name = 'hybrid_sb_dilated_mem_peer_block'


def rms_norm(x, g):
    xf = x.astype(jnp.float32)
    y = xf * lax.rsqrt(jnp.mean(xf * xf, axis=-1, keepdims=True) + EPS)
    return (y * g.astype(jnp.float32)).astype(x.dtype)


def alibi_slopes(n):
    return jnp.exp2(-ALIBI_MAX_BIAS * jnp.arange(1, n + 1, dtype=jnp.float32) / n)


def stick_breaking_attention(q, k, v):
    B, H, S, dh = q.shape
    nb = S // Q_BLOCK
    q_blocks = q.reshape(B, H, nb, Q_BLOCK, dh).transpose(2, 0, 1, 3, 4)
    kpos = jnp.arange(S)
    scale = dh ** -0.5

    def block(args):
        q_blk, n = args
        z = jnp.einsum('bhqd,bhkd->bhqk', q_blk, k).astype(jnp.float32) * scale
        qpos = n * Q_BLOCK + jnp.arange(Q_BLOCK)
        past = kpos[None, :] < qpos[:, None]
        log_keep = jnp.where(past, jax.nn.log_sigmoid(-z), 0.0)
        between = lax.cumsum(log_keep, axis=3, reverse=True) - log_keep
        a = jnp.where(past, jnp.exp(jax.nn.log_sigmoid(z) + between), 0.0)
        return jnp.einsum('bhqk,bhkd->bhqd', a.astype(v.dtype), v)

    o = lax.map(block, (q_blocks, jnp.arange(nb)))
    return o.transpose(1, 2, 0, 3, 4).reshape(B, H, S, dh)


def dilated_group_attention(q, k, v, dilation, slopes):
    B, S, H, dh = q.shape
    L = S // dilation
    nb = -(-L // Q_BLOCK)
    Lp = nb * Q_BLOCK

    def to_streams(t):
        t = t.reshape(B, L, dilation, H, dh).transpose(0, 2, 3, 1, 4)
        t = jnp.pad(t, ((0, 0), (0, 0), (0, 0), (0, Lp - L), (0, 0)))
        return t.reshape(B, dilation, H, nb, Q_BLOCK, dh)

    qs, ks, vs = to_streams(q), to_streams(k), to_streams(v)

    def with_prev(t):
        prev = jnp.pad(t, ((0, 0), (0, 0), (0, 0), (1, 0), (0, 0), (0, 0)))[:, :, :, :nb]
        return jnp.concatenate([prev, t], axis=4)

    kc, vc = with_prev(ks), with_prev(vs)
    qi = jnp.arange(Q_BLOCK)
    kj = jnp.arange(2 * Q_BLOCK) - Q_BLOCK
    gap = qi[:, None] - kj[None, :]
    blk = jnp.arange(nb)
    valid = ((gap >= 0) & (gap <= DIL_STREAM_KEYS))[None] & \
            ((blk[:, None, None] * Q_BLOCK + kj[None, None, :]) >= 0)
    s = jnp.einsum('brhnqd,brhnkd->brhnqk', qs, kc).astype(jnp.float32) * dh ** -0.5
    s = s - slopes[:, None, None, None] * (gap * dilation).astype(jnp.float32)
    s = jnp.where(valid, s, -1e30)
    lse = jax.nn.logsumexp(s, axis=-1)
    p = jnp.exp(s - lse[..., None])
    o = jnp.einsum('brhnqk,brhnkd->brhnqd', p.astype(v.dtype), vc)
    o = o.reshape(B, dilation, H, Lp, dh)[:, :, :, :L].transpose(0, 3, 1, 2, 4).reshape(B, S, H, dh)
    lse = lse.reshape(B, dilation, H, Lp)[..., :L].transpose(0, 3, 1, 2).reshape(B, S, H)
    return o, lse


def dilated_mixture(q, k, v, g_q, g_k):
    q = rms_norm(q, g_q)
    k = rms_norm(k, g_k)
    slopes = alibi_slopes(DIL_HEADS)
    outs, lses = [], []
    for g, (window, dilation) in enumerate(DIL_CONFIG):
        hs = slice(g * DIL_HEADS_PER_GROUP, (g + 1) * DIL_HEADS_PER_GROUP)
        o, lse = dilated_group_attention(q[:, :, hs], k[:, :, hs], v[:, :, hs], dilation, slopes[hs])
        outs.append(o)
        lses.append(lse)
    w = jax.nn.softmax(jnp.stack(lses, axis=2), axis=2)
    o = jnp.stack(outs, axis=2)
    return jnp.sum(w[..., None].astype(o.dtype) * o, axis=2)


def memory_attention(q, mem_h, w_kv, g_q, g_k):
    B, M, _ = mem_h.shape
    kv = (mem_h @ w_kv).reshape(B, M, 2, MEM_HEADS, MEM_HEAD_DIM)
    k = rms_norm(kv[:, :, 0], g_k)
    v = kv[:, :, 1]
    q = rms_norm(q, g_q)
    s = jnp.einsum('bshd,bmhd->bhsm', q, k).astype(jnp.float32) * MEM_HEAD_DIM ** -0.5
    p = jax.nn.softmax(s, axis=-1)
    return jnp.einsum('bhsm,bmhd->bshd', p.astype(v.dtype), v)


def peer_ffn(h, w_q, subkeys, u, v):
    B, S, D = h.shape
    T = B * S
    hf = h.reshape(T, D)
    q = (hf @ w_q).reshape(T, PEER_HEADS, 2, PEER_HALF)
    sc = jnp.einsum('thpd,hpkd->thpk', q, subkeys).astype(jnp.float32)
    s1, i1 = lax.top_k(sc[:, :, 0], PEER_TOPK)
    s2, i2 = lax.top_k(sc[:, :, 1], PEER_TOPK)
    cand = (s1[..., :, None] + s2[..., None, :]).reshape(T, PEER_HEADS, PEER_TOPK * PEER_TOPK)
    cand_idx = (i1[..., :, None] * PEER_KEYS + i2[..., None, :]).reshape(T, PEER_HEADS, PEER_TOPK * PEER_TOPK)
    top_s, pos = lax.top_k(cand, PEER_TOPK)
    idx = jnp.take_along_axis(cand_idx, pos, axis=-1)
    gate = jax.nn.softmax(top_s, axis=-1)
    nc = T // PEER_CHUNK

    def chunk(args):
        hc, ic, gc = args
        act = jax.nn.gelu(jnp.einsum('chkd,cd->chk', u[ic], hc).astype(jnp.float32), approximate=False)
        w = (gc * act).astype(h.dtype)
        return jnp.einsum('chk,chkd->cd', w, v[ic])

    out = lax.map(chunk, (hf.reshape(nc, PEER_CHUNK, D),
                          idx.reshape(nc, PEER_CHUNK, PEER_HEADS, PEER_TOPK),
                          gate.reshape(nc, PEER_CHUNK, PEER_HEADS, PEER_TOPK)))
    return out.reshape(B, S, D)


def setup_inputs(seed: int = 0) -> dict:
    key = jax.random.key(seed)
    ks = jax.random.split(key, 24)
    D = D_MODEL

    def nrm(k, shape, scale):
        return jax.random.normal(k, shape, jnp.float32) * scale

    def gain(k, n):
        return 1.0 + 0.02 * jax.random.normal(k, (DEPTH, n), jnp.float32)

    return {
        'x': nrm(ks[0], (BATCH, SEQ, D), 1.0),
        'mem': nrm(ks[1], (BATCH, MEM_LEN, D), 1.0),
        'g_mix': gain(ks[2], D),
        'g_mem': gain(ks[3], D),
        'w_in': nrm(ks[4], (DEPTH, D, IN_WIDTH), D ** -0.5),
        'w_mem_kv': nrm(ks[5], (DEPTH, D, 2 * MEM_WIDTH), D ** -0.5),
        'g_q_dil': gain(ks[6], HEAD_DIM),
        'g_k_dil': gain(ks[7], HEAD_DIM),
        'g_q_mem': gain(ks[8], MEM_HEAD_DIM),
        'g_k_mem': gain(ks[9], MEM_HEAD_DIM),
        'w_o_sb': nrm(ks[10], (DEPTH, SB_WIDTH, D), SB_WIDTH ** -0.5),
        'w_o_dil': nrm(ks[11], (DEPTH, DIL_OUT_WIDTH, D), DIL_OUT_WIDTH ** -0.5),
        'w_o_mem': nrm(ks[12], (DEPTH, MEM_WIDTH, D), MEM_WIDTH ** -0.5),
        'w_gate': nrm(ks[13], (DEPTH, D, N_BRANCHES * D), D ** -0.5),
        'b_gate': nrm(ks[14], (DEPTH, N_BRANCHES * D), 0.02),
        'w_out': nrm(ks[15], (DEPTH, D, D), D ** -0.5),
        'g_ffn': gain(ks[16], D),
        'w_peer_q': nrm(ks[17], (DEPTH, D, PEER_HEADS * PEER_QDIM), D ** -0.5),
        'peer_subkeys': nrm(ks[18], (DEPTH, PEER_HEADS, 2, PEER_KEYS, PEER_HALF), PEER_HALF ** -0.5),
        'peer_u': nrm(ks[19], (DEPTH, PEER_EXPERTS, D), D ** -0.5),
        'peer_v': nrm(ks[20], (DEPTH, PEER_EXPERTS, D), (PEER_HEADS * PEER_TOPK) ** -0.5),
    }


def reference(x, mem, g_mix, g_mem, w_in, w_mem_kv, g_q_dil, g_k_dil, g_q_mem, g_k_mem,
              w_o_sb, w_o_dil, w_o_mem, w_gate, b_gate, w_out, g_ffn, w_peer_q,
              peer_subkeys, peer_u, peer_v):
    B, S, D = x.shape
    for l in range(DEPTH):
        h = rms_norm(x, g_mix[l])
        proj = h @ w_in[l]
        sb_q, sb_k, sb_v, d_q, d_k, d_v, m_q = jnp.split(proj, IN_SPLITS, axis=-1)

        def sb_heads(t):
            return t.reshape(B, S, SB_HEADS, HEAD_DIM).transpose(0, 2, 1, 3)

        y_sb = stick_breaking_attention(sb_heads(sb_q), sb_heads(sb_k), sb_heads(sb_v))
        y_sb = y_sb.transpose(0, 2, 1, 3).reshape(B, S, SB_WIDTH) @ w_o_sb[l]

        def dil_heads(t):
            return t.reshape(B, S, DIL_HEADS, HEAD_DIM)

        y_dil = dilated_mixture(dil_heads(d_q), dil_heads(d_k), dil_heads(d_v), g_q_dil[l], g_k_dil[l])
        y_dil = y_dil.reshape(B, S, DIL_OUT_WIDTH) @ w_o_dil[l]

        mem_h = rms_norm(mem, g_mem[l])
        y_mem = memory_attention(m_q.reshape(B, S, MEM_HEADS, MEM_HEAD_DIM), mem_h, w_mem_kv[l],
                                 g_q_mem[l], g_k_mem[l])
        y_mem = y_mem.reshape(B, S, MEM_WIDTH) @ w_o_mem[l]

        gates = jax.nn.sigmoid(h @ w_gate[l] + b_gate[l]).reshape(B, S, N_BRANCHES, D)
        merged = gates[:, :, 0] * y_sb + gates[:, :, 1] * y_dil + gates[:, :, 2] * y_mem
        x = x + merged @ w_out[l]

        h2 = rms_norm(x, g_ffn[l])
        x = x + peer_ffn(h2, w_peer_q[l], peer_subkeys[l], peer_u[l], peer_v[l])
    return x
```

```python
import numpy as np
from contextlib import ExitStack
import concourse.bass as bass
import concourse.mybir as mybir
from concourse.alu_op_type import AluOpType as ALU
from concourse.bass_utils import run_bass_kernel_spmd

AF = mybir.ActivationFunctionType
AX = mybir.AxisListType
F32 = mybir.dt.float32
BF16 = mybir.dt.bfloat16
I32 = mybir.dt.int32
U32 = mybir.dt.uint32

S = 2048
D = 1024
NT = 16
EPS = 1e-6
MEM = 256


class Prog:
    ENG = ("pe", "act", "dve", "pool", "sp")

    def __init__(self, nc):
        self.nc = nc
        self.eng = {"pe": nc.tensor, "act": nc.scalar, "dve": nc.vector,
                    "pool": nc.gpsimd, "sp": nc.sync}
        self.sem = {e: nc.alloc_semaphore("s_" + e) for e in self.ENG}
        self.cnt = {e: 0 for e in self.ENG}
        self.seen = {e: {} for e in self.ENG}
        self.last_w = {}
        self.readers = {}
        self.dsem = {}
        self.dcnt = {}
        self.ninst = 0
        self.background = set()

    def _semobj(self, sk):
        return self.sem[sk] if sk in self.sem else self.dsem[sk]

    def _wait(self, eng, deps):
        need = {}
        for d in deps:
            if d is None:
                continue
            sk, v = d
            if need.get(sk, 0) < v:
                need[sk] = v
        seen = self.seen[eng]
        for sk, v in need.items():
            if seen.get(sk, 0) < v:
                self.eng[eng].wait_ge(self._semobj(sk), v)
                seen[sk] = v

    def _deps(self, reads, writes):
        deps = []
        for k in reads:
            if k in self.last_w:
                deps.append(self.last_w[k])
        for k in writes:
            if k in self.last_w:
                deps.append(self.last_w[k])
            deps.extend(self.readers.get(k, {}).items())
        return deps

    def _commit(self, h, reads, writes):
        for k in writes:
            self.last_w[k] = h
            self.readers[k] = {}
        for k in reads:
            r = self.readers.setdefault(k, {})
            if r.get(h[0], 0) < h[1]:
                r[h[0]] = h[1]

    def op(self, eng, fn, reads=(), writes=(), extra=()):
        deps = self._deps(reads, writes) + list(extra)
        if eng == "pe":
            deps = [d for d in deps if d is not None and d[0] != "pe"]
        self._wait(eng, deps)
        inst = fn(self.eng[eng])
        self.cnt[eng] += 1
        inst.then_inc(self.sem[eng], 1)
        h = (eng, self.cnt[eng])
        self._commit(h, reads, writes)
        self.ninst += 1
        return h

    def dma(self, q, fn, semname, reads=(), writes=(), extra=()):
        if semname not in self.dsem:
            self.dsem[semname] = self.nc.alloc_semaphore("d_" + str(semname))
            self.dcnt[semname] = 0
        deps = self._deps(reads, writes) + list(extra)
        deps = [d for d in deps if d is not None and d[0] != semname]
        self._wait(q, deps)
        inst = fn(self.eng[q])
        self.dcnt[semname] += 16
        inst.then_inc(self.dsem[semname], 16)
        h = (semname, self.dcnt[semname])
        self._commit(h, reads, writes)
        self.ninst += 1
        return h

    def barrier(self, final=False):
        allh = [(e, self.cnt[e]) for e in self.ENG if self.cnt[e] > 0]
        allh += [(s, c) for s, c in self.dcnt.items() if c > 0 and (final or s not in self.background)]
        for e in self.ENG:
            self._wait(e, allh)
        self.last_w = {}
        self.readers = {}


def build_program(debug=None):
    nc = bass.Bass("TRN2", target_bir_lowering=False)
    P = Prog(nc)

    def din(name, shape, dt=F32):
        return nc.dram_tensor(name, shape, dt, kind="ExternalInput").ap()

    x = din("x", [S, D])
    mem = din("mem", [MEM, D])
    g_mix = din("g_mix", [1, D])
    g_mem = din("g_mem", [1, D])
    w_in = din("w_in", [1, D, 4352])
    w_mem_kv = din("w_mem_kv", [1, D, 1024])
    g_q_dil = din("g_q_dil", [1, 64])
    g_k_dil = din("g_k_dil", [1, 64])
    g_q_mem = din("g_q_mem", [1, 128])
    g_k_mem = din("g_k_mem", [1, 128])
    w_o_sb = din("w_o_sb", [1, 512, D])
    w_o_dil = din("w_o_dil", [1, 256, D])
    w_o_mem = din("w_o_mem", [1, 512, D])
    w_gate = din("w_gate", [1, D, 3072])
    b_gate = din("b_gate", [1, 3072])
    w_out = din("w_out", [1, D, D])
    g_ffn = din("g_ffn", [1, D])
    w_peer_q = din("w_peer_q", [1, D, 2048])
    peer_subkeys = din("peer_subkeys", [1, 8, 2, 128, 128])
    peer_u = din("peer_u", [1, 16384, D])
    peer_v = din("peer_v", [1, 16384, D])
    out = nc.dram_tensor("out", [S, D], F32, kind="ExternalOutput").ap()
    dbg = {}

    def dbg_out(name, shape):
        dbg[name] = nc.dram_tensor("dbg_" + name, shape, F32, kind="ExternalOutput").ap()
        return dbg[name]

    win_r = w_in[0].rearrange("(c p) n -> p c n", p=128)

    psall = nc.alloc_psum_tensor("psall", [128, 8, 512], F32)

    def psf(b):
        return psall[:, b, :]

    def psh(b):
        return psall[:, b, :].bitcast(BF16)

    ident = nc.alloc_sbuf_tensor("ident", [128, 128], BF16)
    iot = nc.alloc_sbuf_tensor("iot", [128, 128], I32)
    nident = nc.alloc_sbuf_tensor("nident", [128, 128], BF16)
    umask = nc.alloc_sbuf_tensor("umask", [128, 128], U32)
    onesf = nc.alloc_sbuf_tensor("onesf", [128, 128], F32)
    gmixT = nc.alloc_sbuf_tensor("gmixT", [128, 8], F32)
    gmemT = nc.alloc_sbuf_tensor("gmemT", [128, 8], F32)
    epsc = nc.alloc_sbuf_tensor("epsc", [128, 1], F32)
    blockones = nc.alloc_sbuf_tensor("blockones", [128, 128], BF16)
    ones_bf = nc.alloc_sbuf_tensor("ones_bf", [128, 128], BF16)
    gdil = nc.alloc_sbuf_tensor("gdil", [128, 2], F32)
    gmemh = nc.alloc_sbuf_tensor("gmemh", [128, 2], F32)
    es = ExitStack()
    hT = es.enter_context(nc.sbuf_tensor("hT", [128, 8, S], BF16))
    osbT = es.enter_context(nc.sbuf_tensor("osbT", [128, 4, S], BF16))

    odilT = es.enter_context(nc.sbuf_tensor("odilT", [128, 2, S], BF16))
    omemT = es.enter_context(nc.sbuf_tensor("omemT", [128, 4, S], BF16))
    P.op("dve", lambda e: e.memset(epsc[:], EPS), writes=["consts"])
    P.op("dve", lambda e: e.memset(blockones[:], 0.0), writes=["consts"])
    P.op("dve", lambda e: e.memset(blockones[0:64, 0:64], 1.0), writes=["consts"])
    P.op("dve", lambda e: e.memset(blockones[64:128, 64:128], 1.0), writes=["consts"])
    P.op("dve", lambda e: e.memset(ones_bf[:], 1.0), writes=["consts"])
    with nc.allow_non_contiguous_dma(reason="tiny gain vectors"):
        P.dma("sp", lambda e: e.dma_start(out=gmemT[:], in_=g_mem.rearrange("o (c p) -> p (o c)", p=128)),
              "gmemT", writes=["gmemT"])
        for half in range(2):
            P.dma("sp", lambda e: e.dma_start(out=gdil[half * 64:(half + 1) * 64, 0:1], in_=g_q_dil.rearrange("o d -> d o")),
                  "gsm", writes=["consts"])
            P.dma("sp", lambda e: e.dma_start(out=gdil[half * 64:(half + 1) * 64, 1:2], in_=g_k_dil.rearrange("o d -> d o")),
                  "gsm", writes=["consts"])
        P.dma("sp", lambda e: e.dma_start(out=gmemh[:, 0:1], in_=g_q_mem.rearrange("o d -> d o")), "gsm", writes=["consts"])
        P.dma("sp", lambda e: e.dma_start(out=gmemh[:, 1:2], in_=g_k_mem.rearrange("o d -> d o")), "gsm", writes=["consts"])

    P.op("pool", lambda e: e.iota(iot[:], pattern=[[1, 128]], base=0, channel_multiplier=-1), writes=["iot"])
    P.op("dve", lambda e: e.tensor_single_scalar(out=ident[:], in_=iot[:], scalar=0.0, op=ALU.is_equal),
         reads=["iot"], writes=["ident"])
    P.op("dve", lambda e: e.tensor_scalar(out=nident[:], in0=iot[:], scalar1=0.0, scalar2=-1.0, op0=ALU.is_equal, op1=ALU.mult),
         reads=["iot"], writes=["ident"])
    P.op("dve", lambda e: e.tensor_single_scalar(out=umask[:], in_=iot[:], scalar=0.0, op=ALU.is_ge),
         reads=["iot"], writes=["umask"])
    P.op("dve", lambda e: e.memset(onesf[:], 1.0), writes=["umask"])
    with nc.allow_non_contiguous_dma(reason="tiny gain vectors"):
        P.dma("sp", lambda e: e.dma_start(out=gmixT[:], in_=g_mix.rearrange("o (c p) -> p (o c)", p=128)),
              "gmixT", writes=["gmixT"])

    fill_one = nc.gpsimd.to_reg(1.0)

    uvb = nc.dram_tensor("uvb", [16384, 2048], BF16).ap()
    P.background.add("cvt")
    cvt_state = {"k": 0, "h": None}

    def cvt_step(n):
        for _ in range(n):
            k = cvt_state["k"]
            if k >= 16:
                return
            cvt_state["k"] += 1
            src = (peer_u, peer_v)[k % 2][0]
            rows = slice((k // 2) * 2048, (k // 2 + 1) * 2048)
            cvt_state["h"] = P.dma("pool", lambda e: e.dma_start(out=uvb[rows, (k % 2) * 1024:(k % 2 + 1) * 1024],
                                                                 in_=src[rows, :]), "cvt")

    def rms_to_T(src, ntiles, gT, dstT, tag):
        with nc.sbuf_tensor(tag + "_xs", [128, 2, D], F32) as xs, \
             nc.sbuf_tensor(tag + "_xn", [128, 2, D], BF16) as xn, \
             nc.sbuf_tensor(tag + "_junk", [128, D], BF16) as junk, \
             nc.sbuf_tensor(tag + "_st", [128, 4, ntiles], F32) as st:
            for tt in range(ntiles):
                sl = tt % 2
                P.dma("sp", lambda e: e.dma_start(out=xs[:, sl, :], in_=src[tt * 128:(tt + 1) * 128, :]),
                      (tag, "xs", sl), writes=[(tag, "xs", sl)])
                P.op("act", lambda e: e.activation(out=junk[:], in_=xs[:, sl, :], func=AF.Square,
                                                   accum_out=st[:, 0, tt:tt + 1]),
                     reads=[(tag, "xs", sl)], writes=[(tag, "junk"), (tag, "st", tt)])
                P.op("act", lambda e: e.activation(out=st[:, 1, tt:tt + 1], in_=st[:, 0, tt:tt + 1], func=AF.Ln,
                                                   scale=1.0 / D, bias=epsc[:, 0:1]),
                     reads=[(tag, "st", tt), "consts"], writes=[(tag, "st", tt)])
                P.op("act", lambda e: e.activation(out=st[:, 3, tt:tt + 1], in_=st[:, 1, tt:tt + 1], func=AF.Exp, scale=-0.5),
                     reads=[(tag, "st", tt)], writes=[(tag, "st", tt)])
                P.op("act", lambda e: e.activation(out=xn[:, sl, :], in_=xs[:, sl, :], func=AF.Copy,
                                                   scale=st[:, 3, tt:tt + 1]),
                     reads=[(tag, "xs", sl), (tag, "st", tt)], writes=[(tag, "xn", sl)])
                b = tt % 2
                for c in range(8):
                    P.op("pe", lambda e: e.transpose(out=psh(b)[:, c * 128:(c + 1) * 128],
                                                     in_=xn[:, sl, c * 128:(c + 1) * 128], identity=ident[:]),
                         reads=[(tag, "xn", sl), "ident"], writes=[("ps", b)])
                P.op("dve", lambda e: e.tensor_tensor(
                    out=dstT[:, :, tt * 128:(tt + 1) * 128],
                    in0=psh(b).rearrange("p (c t) -> p c t", c=8),
                    in1=gT[:, :].unsqueeze(2).broadcast_to([128, 8, 128]), op=ALU.mult),
                    reads=[("ps", b), "gmixT", "gmemT"], writes=[(tag, "dstT", tt)])
            P.barrier()

    rms_to_T(x, NT, gmixT, hT, "x")
    HT_KEYS = [("x", "dstT", tt) for tt in range(NT)]

    cp_toggle = [0]

    def evac(out_ap, in_ap, reads, writes, eng=None):
        if eng is None:
            eng = ("act", "dve")[cp_toggle[0] % 2]
            cp_toggle[0] += 1
        if eng == "act":
            return P.op("act", lambda e: e.activation(out=out_ap, in_=in_ap, func=AF.Copy), reads=reads, writes=writes)
        return P.op(eng, lambda e: e.tensor_copy(out=out_ap, in_=in_ap), reads=reads, writes=writes)

    NSB = 4
    with ExitStack() as sbs:
        qkT = sbs.enter_context(nc.sbuf_tensor("qkT", [128, 8, S], BF16))
        vsb = sbs.enter_context(nc.sbuf_tensor("vsb", [128, NT, 512], BF16))
        wsb_cm = nc.sbuf_tensor("wsb", [128, 8, 1536], BF16)
        wsb = wsb_cm.__enter__()
        P.dma("pool", lambda e: e.dma_start(out=wsb[:], in_=win_r[:, :, 0:1536]), "wsb", writes=["wsb"])
        WSB = ["wsb"]
        nb = 0
        for fc in range(8):
            for tg in range(4):
                b = nb % 4
                nb += 1
                for c in range(8):
                    P.op("pe", lambda e: e.matmul(psf(b), lhsT=wsb[:, c, fc * 128:(fc + 1) * 128],
                                                  rhs=hT[:, c, tg * 512:(tg + 1) * 512], start=(c == 0), stop=(c == 7)),
                         reads=WSB + HT_KEYS if c in (0, 7) else [], writes=[("ps", b)])
                evac(qkT[:, fc, tg * 512:(tg + 1) * 512], psf(b), [("ps", b)], [("qkT", fc, tg)])
        for tt in range(NT):
            b = nb % 4
            nb += 1
            for c in range(8):
                P.op("pe", lambda e: e.matmul(psf(b), lhsT=hT[:, c, tt * 128:(tt + 1) * 128],
                                              rhs=wsb[:, c, 1024:1536], start=(c == 0), stop=(c == 7)),
                     reads=WSB + HT_KEYS if c in (0, 7) else [], writes=[("ps", b)])
            evac(vsb[:, tt, :], psf(b), [("ps", b)], [("vsb", tt)])

        P.barrier()
        wsb_cm.__exit__(None, None, None)
        mt = sbs.enter_context(nc.sbuf_tensor("mt", [128, NSB, S + 8], F32))
        Abf = sbs.enter_context(nc.sbuf_tensor("Abf", [128, NSB, S + 8], BF16))
        ATs = sbs.enter_context(nc.sbuf_tensor("ATs", [128, NSB, S], BF16))
        st = {"zb": 0, "tb": 0}

        def sb_stage_a(h, qi, sl):
            ch = h // 2
            pb = (h % 2) * 64
            nk = (qi + 1) * 128
            nchunk = (nk + 511) // 512
            qkeys = [("qkT", ch, qi // 4)]
            for cc in range(nchunk):
                w = min(512, nk - cc * 512)
                b = st["zb"] % 4
                st["zb"] += 1
                P.op("pe", lambda e: e.matmul(psf(b)[:, 0:w], lhsT=qkT[pb:pb + 64, ch, qi * 128:(qi + 1) * 128],
                                              rhs=qkT[pb:pb + 64, 4 + ch, cc * 512:cc * 512 + w],
                                              start=True, stop=True),
                     reads=qkeys + [("qkT", 4 + ch, cc)], writes=[("ps", b)])
                P.op("act", lambda e: e.activation(out=mt[:, sl, cc * 512:cc * 512 + w], in_=psf(b)[:, 0:w],
                                                   func=AF.Sigmoid, scale=-0.125),
                     reads=[("ps", b)], writes=[("mt", sl)])
            P.op("pool", lambda e: e.affine_select(out=mt[:, sl, qi * 128:(qi + 1) * 128],
                                                   in_=mt[:, sl, qi * 128:(qi + 1) * 128],
                                                   pattern=[[-1, 128]], compare_op=ALU.is_gt, fill=fill_one,
                                                   base=0, channel_multiplier=1),
                 reads=[("mt", sl)], writes=[("mt", sl)])
            P.op("pool", lambda e: e.memset(Abf[:, sl, nk:nk + 1], 1.0), reads=[], writes=[("Abf", sl)])

        def sb_stage_a2(h, qi, sl):
            nk = (qi + 1) * 128
            rev = mt[:, sl, nk - 1::-1]
            P.op("dve", lambda e: e.tensor_tensor_scan(out=Abf[:, sl, nk - 1::-1], data0=rev, data1=rev, initial=1.0,
                                                       op0=ALU.mult, op1=ALU.bypass),
                 reads=[("mt", sl)], writes=[("Abf", sl)])

        def sb_stage_b(h, qi, sl):
            ch = h // 2
            pb = (h % 2) * 64
            for kb0 in range(0, qi + 1, 4):
                nblk = min(4, qi + 1 - kb0)
                b = 4 + st["tb"] % 2
                st["tb"] += 1
                for j in range(nblk):
                    kb = kb0 + j
                    P.op("pe", lambda e: e.matmul(psf(b)[:, j * 128:(j + 1) * 128],
                                                  lhsT=Abf[:, sl, kb * 128 + 1:(kb + 1) * 128 + 1], rhs=ident[:],
                                                  start=True, stop=False),
                         reads=[("Abf", sl), "ident"], writes=[("ps", b)])
                    P.op("pe", lambda e: e.matmul(psf(b)[:, j * 128:(j + 1) * 128],
                                                  lhsT=Abf[:, sl, kb * 128:(kb + 1) * 128], rhs=nident[:],
                                                  start=False, stop=True),
                         reads=[("Abf", sl), "ident"], writes=[("ps", b)])
                evac(ATs[:, sl, kb0 * 128:(kb0 + nblk) * 128], psf(b)[:, 0:nblk * 128],
                     [("ps", b)], [("ATs", sl, kb0)], eng="act")

        def sb_stage_c(h, qi, sl):
            ch = h // 2
            pb = (h % 2) * 64
            ab = 6 + (qi // 4) % 2
            for kb in range(qi + 1):
                P.op("pe", lambda e: e.matmul(psf(ab)[pb:pb + 64, (qi % 4) * 128:(qi % 4 + 1) * 128],
                                              lhsT=vsb[:, kb, h * 64:(h + 1) * 64],
                                              rhs=ATs[:, sl, kb * 128:(kb + 1) * 128],
                                              start=(kb == 0), stop=(kb == qi)),
                     reads=[("vsb", kb), ("ATs", sl, (kb // 4) * 4)], writes=[("ps", ab)])
            if qi % 4 == 3:
                evac(osbT[pb:pb + 64, ch, (qi - 3) * 128:(qi + 1) * 128], psf(ab)[pb:pb + 64, :],
                     [("ps", ab)], [("osbT", h, qi // 4)], eng="act")

        units = [(h, qi) for h in range(8) for qi in range(NT)]
        nun = len(units)
        for u in range(nun + 4):
            if u < nun:
                if units[u][1] == 0:
                    cvt_step(2)
                sb_stage_a(units[u][0], units[u][1], u % NSB)
            for lag, fn in ((2, sb_stage_a2), (3, sb_stage_b), (4, sb_stage_c)):
                v = u - lag
                if 0 <= v < nun:
                    fn(units[v][0], units[v][1], v % NSB)
        P.barrier()

    if debug == "sb":
        d1 = dbg_out("osbT", [128, 4 * S])
        d2 = dbg_out("hT", [128, 8 * S])
        P.dma("pool", lambda e: e.dma_start(out=d1, in_=osbT[:].rearrange("p c t -> p (c t)"), max_dma_last_dim=4096), "dbg1")
        P.dma("pool", lambda e: e.dma_start(out=d2, in_=hT[:].rearrange("p c t -> p (c t)"), max_dma_last_dim=4096), "dbg2")
        P.barrier(final=True)
        return nc, P

    def head_norm(b_in, b_ss, w, onesT, inv_n, gcol, dst_ap, sq, t1, tagk, dst_key):
        P.op("act", lambda e: e.activation(out=sq[:, 0:w], in_=psf(b_in)[:, 0:w], func=AF.Square),
             reads=[("ps", b_in)], writes=[(tagk, "sq")])
        P.op("pe", lambda e: e.matmul(psf(b_ss)[:, 0:w], lhsT=onesT, rhs=sq[:, 0:w], start=True, stop=True),
             reads=[(tagk, "sq"), "consts"], writes=[("ps", b_ss)])
        P.op("act", lambda e: e.activation(out=t1[:, 0:w], in_=psf(b_ss)[:, 0:w], func=AF.Ln, scale=inv_n,
                                           bias=epsc[:, 0:1]),
             reads=[("ps", b_ss), "consts"], writes=[(tagk, "t1")])
        P.op("act", lambda e: e.activation(out=t1[:, 0:w], in_=t1[:, 0:w], func=AF.Exp, scale=-0.5),
             reads=[(tagk, "t1")], writes=[(tagk, "t1")])
        P.op("dve", lambda e: e.scalar_tensor_tensor(out=dst_ap, in0=psf(b_in)[:, 0:w], scalar=gcol,
                                                     in1=t1[:, 0:w], op0=ALU.mult, op1=ALU.mult),
             reads=[("ps", b_in), (tagk, "t1"), "consts"], writes=[dst_key])

    DIL_R = (1, 4, 16)
    with nc.sbuf_tensor("dtab", [128, 6, 512], F32) as dtab, \
         nc.sbuf_tensor("gapi", [128, 256], I32) as gapi, \
         nc.sbuf_tensor("gapf", [128, 256], F32) as gapf, \
         nc.sbuf_tensor("maskf", [128, 256], F32) as maskf, \
         nc.sbuf_tensor("wd", [128, 8, 9, 128], BF16) as wd, \
         nc.sbuf_tensor("dq", [128, 3, S], BF16) as dq, \
         nc.sbuf_tensor("dk", [128, 3, S], BF16) as dk, \
         nc.sbuf_tensor("dv", [128, 3, 16, 128], BF16) as dv, \
         nc.sbuf_tensor("ND", [128, 2, S], F32) as ND, \
         nc.sbuf_tensor("dsq", [128, 512], BF16) as dsq, \
         nc.sbuf_tensor("dt1", [128, 512], F32) as dt1, \
         nc.sbuf_tensor("dE", [128, 3, 512], F32) as dE, \
         nc.sbuf_tensor("dPT", [128, 3, 512], BF16) as dPT:
        P.op("pool", lambda e: e.iota(gapi[:].rearrange("p (a q) -> p a q", a=2), pattern=[[128, 2], [1, 128]],
                                      base=0, channel_multiplier=-1), writes=["gapi"])
        P.op("dve", lambda e: e.tensor_copy(out=gapf[:], in_=gapi[:]), reads=["gapi"], writes=["gapf"])
        P.op("dve", lambda e: e.tensor_single_scalar(out=maskf[:, 0:128], in_=gapf[:, 0:128], scalar=0.0, op=ALU.is_ge),
             reads=["gapf"], writes=["maskf0"])
        P.op("dve", lambda e: e.tensor_single_scalar(out=maskf[:, 128:256], in_=gapf[:, 128:256], scalar=128.0,
                                                     op=ALU.is_le), reads=["gapf"], writes=["maskf1"])
        P.op("dve", lambda e: e.tensor_scalar_max(out=gapf[:, 0:128], in0=gapf[:, 0:128], scalar1=0.0),
             reads=["gapf", "maskf0"], writes=["gapf"])
        for g in range(3):
            for j in range(2):
                for e_ in range(2):
                    H = 4 * g + 2 * j + e_
                    slope = 2.0 ** (-8.0 * (H + 1) / 12.0)
                    for part in range(2):
                        col = (e_ * 2 + part) * 128
                        P.op("act", lambda e: e.activation(out=dtab[:, g * 2 + j, col:col + 128],
                                                           in_=gapf[:, part * 128:(part + 1) * 128], func=AF.Exp,
                                                           scale=-slope * DIL_R[g]),
                             reads=["gapf"], writes=[("dtab", g, j, part, e_)])
                        P.op("dve", lambda e: e.tensor_tensor(out=dtab[:, g * 2 + j, col:col + 128],
                                                              in0=dtab[:, g * 2 + j, col:col + 128],
                                                              in1=maskf[:, part * 128:(part + 1) * 128], op=ALU.mult),
                             reads=[("dtab", g, j, part, e_), "maskf0", "maskf1"], writes=[("dtab", g, j, part, e_)])
        DTAB = lambda g, j: [("dtab", g, j, pp, ee) for pp in range(2) for ee in range(2)]
        if debug == "dil_a":
            P.barrier(final=True)
            return nc, P

        for j in range(2):
            for t3 in range(3):
                for g in range(3):
                    col = 1536 + t3 * 768 + (2 * g + j) * 128
                    P.dma("pool", lambda e: e.dma_start(out=wd[:, :, t3 * 3 + g, :], in_=win_r[:, :, col:col + 128]),
                          ("wd", t3 * 3 + g), writes=[("wd", t3 * 3 + g)])
            nb = 0
            for t3 in range(2):
                dst = dq if t3 == 0 else dk
                for g in range(3):
                    for tg in range(4):
                        b = nb % 3
                        nb += 1
                        for c in range(8):
                            P.op("pe", lambda e: e.matmul(psf(b), lhsT=wd[:, c, t3 * 3 + g, :],
                                                          rhs=hT[:, c, tg * 512:(tg + 1) * 512],
                                                          start=(c == 0), stop=(c == 7)),
                                 reads=[("wd", t3 * 3 + g)] if c in (0, 7) else [], writes=[("ps", b)])
                        head_norm(b, 3, 512, blockones[:], 1.0 / 64, gdil[:, t3:t3 + 1],
                                  dst[:, g, tg * 512:(tg + 1) * 512], dsq, dt1, "dil", ("dqk", t3, g, tg))
            if debug == "dil_b":
                P.barrier(final=True)
                return nc, P
            for g in range(3):
                r = DIL_R[g]
                nbk = 16 // r
                for bi0 in range(0, 16, 4):
                    b = nb % 3
                    nb += 1
                    for jj in range(4):
                        bi = bi0 + jj
                        c_, n_ = bi // nbk, bi % nbk
                        st_ = c_ + r * 128 * n_
                        for c in range(8):
                            P.op("pe", lambda e: e.matmul(psf(b)[:, jj * 128:(jj + 1) * 128],
                                                          lhsT=hT[:, c, st_:st_ + 127 * r + 1:r],
                                                          rhs=wd[:, c, 6 + g, :], start=(c == 0), stop=(c == 7)),
                                 reads=[("wd", 6 + g)] if c in (0, 7) else [], writes=[("ps", b)])
                    evac(dv[:, g, bi0:bi0 + 4, :], psf(b).rearrange("p (a f) -> p a f", a=4), [("ps", b)],
                         [("dv", g, bi0)])
            if debug == "dil_c":
                P.barrier(final=True)
                return nc, P
            def dil_geom(g, bi):
                r = DIL_R[g]
                nbk = 16 // r
                c_, n_ = bi // nbk, bi % nbk
                st_ = c_ + r * 128 * n_
                tsl = slice(st_, st_ + 127 * r + 1, r)
                psl = slice(st_ - 128 * r, st_ - 128 * r + 127 * r + 1, r)
                return n_, tsl, psl

            def dil_stage_a(g, bi, un):
                n_, tsl, psl = dil_geom(g, bi)
                sl = un % 3
                sb_ = 2 * (un % 3)
                wp = 256 if n_ > 0 else 128
                rk = [("dqk", 0, g, tt_) for tt_ in range(4)] + [("dqk", 1, g, tt_) for tt_ in range(4)]
                for e_ in range(2):
                    for part in range(2 if n_ > 0 else 1):
                        ksl = tsl if part == 0 else psl
                        P.op("pe", lambda e: e.matmul(psf(sb_ + e_)[:, part * 128:(part + 1) * 128],
                                                      lhsT=dk[e_ * 64:(e_ + 1) * 64, g, ksl],
                                                      rhs=dq[e_ * 64:(e_ + 1) * 64, g, tsl], start=True, stop=True),
                             reads=rk, writes=[("ps", sb_ + e_)])
                P.op("act", lambda e: e.activation(out=dE[:, sl, :].rearrange("p (a q) -> p a q", a=2)[:, :, 0:wp],
                                                   in_=psall[:, sb_:sb_ + 2, 0:wp], func=AF.Exp, scale=0.125),
                     reads=[("ps", sb_), ("ps", sb_ + 1)], writes=[("dE", sl)])
                P.op("dve", lambda e: e.tensor_tensor(
                    out=dPT[:, sl, :].rearrange("p (a q) -> p a q", a=2)[:, :, 0:wp],
                    in0=dE[:, sl, :].rearrange("p (a q) -> p a q", a=2)[:, :, 0:wp],
                    in1=dtab[:, g * 2 + j, :].rearrange("p (a q) -> p a q", a=2)[:, :, 0:wp], op=ALU.mult),
                     reads=[("dE", sl)] + DTAB(g, j), writes=[("dPT", sl)])

            def dil_stage_b(g, bi, un):
                n_, tsl, psl = dil_geom(g, bi)
                sl = un % 3
                ob_ = 6 + un % 2
                for kind in range(2):
                    for e_ in range(2):
                        for part in range(2 if n_ > 0 else 1):
                            col = (e_ * 2 + part) * 128
                            if kind == 0:
                                lt = dv[:, g, bi - part, e_ * 64:(e_ + 1) * 64]
                            else:
                                lt = ones_bf[:, 0:64]
                            P.op("pe", lambda e: e.matmul(psf(ob_)[e_ * 64:(e_ + 1) * 64, kind * 128:(kind + 1) * 128],
                                                          lhsT=lt, rhs=dPT[:, sl, col:col + 128],
                                                          start=(part == 0), stop=(part == (1 if n_ > 0 else 0))),
                                 reads=[("dPT", sl), ("dv", g, (bi // 4) * 4), ("dv", g, ((bi - part) // 4) * 4), "consts"],
                                 writes=[("ps", ob_)])
                nd_out = ND[:, :, tsl]
                nd_in = psf(ob_)[:, 0:256].rearrange("p (a q) -> p a q", a=2)
                if g == 0:
                    P.op("dve", lambda e: e.tensor_copy(out=nd_out, in_=nd_in), reads=[("ps", ob_)], writes=["ND"])
                else:
                    P.op("dve", lambda e: e.tensor_tensor(out=nd_out, in0=nd_out, in1=nd_in, op=ALU.add),
                         reads=[("ps", ob_), "ND"], writes=["ND"])

            dunits = [(g, bi) for g in range(3) for bi in range(16)]
            DLOOK = 2
            for u in range(len(dunits) + DLOOK):
                if u < len(dunits):
                    dil_stage_a(dunits[u][0], dunits[u][1], u)
                if u >= DLOOK:
                    dil_stage_b(dunits[u - DLOOK][0], dunits[u - DLOOK][1], u - DLOOK)
            P.op("dve", lambda e: e.reciprocal(out=ND[:, 1, :], in_=ND[:, 1, :]), reads=["ND"], writes=["ND"])
            P.op("dve", lambda e: e.tensor_tensor(out=odilT[:, j, :], in0=ND[:, 0, :], in1=ND[:, 1, :], op=ALU.mult),
                 reads=["ND"], writes=[("odilT", j)])
        P.barrier()

    if debug == "dil":
        d1 = dbg_out("odilT", [128, 2 * S])
        P.dma("pool", lambda e: e.dma_start(out=d1, in_=odilT[:].rearrange("p c t -> p (c t)"), max_dma_last_dim=4096), "dbg1")
        P.barrier(final=True)
        return nc, P

    pre = ExitStack()
    wgate = pre.enter_context(nc.sbuf_tensor("wgate", [128, 8, 3072], BF16))
    wg_r = w_gate[0].rearrange("(c p) n -> p c n", p=128)
    for br in range(3):
        P.dma("pool", lambda e: e.dma_start(out=wgate[:, :, br * 1024:(br + 1) * 1024],
                                             in_=wg_r[:, :, br * 1024:(br + 1) * 1024]), ("wgate", br), writes=[("wgate", br)])

    with nc.sbuf_tensor("wkv", [128, 8, 1024], BF16) as wkv, \
         nc.sbuf_tensor("wmq", [128, 8, 512], BF16) as wmq, \
         nc.sbuf_tensor("memhT", [128, 8, MEM], BF16) as memhT, \
         nc.sbuf_tensor("kmT", [128, 4, MEM], BF16) as kmT, \
         nc.sbuf_tensor("vm", [128, 2, 512], BF16) as vm, \
         nc.sbuf_tensor("qmT", [128, 4, S], BF16) as qmT, \
         nc.sbuf_tensor("msq", [128, 512], BF16) as msq, \
         nc.sbuf_tensor("mt1", [128, 512], F32) as mt1, \
         nc.sbuf_tensor("mPT", [128, 2, 2, 512], BF16) as mPT, \
         nc.sbuf_tensor("mrd", [128, 2, 512], F32) as mrd:
        P.dma("pool", lambda e: e.dma_start(out=wkv[:], in_=w_mem_kv[0].rearrange("(c p) n -> p c n", p=128)),
              "wkv", writes=["wkv"])
        P.dma("pool", lambda e: e.dma_start(out=wmq[:], in_=win_r[:, :, 3840:4352]), "wmq", writes=["wmq"])
        rms_to_T(mem, 2, gmemT, memhT, "m")
        for hd in range(4):
            b = hd % 2
            for c in range(8):
                P.op("pe", lambda e: e.matmul(psf(b)[:, 0:MEM], lhsT=wkv[:, c, hd * 128:(hd + 1) * 128],
                                              rhs=memhT[:, c, :], start=(c == 0), stop=(c == 7)),
                     reads=["wkv"] if c in (0, 7) else [], writes=[("ps", b)])
            head_norm(b, 3, MEM, ones_bf[:], 1.0 / 128, gmemh[:, 1:2], kmT[:, hd, :], msq, mt1, "mem", ("kmT", hd))
        for blk in range(2):
            b = blk % 2
            for c in range(8):
                P.op("pe", lambda e: e.matmul(psf(b), lhsT=memhT[:, c, blk * 128:(blk + 1) * 128],
                                              rhs=wkv[:, c, 512:1024], start=(c == 0), stop=(c == 7)),
                     reads=["wkv"] if c in (0, 7) else [], writes=[("ps", b)])
            evac(vm[:, blk, :], psf(b), [("ps", b)], [("vm", blk)])
        nb = 0
        for hd in range(4):
            for tg in range(4):
                b = nb % 3
                nb += 1
                for c in range(8):
                    P.op("pe", lambda e: e.matmul(psf(b), lhsT=wmq[:, c, hd * 128:(hd + 1) * 128],
                                                  rhs=hT[:, c, tg * 512:(tg + 1) * 512], start=(c == 0), stop=(c == 7)),
                         reads=["wmq"] if c in (0, 7) else [], writes=[("ps", b)])
                head_norm(b, 3, 512, ones_bf[:], 1.0 / 128, gmemh[:, 0:1], qmT[:, hd, tg * 512:(tg + 1) * 512],
                          msq, mt1, "mem", ("qmT", hd, tg))
        un = 0
        for hd in range(4):
            for tg in range(4):
                sl = un % 2
                un += 1
                for blk in range(2):
                    sb_ = 4 + blk
                    P.op("pe", lambda e: e.matmul(psf(sb_), lhsT=kmT[:, hd, blk * 128:(blk + 1) * 128],
                                                  rhs=qmT[:, hd, tg * 512:(tg + 1) * 512], start=True, stop=True),
                         reads=[("kmT", hd), ("qmT", hd, tg)], writes=[("ps", sb_)])
                    P.op("act", lambda e: e.activation(out=mPT[:, sl, blk, :], in_=psf(sb_), func=AF.Exp,
                                                       scale=128.0 ** -0.5),
                         reads=[("ps", sb_)], writes=[("mPT", sl, blk)])
                for blk in range(2):
                    P.op("pe", lambda e: e.matmul(psf(6), lhsT=vm[:, blk, hd * 128:(hd + 1) * 128],
                                                  rhs=mPT[:, sl, blk, :], start=(blk == 0), stop=(blk == 1)),
                         reads=[("mPT", sl, blk), ("vm", blk)], writes=[("ps", 6)])
                for blk in range(2):
                    P.op("pe", lambda e: e.matmul(psf(7), lhsT=ones_bf[:], rhs=mPT[:, sl, blk, :],
                                                  start=(blk == 0), stop=(blk == 1)),
                         reads=[("mPT", sl, blk), "consts"], writes=[("ps", 7)])
                P.op("dve", lambda e: e.reciprocal(out=mrd[:, sl, :], in_=psf(7)), reads=[("ps", 7)], writes=[("mrd", sl)])
                P.op("dve", lambda e: e.tensor_tensor(out=omemT[:, hd, tg * 512:(tg + 1) * 512], in0=psf(6),
                                                      in1=mrd[:, sl, :], op=ALU.mult),
                     reads=[("ps", 6), ("mrd", sl)], writes=[("omemT", hd, tg)])
        P.barrier()

    if debug == "mem":
        d1 = dbg_out("omemT", [128, 4 * S])
        P.dma("pool", lambda e: e.dma_start(out=d1, in_=omemT[:].rearrange("p c t -> p (c t)"), max_dma_last_dim=4096), "dbg1")
        P.barrier(final=True)
        return nc, P

    if debug == "x1":
        x1d = dbg_out("x1", [S, D])
    else:
        x1d = nc.dram_tensor("x1d", [S, D], F32).ap()
    with nc.sbuf_tensor("wosb", [128, 4, D], BF16) as wosb, \
         nc.sbuf_tensor("wodil", [128, 2, D], BF16) as wodil, \
         nc.sbuf_tensor("womem", [128, 4, D], BF16) as womem, \
         nc.sbuf_tensor("wout", [128, 8, D], BF16) as wout, \
         nc.sbuf_tensor("bg", [128, 24], F32) as bg, \
         nc.sbuf_tensor("gsb", [128, 2, 3, 512], F32) as gsb, \
         nc.sbuf_tensor("macc", [128, 2, 512], F32) as macc, \
         nc.sbuf_tensor("mtmp", [128, 2, 512], F32) as mtmp, \
         nc.sbuf_tensor("mrgT", [128, 8, 512], BF16) as mrgT, \
         nc.sbuf_tensor("xr", [128, 2, D], F32) as xr, \
         nc.sbuf_tensor("x1s", [128, 2, D], F32) as x1s:
        P.dma("pool", lambda e: e.dma_start(out=wosb[:], in_=w_o_sb[0].rearrange("(c p) n -> p c n", p=128)), "wosb", writes=["wosb"])
        P.dma("pool", lambda e: e.dma_start(out=wodil[:], in_=w_o_dil[0].rearrange("(c p) n -> p c n", p=128)), "wodil", writes=["wodil"])
        P.dma("pool", lambda e: e.dma_start(out=womem[:], in_=w_o_mem[0].rearrange("(c p) n -> p c n", p=128)), "womem", writes=["womem"])
        P.dma("pool", lambda e: e.dma_start(out=wout[:], in_=w_out[0].rearrange("(c p) n -> p c n", p=128)), "wout", writes=["wout"])
        with nc.allow_non_contiguous_dma(reason="tiny bias vector"):
            P.dma("sp", lambda e: e.dma_start(out=bg[:], in_=b_gate.rearrange("o (c p) -> p (o c)", p=128)), "bg", writes=["bg"])
        branches = [(wosb, osbT, 4, "wosb"), (wodil, odilT, 2, "wodil"), (womem, omemT, 4, "womem")]
        un = 0
        xt = 0
        for tg in range(4):
            tsl = slice(tg * 512, (tg + 1) * 512)
            for dc in range(8):
                sl = un % 2
                un += 1
                for br in range(3):
                    gb = br
                    for c in range(8):
                        P.op("pe", lambda e: e.matmul(psf(gb), lhsT=wgate[:, c, br * 1024 + dc * 128:br * 1024 + (dc + 1) * 128],
                                                      rhs=hT[:, c, tsl], start=(c == 0), stop=(c == 7)),
                             reads=[("wgate", br)] if c in (0, 7) else [], writes=[("ps", gb)])
                    P.op("act", lambda e: e.activation(out=gsb[:, sl, br, :], in_=psf(gb), func=AF.Sigmoid,
                                                       bias=bg[:, br * 8 + dc:br * 8 + dc + 1], scale=1.0),
                         reads=[("ps", gb), "bg"], writes=[("gsb", sl, br)])
                for br in range(3):
                    wt, oT, nkc, wkey = branches[br]
                    yb = 3 + br
                    for kc in range(nkc):
                        P.op("pe", lambda e: e.matmul(psf(yb), lhsT=wt[:, kc, dc * 128:(dc + 1) * 128], rhs=oT[:, kc, tsl],
                                                      start=(kc == 0), stop=(kc == nkc - 1)),
                             reads=[wkey] if kc in (0, nkc - 1) else [], writes=[("ps", yb)])
                P.op("dve", lambda e: e.tensor_tensor(out=macc[:, sl, :], in0=psf(3), in1=gsb[:, sl, 0, :], op=ALU.mult),
                     reads=[("ps", 3), ("gsb", sl, 0)], writes=[("macc", sl)])
                P.op("dve", lambda e: e.tensor_tensor(out=mtmp[:, sl, :], in0=psf(4), in1=gsb[:, sl, 1, :], op=ALU.mult),
                     reads=[("ps", 4), ("gsb", sl, 1)], writes=[("mtmp", sl)])
                P.op("pool", lambda e: e.tensor_tensor(out=macc[:, sl, :], in0=macc[:, sl, :], in1=mtmp[:, sl, :], op=ALU.add),
                     reads=[("macc", sl), ("mtmp", sl)], writes=[("macc", sl)])
                P.op("dve", lambda e: e.tensor_tensor(out=mtmp[:, sl, :], in0=psf(5), in1=gsb[:, sl, 2, :], op=ALU.mult),
                     reads=[("ps", 5), ("gsb", sl, 2)], writes=[("mtmp", sl)])
                P.op("pool", lambda e: e.tensor_tensor(out=mrgT[:, dc, :], in0=macc[:, sl, :], in1=mtmp[:, sl, :], op=ALU.add),
                     reads=[("macc", sl), ("mtmp", sl)], writes=[("mrgT", dc)])
            for tl in range(4):
                tt = tg * 4 + tl
                xs_ = xt % 2
                xt += 1
                P.dma("sp", lambda e: e.dma_start(out=xr[:, xs_, :], in_=x[tt * 128:(tt + 1) * 128, :]), ("xr", xs_),
                      writes=[("xr", xs_)])
                for half in range(2):
                    ob = 6 + half
                    for dc in range(8):
                        P.op("pe", lambda e: e.matmul(psf(ob), lhsT=mrgT[:, dc, tl * 128:(tl + 1) * 128],
                                                      rhs=wout[:, dc, half * 512:(half + 1) * 512],
                                                      start=(dc == 0), stop=(dc == 7)),
                             reads=["wout"] + [("mrgT", d_) for d_ in range(8)] if dc in (0, 7) else [], writes=[("ps", ob)])
                    P.op("dve", lambda e: e.tensor_tensor(out=x1s[:, xs_, half * 512:(half + 1) * 512], in0=psf(ob),
                                                          in1=xr[:, xs_, half * 512:(half + 1) * 512], op=ALU.add),
                         reads=[("ps", ob), ("xr", xs_)], writes=[("x1s", xs_, half)])
                P.dma("sp", lambda e: e.dma_start(out=x1d[tt * 128:(tt + 1) * 128, :], in_=x1s[:, xs_, :]), ("x1o", xs_),
                      reads=[("x1s", xs_, 0), ("x1s", xs_, 1)], writes=[("x1d", tt)])
        P.barrier()

    if debug == "x1":
        P.barrier(final=True)
        return nc, P
    pre.close()
    es.close()

    cvt_step(16)
    NR = 20
    GS = 4
    FUSE_EVERY = 4
    pcnt = [0]
    jcnt = [0]
    with ExitStack() as pes:
        def sb(name, shape, dt):
            return pes.enter_context(nc.sbuf_tensor(name, shape, dt))
        wpq = sb("wpq", [128, 8, 2048], BF16)
        subkT = sb("subkT", [128, 16, 128], BF16)
        gffn_b = sb("gffn_b", [128, D], F32)
        iota16 = sb("iota16", [128, 16], F32)
        x1t = sb("x1t", [128, 2, D], F32)
        outt = sb("outt", [128, D], F32)
        h2 = sb("h2", [128, 2, D], F32)
        h2b = sb("h2b", [128, 2, D], BF16)
        prodr = sb("prodr", [128, 4, D], BF16)
        junkB = sb("junkB", [128, 2, D], mybir.dt.float8e4)
        h2T = sb("h2T", [128, 8, 128], BF16)
        pst = sb("pst", [128, 4], F32)
        qT = sb("qT", [128, 16, 128], BF16)
        sc = sb("sc", [128, 16, 128], F32)
        scw = sb("scw", [128, 16, 128], F32)
        top = sb("top", [128, 16, 16], F32)
        idx = sb("idx", [128, 16, 16], U32)
        idxf = sb("idxf", [128, 16, 16], F32)
        cand = sb("cand", [128, 8, 256], F32)
        cwk = scw[:].rearrange("p q k -> p (q k)").rearrange("p (h a) -> p h a", h=8)
        ctop = sb("ctop", [128, 8, 16], F32)
        cpos = sb("cpos", [128, 8, 16], U32)
        abu = sb("abu", [128, 2, 128], U32)
        abf = sb("abf", [128, 2, 128], F32)
        oh = sc[:].rearrange("p q k -> p (q k)").rearrange("p (h a) -> p h a", h=8)
        isel = sb("isel", [128, 2, 128], F32)
        eidxf = sb("eidxf", [128, 128], F32)
        eidx = sb("eidx", [128, 2, 128], U32)
        gate = sb("gate", [128, 2, 128], F32)
        gsum = sb("gsum", [128, 2, 8], F32)
        apre = sb("apre", [128, 2, 128], F32)
        wgt = sb("wgt", [128, 2, 128], F32)
        junkA = sb("junkA", [128, D], BF16)
        junkD = sb("junkD", [128, 1, D], BF16)
        diag = sb("diag", [128, 16, 128], BF16)

        P.dma("pool", lambda e: e.dma_start(out=wpq[:], in_=w_peer_q[0].rearrange("(c p) n -> p c n", p=128)),
              "wpq", writes=["wpq"])
        P.dma("sp", lambda e: e.dma_start(out=gffn_b[:], in_=g_ffn.broadcast_to([128, D])), "gffn", writes=["gffn"])
        P.op("pool", lambda e: e.iota(iot[:, 0:16], pattern=[[1, 16]], base=0, channel_multiplier=0), writes=["iot"])
        P.op("dve", lambda e: e.tensor_copy(out=iota16[:], in_=iot[:, 0:16]), reads=["iot"], writes=["iota16"])
        with nc.sbuf_tensor("subk", [128, 16, 128], BF16) as subk:
            P.dma("pool", lambda e: e.dma_start(out=subk[:], in_=peer_subkeys[0].rearrange("h t k d -> k (h t) d")),
                  "subk", writes=["subk"])
            for qg in range(4):
                b = qg % 2
                for jj in range(4):
                    qc = qg * 4 + jj
                    P.op("pe", lambda e: e.transpose(out=psh(b)[:, jj * 128:(jj + 1) * 128], in_=subk[:, qc, :], identity=ident[:]),
                         reads=["subk", "ident"], writes=[("ps", b)])
                evac(subkT[:, qg * 4:(qg + 1) * 4, :], psh(b)[:, 0:512].rearrange("p (a k) -> p a k", a=4), [("ps", b)], ["subkT"])
            P.barrier()
        UV = sb("UV", [128, NR, 2048], BF16)

        def topk_ops(tt):
            p_ = tt % 2
            K = lambda n: (n, p_)
            P.dma("sp", lambda e: e.dma_start(out=x1t[:, p_, :], in_=x1d[tt * 128:(tt + 1) * 128, :]), ("x1t", p_), writes=[K("x1t")])
            P.op("act", lambda e: e.activation(out=junkA[:], in_=x1t[:, p_, :], func=AF.Square, accum_out=pst[:, 0:1]),
                 reads=[K("x1t")], writes=["junkA", "pst"])
            P.op("dve", lambda e: e.tensor_scalar(out=pst[:, 1:2], in0=pst[:, 0:1], scalar1=1.0 / D, scalar2=EPS,
                                                  op0=ALU.mult, op1=ALU.add), reads=["pst"], writes=["pst"])
            P.op("act", lambda e: e.activation(out=pst[:, 2:3], in_=pst[:, 1:2], func=AF.Sqrt), reads=["pst"], writes=["pst"])
            P.op("dve", lambda e: e.reciprocal(out=pst[:, 3:4], in_=pst[:, 2:3]), reads=["pst"], writes=["pst"])
            P.op("dve", lambda e: e.scalar_tensor_tensor(out=h2[:, p_, :], in0=x1t[:, p_, :], scalar=pst[:, 3:4], in1=gffn_b[:],
                                                         op0=ALU.mult, op1=ALU.mult),
                 reads=[K("x1t"), "pst", "gffn"], writes=[K("h2")])
            P.op("act", lambda e: e.activation(out=h2b[:, p_, :], in_=h2[:, p_, :], func=AF.Copy), reads=[K("h2")], writes=[K("h2b")])
            for c in range(8):
                P.op("pe", lambda e: e.transpose(out=psh(0)[:, c * 128:(c + 1) * 128], in_=h2b[:, p_, c * 128:(c + 1) * 128],
                                                 identity=ident[:]), reads=[K("h2b"), "ident"], writes=[("ps", 0)])
            evac(h2T[:].rearrange("p c t -> p (c t)"), psh(0), [("ps", 0)], ["h2T"], eng="act")
            yield
            for qg in range(4):
                b = 1 + qg % 2
                for jj in range(4):
                    qc = qg * 4 + jj
                    for c in range(8):
                        P.op("pe", lambda e: e.matmul(psf(b)[:, jj * 128:(jj + 1) * 128], lhsT=wpq[:, c, qc * 128:(qc + 1) * 128],
                                                      rhs=h2T[:, c, :], start=(c == 0), stop=(c == 7)),
                             reads=["wpq", "h2T"] if c in (0, 7) else [], writes=[("ps", b)])
                evac(qT[:, qg * 4:(qg + 1) * 4, :], psf(b).rearrange("p (a k) -> p a k", a=4), [("ps", b)], [("qT", qg)],
                     eng="act")
            for qg in range(4):
                b = 3
                for jj in range(4):
                    qc = qg * 4 + jj
                    P.op("pe", lambda e: e.matmul(psf(b)[:, jj * 128:(jj + 1) * 128], lhsT=qT[:, qc, :], rhs=subkT[:, qc, :],
                                                  start=True, stop=True),
                         reads=[("qT", qg), "subkT"], writes=[("ps", b)])
                evac(sc[:, qg * 4:(qg + 1) * 4, :], psf(b).rearrange("p (a k) -> p a k", a=4), [("ps", b)], [("sc", qg)],
                     eng="act")
            yield
            for qc in range(16):
                k_sc = ("sc", qc // 4)
                P.op("dve", lambda e: e.max(out=top[:, qc, 0:8], in_=sc[:, qc, :]), reads=[k_sc], writes=[("top", qc)])
                P.op("dve", lambda e: e.max_index(out=idx[:, qc, 0:8], in_max=top[:, qc, 0:8], in_values=sc[:, qc, :]),
                     reads=[k_sc, ("top", qc)], writes=[("idx", qc)])
                P.op("dve", lambda e: e.match_replace(out=scw[:, qc, :], in_to_replace=top[:, qc, 0:8], in_values=sc[:, qc, :],
                                                      imm_value=-1e30), reads=[k_sc, ("top", qc)], writes=[("scw", qc)])
                P.op("dve", lambda e: e.max(out=top[:, qc, 8:16], in_=scw[:, qc, :]), reads=[("scw", qc)], writes=[("top", qc)])
                P.op("dve", lambda e: e.max_index(out=idx[:, qc, 8:16], in_max=top[:, qc, 8:16], in_values=scw[:, qc, :]),
                     reads=[("scw", qc), ("top", qc)], writes=[("idx", qc)])
                if qc % 2 == 1:
                    yield
            TOPK = [("top", qc) for qc in range(16)]
            IDXK = [("idx", qc) for qc in range(16)]
            top_v = top[:].rearrange("p (h t) k -> p h t k", t=2)
            P.op("dve", lambda e: e.tensor_tensor(
                out=cand[:].rearrange("p h (a b) -> p h a b", a=16),
                in0=top_v[:, :, 0, :].unsqueeze(3).broadcast_to([128, 8, 16, 16]),
                in1=top_v[:, :, 1, :].unsqueeze(2).broadcast_to([128, 8, 16, 16]), op=ALU.add),
                reads=TOPK, writes=["cand"])
            for hd in range(8):
                P.op("dve", lambda e: e.max(out=ctop[:, hd, 0:8], in_=cand[:, hd, :]), reads=["cand"], writes=[("ctop", hd)])
                P.op("dve", lambda e: e.max_index(out=cpos[:, hd, 0:8], in_max=ctop[:, hd, 0:8], in_values=cand[:, hd, :]),
                     reads=["cand", ("ctop", hd)], writes=[("cpos", hd)])
                P.op("dve", lambda e: e.match_replace(out=cwk[:, hd, :], in_to_replace=ctop[:, hd, 0:8], in_values=cand[:, hd, :],
                                                      imm_value=-1e30), reads=["cand", ("ctop", hd)], writes=[("scw", 2 * hd), ("scw", 2 * hd + 1)])
                P.op("dve", lambda e: e.max(out=ctop[:, hd, 8:16], in_=cwk[:, hd, :]), reads=[("scw", 2 * hd), ("scw", 2 * hd + 1)], writes=[("ctop", hd)])
                P.op("dve", lambda e: e.max_index(out=cpos[:, hd, 8:16], in_max=ctop[:, hd, 8:16], in_values=cwk[:, hd, :]),
                     reads=[("scw", 2 * hd), ("scw", 2 * hd + 1), ("ctop", hd)], writes=[("cpos", hd)])
                if hd % 2 == 1:
                    yield
            CTOP = [("ctop", hd) for hd in range(8)]
            CPOS = [("cpos", hd) for hd in range(8)]
            gate_v = gate[:, p_, :].rearrange("p (h k) -> p h k", h=8)
            P.op("dve", lambda e: e.tensor_tensor(out=gate_v, in0=ctop[:], in1=ctop[:, :, 0:1].broadcast_to([128, 8, 16]),
                                                  op=ALU.subtract), reads=CTOP, writes=[K("gate")])
            P.op("act", lambda e: e.activation(out=gate[:, p_, :], in_=gate[:, p_, :], func=AF.Exp), reads=[K("gate")], writes=[K("gate")])
            P.op("dve", lambda e: e.tensor_reduce(out=gsum[:, 0, :], in_=gate_v, axis=AX.X, op=ALU.add),
                 reads=[K("gate")], writes=["gsum"])
            P.op("dve", lambda e: e.reciprocal(out=gsum[:, 1, :], in_=gsum[:, 0, :]), reads=["gsum"], writes=["gsum"])
            P.op("dve", lambda e: e.tensor_tensor(out=gate_v, in0=gate_v,
                                                  in1=gsum[:, 1, :].unsqueeze(2).broadcast_to([128, 8, 16]), op=ALU.mult),
                 reads=[K("gate"), "gsum"], writes=[K("gate")])
            yield
            cpf = cpos[:].rearrange("p h k -> p (h k)")
            P.op("dve", lambda e: e.tensor_single_scalar(out=abu[:, 0, :], in_=cpf, scalar=4, op=ALU.logical_shift_right),
                 reads=CPOS, writes=["abu0"])
            P.op("dve", lambda e: e.tensor_single_scalar(out=abu[:, 1, :], in_=cpf, scalar=15, op=ALU.bitwise_and),
                 reads=CPOS, writes=["abu1"])
            P.op("dve", lambda e: e.tensor_copy(out=abf[:], in_=abu[:]), reads=["abu0", "abu1"], writes=["abf"])
            P.op("dve", lambda e: e.tensor_copy(out=idxf[:], in_=idx[:]), reads=IDXK, writes=["idxf"])
            idxf_v = idxf[:].rearrange("p (h t) k -> p h t k", t=2)
            oh_v = oh[:].rearrange("p h (k a) -> p h k a", k=16)
            for t_ in range(2):
                P.op("dve", lambda e: e.tensor_tensor(
                    out=oh_v, in0=abf[:, t_, :].rearrange("p (h k) -> p h k", h=8).unsqueeze(3).broadcast_to([128, 8, 16, 16]),
                    in1=iota16[:, :].unsqueeze(1).unsqueeze(1).broadcast_to([128, 8, 16, 16]), op=ALU.is_equal),
                    reads=["abf", "iota16"], writes=[("sc", q_) for q_ in range(4)])
                P.op("dve", lambda e: e.tensor_tensor(
                    out=oh_v, in0=oh_v, in1=idxf_v[:, :, t_, :].unsqueeze(2).broadcast_to([128, 8, 16, 16]), op=ALU.mult),
                    reads=[("sc", q_) for q_ in range(4)] + ["idxf"], writes=[("sc", q_) for q_ in range(4)])
                P.op("dve", lambda e: e.tensor_reduce(out=isel[:, t_, :].rearrange("p (h k) -> p h k", h=8), in_=oh_v,
                                                      axis=AX.X, op=ALU.add), reads=[("sc", q_) for q_ in range(4)], writes=[("isel", t_)])
                yield
            P.op("dve", lambda e: e.scalar_tensor_tensor(out=eidxf[:], in0=isel[:, 0, :], scalar=128.0, in1=isel[:, 1, :],
                                                         op0=ALU.mult, op1=ALU.add),
                 reads=[("isel", 0), ("isel", 1)], writes=["eidxf"])
            P.op("dve", lambda e: e.tensor_copy(out=eidx[:, p_, :], in_=eidxf[:]), reads=["eidxf"], writes=[K("eidx")])

        cvt_h = cvt_state["h"]
        for _ in topk_ops(0):
            pass
        if debug == "peer_idx":
            d1 = dbg_out("eidx", [128, 128])
            d2 = dbg_out("gate", [128, 128])
            P.dma("sp", lambda e: e.dma_start(out=d1, in_=eidxf[:]), "dbg1", reads=["eidxf"])
            P.dma("sp", lambda e: e.dma_start(out=d2, in_=gate[:, 0, :]), "dbg2", reads=[("gate", 0)])
            P.barrier(final=True)
            return nc, P
        NG = 128 // GS
        TOT = NT * NG
        DLY = 1
        ginfo = {}
        gen = {"g": None}

        def peer_gather(G):
            tt, grp = G // NG, G % NG
            p_ = tt % 2
            slots = []
            for k_ in range(GS):
                s_ = grp * GS + k_
                r = (G * GS + k_) % NR
                slots.append((s_, r))
                P.dma("pool", lambda e: e.indirect_dma_start(
                    out=UV[:, r, :], out_offset=None, in_=uvb,
                    in_offset=bass.IndirectOffsetOnAxis(ap=eidx[:, p_, s_:s_ + 1], axis=0)),
                    ("UV", r), reads=[("eidx", p_)], writes=[("UV", r)], extra=[cvt_h])
            ginfo[G] = [slots, None, None]

        def peer_dots(G, k0, k1):
            tt, grp = G // NG, G % NG
            p_ = tt % 2
            info = ginfo[G]
            for (s_, r) in info[0][k0:k1]:
                if FUSE_EVERY and s_ % FUSE_EVERY == 0:
                    jd = 0
                    jcnt[0] += 1
                    info[1] = P.op("dve", lambda e: e.scalar_tensor_tensor(out=junkD[:, jd, :], in0=UV[:, r, 0:1024], scalar=1.0,
                                                                           in1=h2[:, p_, :], op0=ALU.mult, op1=ALU.mult,
                                                                           accum_out=apre[:, p_, s_:s_ + 1]),
                                   reads=[("UV", r), ("h2", p_)], writes=[("junkD", jd), ("apre", p_, s_)])
                else:
                    pr = pcnt[0] % 4
                    jb = pcnt[0] % 2
                    pcnt[0] += 1
                    P.op("dve", lambda e: e.tensor_tensor(out=prodr[:, pr, :], in0=UV[:, r, 0:1024], in1=h2b[:, p_, :],
                                                          op=ALU.mult),
                         reads=[("UV", r), ("h2b", p_)], writes=[("prodr", pr)])
                    info[2] = P.op("act", lambda e: e.activation(out=junkB[:, jb, :], in_=prodr[:, pr, :], func=AF.Copy,
                                                                 accum_out=apre[:, p_, s_:s_ + 1], saturate=False),
                                   reads=[("prodr", pr)], writes=[("junkB", jb), ("apre", p_, s_)])

        def peer_gelu(G):
            tt, grp = G // NG, G % NG
            p_ = tt % 2
            gsl = slice(grp * GS, (grp + 1) * GS)
            P.op("act", lambda e: e.activation(out=wgt[:, p_, gsl], in_=apre[:, p_, gsl], func=AF.Gelu),
                 extra=[ginfo[G][1], ginfo[G][2]], reads=[("apre", p_, s_) for s_ in range(grp * GS, (grp + 1) * GS)],
                 writes=[("wgt", p_, grp)])

        def peer_back(G):
            tt, grp = G // NG, G % NG
            p_ = tt % 2
            ab = 4 + 2 * p_
            slots = ginfo.pop(G)[0]
            gsl = slice(grp * GS, (grp + 1) * GS)
            P.op("dve", lambda e: e.tensor_tensor(out=wgt[:, p_, gsl], in0=wgt[:, p_, gsl], in1=gate[:, p_, gsl], op=ALU.mult),
                 reads=[("wgt", p_, grp), ("gate", p_)], writes=[("wgt", p_, grp)])
            for (s_, r) in slots:
                dsl = s_ % 16
                P.op("act", lambda e: e.activation(out=diag[:, dsl, :], in_=ident[:], func=AF.Copy,
                                                   scale=wgt[:, p_, s_:s_ + 1]),
                     reads=[("wgt", p_, grp), "ident"], writes=[("diag", dsl)])
                for half in range(2):
                    P.op("pe", lambda e: e.matmul(psf(ab + half), lhsT=diag[:, dsl, :],
                                                  rhs=UV[:, r, 1024 + half * 512:1024 + (half + 1) * 512],
                                                  start=(s_ == 0), stop=(s_ == 127)),
                         reads=[("diag", dsl), ("UV", r)], writes=[("ps", ab + half)])
            if grp == NG - 1:
                for half in range(2):
                    P.op("dve", lambda e: e.tensor_tensor(out=outt[:, half * 512:(half + 1) * 512], in0=psf(ab + half),
                                                          in1=x1t[:, p_, half * 512:(half + 1) * 512], op=ALU.add),
                         reads=[("ps", ab + half), ("x1t", p_)], writes=[("outt", half)])
                P.dma("sp", lambda e: e.dma_start(out=out[tt * 128:(tt + 1) * 128, :], in_=outt[:]), "outd",
                      reads=[("outt", 0), ("outt", 1)], writes=[("out", tt)])

        HALF = GS // 2
        if debug == "peer_gatheronly":
            peer_dots = lambda *a: None
            peer_gelu = lambda *a: None
            peer_back = lambda G: ginfo.pop(G)
        for G in range(TOT + 1):
            tt, grp = G // NG, G % NG
            if G < TOT:
                if grp == 1 and tt + 1 < NT:
                    gen["g"] = topk_ops(tt + 1)
                peer_gather(G)
                peer_dots(G, 0, HALF)
            if G >= 1:
                peer_back(G - 1)
            if G < TOT:
                peer_dots(G, HALF, GS)
                peer_gelu(G)
                if gen["g"] is not None and grp >= 1:
                    if grp == NG - 1:
                        for _ in gen["g"]:
                            pass
                        gen["g"] = None
                    else:
                        next(gen["g"], None)
        P.barrier(final=True)

    P.barrier(final=True)
    return nc, P


_CACHE = {}


def kernel(**inputs):
    if "nc" not in _CACHE:
        _CACHE["nc"] = build_program()[0]
    nc = _CACHE["nc"]
    n = 8
    in_maps = []
    for b in range(n):
        m = {}
        for k, v in inputs.items():
            a = np.asarray(v)
            if k in ("x", "mem"):
                a = a[b]
            m[k] = np.ascontiguousarray(a, dtype=np.float32)
        in_maps.append(m)
    res = run_bass_kernel_spmd(nc, in_maps, core_ids=list(range(n)))
    return np.stack([np.asarray(r["out"], dtype=np.float32) for r in res.results], axis=0)
```

```python
import numpy as np
from contextlib import ExitStack
import concourse.bass as bass
import concourse.mybir as mybir
from concourse.alu_op_type import AluOpType as ALU
from concourse.bass_utils import run_bass_kernel_spmd

AF = mybir.ActivationFunctionType
AX = mybir.AxisListType
F32 = mybir.dt.float32
BF16 = mybir.dt.bfloat16
I32 = mybir.dt.int32
U32 = mybir.dt.uint32

S = 2048
D = 1024
NT = 16
EPS = 1e-6
MEM = 256


class Prog:
    ENG = ("pe", "act", "dve", "pool", "sp")

    def __init__(self, nc):
        self.nc = nc
        self.eng = {"pe": nc.tensor, "act": nc.scalar, "dve": nc.vector,
                    "pool": nc.gpsimd, "sp": nc.sync}
        self.sem = {e: nc.alloc_semaphore("s_" + e) for e in self.ENG}
        self.cnt = {e: 0 for e in self.ENG}
        self.seen = {e: {} for e in self.ENG}
        self.last_w = {}
        self.readers = {}
        self.dsem = {}
        self.dcnt = {}
        self.ninst = 0
        self.background = set()

    def _semobj(self, sk):
        return self.sem[sk] if sk in self.sem else self.dsem[sk]

    def _wait(self, eng, deps):
        need = {}
        for d in deps:
            if d is None:
                continue
            sk, v = d
            if need.get(sk, 0) < v:
                need[sk] = v
        seen = self.seen[eng]
        for sk, v in need.items():
            if seen.get(sk, 0) < v:
                self.eng[eng].wait_ge(self._semobj(sk), v)
                seen[sk] = v

    def _deps(self, reads, writes):
        deps = []
        for k in reads:
            if k in self.last_w:
                deps.append(self.last_w[k])
        for k in writes:
            if k in self.last_w:
                deps.append(self.last_w[k])
            deps.extend(self.readers.get(k, {}).items())
        return deps

    def _commit(self, h, reads, writes):
        for k in writes:
            self.last_w[k] = h
            self.readers[k] = {}
        for k in reads:
            r = self.readers.setdefault(k, {})
            if r.get(h[0], 0) < h[1]:
                r[h[0]] = h[1]

    def op(self, eng, fn, reads=(), writes=(), extra=()):
        deps = self._deps(reads, writes) + list(extra)
        if eng == "pe":
            deps = [d for d in deps if d is not None and d[0] != "pe"]
        self._wait(eng, deps)
        inst = fn(self.eng[eng])
        self.cnt[eng] += 1
        inst.then_inc(self.sem[eng], 1)
        h = (eng, self.cnt[eng])
        self._commit(h, reads, writes)
        self.ninst += 1
        return h

    def dma(self, q, fn, semname, reads=(), writes=(), extra=()):
        if semname not in self.dsem:
            self.dsem[semname] = self.nc.alloc_semaphore("d_" + str(semname))
            self.dcnt[semname] = 0
        deps = self._deps(reads, writes) + list(extra)
        deps = [d for d in deps if d is not None and d[0] != semname]
        self._wait(q, deps)
        inst = fn(self.eng[q])
        self.dcnt[semname] += 16
        inst.then_inc(self.dsem[semname], 16)
        h = (semname, self.dcnt[semname])
        self._commit(h, reads, writes)
        self.ninst += 1
        return h

    def barrier(self, final=False):
        allh = [(e, self.cnt[e]) for e in self.ENG if self.cnt[e] > 0]
        allh += [(s, c) for s, c in self.dcnt.items() if c > 0 and (final or s not in self.background)]
        for e in self.ENG:
            self._wait(e, allh)
        self.last_w = {}
        self.readers = {}


def build_program(debug=None):
    nc = bass.Bass("TRN2", target_bir_lowering=False)
    P = Prog(nc)

    def din(name, shape, dt=F32):
        return nc.dram_tensor(name, shape, dt, kind="ExternalInput").ap()

    x = din("x", [S, D])
    mem = din("mem", [MEM, D])
    g_mix = din("g_mix", [1, D])
    g_mem = din("g_mem", [1, D])
    w_in = din("w_in", [1, D, 4352])
    w_mem_kv = din("w_mem_kv", [1, D, 1024])
    g_q_dil = din("g_q_dil", [1, 64])
    g_k_dil = din("g_k_dil", [1, 64])
    g_q_mem = din("g_q_mem", [1, 128])
    g_k_mem = din("g_k_mem", [1, 128])
    w_o_sb = din("w_o_sb", [1, 512, D])
    w_o_dil = din("w_o_dil", [1, 256, D])
    w_o_mem = din("w_o_mem", [1, 512, D])
    w_gate = din("w_gate", [1, D, 3072])
    b_gate = din("b_gate", [1, 3072])
    w_out = din("w_out", [1, D, D])
    g_ffn = din("g_ffn", [1, D])
    w_peer_q = din("w_peer_q", [1, D, 2048])
    peer_subkeys = din("peer_subkeys", [1, 8, 2, 128, 128])
    peer_u = din("peer_u", [1, 16384, D])
    peer_v = din("peer_v", [1, 16384, D])
    out = nc.dram_tensor("out", [S, D], F32, kind="ExternalOutput").ap()
    dbg = {}

    def dbg_out(name, shape):
        dbg[name] = nc.dram_tensor("dbg_" + name, shape, F32, kind="ExternalOutput").ap()
        return dbg[name]

    win_r = w_in[0].rearrange("(c p) n -> p c n", p=128)

    psall = nc.alloc_psum_tensor("psall", [128, 8, 512], F32)

    def psf(b):
        return psall[:, b, :]

    def psh(b):
        return psall[:, b, :].bitcast(BF16)

    ident = nc.alloc_sbuf_tensor("ident", [128, 128], BF16)
    iot = nc.alloc_sbuf_tensor("iot", [128, 128], I32)
    nident = nc.alloc_sbuf_tensor("nident", [128, 128], BF16)
    umask = nc.alloc_sbuf_tensor("umask", [128, 128], U32)
    onesf = nc.alloc_sbuf_tensor("onesf", [128, 128], F32)
    gmixT = nc.alloc_sbuf_tensor("gmixT", [128, 8], F32)
    gmemT = nc.alloc_sbuf_tensor("gmemT", [128, 8], F32)
    epsc = nc.alloc_sbuf_tensor("epsc", [128, 1], F32)
    blockones = nc.alloc_sbuf_tensor("blockones", [128, 128], BF16)
    ones_bf = nc.alloc_sbuf_tensor("ones_bf", [128, 128], BF16)
    gdil = nc.alloc_sbuf_tensor("gdil", [128, 2], F32)
    gmemh = nc.alloc_sbuf_tensor("gmemh", [128, 2], F32)
    es = ExitStack()
    hT = es.enter_context(nc.sbuf_tensor("hT", [128, 8, S], BF16))
    osbT = es.enter_context(nc.sbuf_tensor("osbT", [128, 4, S], BF16))

    odilT = es.enter_context(nc.sbuf_tensor("odilT", [128, 2, S], BF16))
    omemT = es.enter_context(nc.sbuf_tensor("omemT", [128, 4, S], BF16))
    P.op("dve", lambda e: e.memset(epsc[:], EPS), writes=["consts"])
    P.op("dve", lambda e: e.memset(blockones[:], 0.0), writes=["consts"])
    P.op("dve", lambda e: e.memset(blockones[0:64, 0:64], 1.0), writes=["consts"])
    P.op("dve", lambda e: e.memset(blockones[64:128, 64:128], 1.0), writes=["consts"])
    P.op("dve", lambda e: e.memset(ones_bf[:], 1.0), writes=["consts"])
    with nc.allow_non_contiguous_dma(reason="tiny gain vectors"):
        P.dma("sp", lambda e: e.dma_start(out=gmemT[:], in_=g_mem.rearrange("o (c p) -> p (o c)", p=128)),
              "gmemT", writes=["gmemT"])
        for half in range(2):
            P.dma("sp", lambda e: e.dma_start(out=gdil[half * 64:(half + 1) * 64, 0:1], in_=g_q_dil.rearrange("o d -> d o")),
                  "gsm", writes=["consts"])
            P.dma("sp", lambda e: e.dma_start(out=gdil[half * 64:(half + 1) * 64, 1:2], in_=g_k_dil.rearrange("o d -> d o")),
                  "gsm", writes=["consts"])
        P.dma("sp", lambda e: e.dma_start(out=gmemh[:, 0:1], in_=g_q_mem.rearrange("o d -> d o")), "gsm", writes=["consts"])
        P.dma("sp", lambda e: e.dma_start(out=gmemh[:, 1:2], in_=g_k_mem.rearrange("o d -> d o")), "gsm", writes=["consts"])

    P.op("pool", lambda e: e.iota(iot[:], pattern=[[1, 128]], base=0, channel_multiplier=-1), writes=["iot"])
    P.op("dve", lambda e: e.tensor_single_scalar(out=ident[:], in_=iot[:], scalar=0.0, op=ALU.is_equal),
         reads=["iot"], writes=["ident"])
    P.op("dve", lambda e: e.tensor_scalar(out=nident[:], in0=iot[:], scalar1=0.0, scalar2=-1.0, op0=ALU.is_equal, op1=ALU.mult),
         reads=["iot"], writes=["ident"])
    P.op("dve", lambda e: e.tensor_single_scalar(out=umask[:], in_=iot[:], scalar=0.0, op=ALU.is_ge),
         reads=["iot"], writes=["umask"])
    P.op("dve", lambda e: e.memset(onesf[:], 1.0), writes=["umask"])
    with nc.allow_non_contiguous_dma(reason="tiny gain vectors"):
        P.dma("sp", lambda e: e.dma_start(out=gmixT[:], in_=g_mix.rearrange("o (c p) -> p (o c)", p=128)),
              "gmixT", writes=["gmixT"])

    fill_one = nc.gpsimd.to_reg(1.0)

    uvb = nc.dram_tensor("uvb", [16384, 2048], BF16).ap()
    P.background.add("cvt")
    cvt_state = {"k": 0, "h": None}

    def cvt_step(n):
        for _ in range(n):
            k = cvt_state["k"]
            if k >= 16:
                return
            cvt_state["k"] += 1
            src = (peer_u, peer_v)[k % 2][0]
            rows = slice((k // 2) * 2048, (k // 2 + 1) * 2048)
            cvt_state["h"] = P.dma("pool", lambda e: e.dma_start(out=uvb[rows, (k % 2) * 1024:(k % 2 + 1) * 1024],
                                                                 in_=src[rows, :]), "cvt")

    def rms_to_T(src, ntiles, gT, dstT, tag):
        with nc.sbuf_tensor(tag + "_xs", [128, 2, D], F32) as xs, \
             nc.sbuf_tensor(tag + "_xn", [128, 2, D], BF16) as xn, \
             nc.sbuf_tensor(tag + "_junk", [128, D], BF16) as junk, \
             nc.sbuf_tensor(tag + "_st", [128, 4, ntiles], F32) as st:
            for tt in range(ntiles):
                sl = tt % 2
                P.dma("sp", lambda e: e.dma_start(out=xs[:, sl, :], in_=src[tt * 128:(tt + 1) * 128, :]),
                      (tag, "xs", sl), writes=[(tag, "xs", sl)])
                P.op("act", lambda e: e.activation(out=junk[:], in_=xs[:, sl, :], func=AF.Square,
                                                   accum_out=st[:, 0, tt:tt + 1]),
                     reads=[(tag, "xs", sl)], writes=[(tag, "junk"), (tag, "st", tt)])
                P.op("act", lambda e: e.activation(out=st[:, 1, tt:tt + 1], in_=st[:, 0, tt:tt + 1], func=AF.Ln,
                                                   scale=1.0 / D, bias=epsc[:, 0:1]),
                     reads=[(tag, "st", tt), "consts"], writes=[(tag, "st", tt)])
                P.op("act", lambda e: e.activation(out=st[:, 3, tt:tt + 1], in_=st[:, 1, tt:tt + 1], func=AF.Exp, scale=-0.5),
                     reads=[(tag, "st", tt)], writes=[(tag, "st", tt)])
                P.op("act", lambda e: e.activation(out=xn[:, sl, :], in_=xs[:, sl, :], func=AF.Copy,
                                                   scale=st[:, 3, tt:tt + 1]),
                     reads=[(tag, "xs", sl), (tag, "st", tt)], writes=[(tag, "xn", sl)])
                b = tt % 2
                for c in range(8):
                    P.op("pe", lambda e: e.transpose(out=psh(b)[:, c * 128:(c + 1) * 128],
                                                     in_=xn[:, sl, c * 128:(c + 1) * 128], identity=ident[:]),
                         reads=[(tag, "xn", sl), "ident"], writes=[("ps", b)])
                P.op("dve", lambda e: e.tensor_tensor(
                    out=dstT[:, :, tt * 128:(tt + 1) * 128],
                    in0=psh(b).rearrange("p (c t) -> p c t", c=8),
                    in1=gT[:, :].unsqueeze(2).broadcast_to([128, 8, 128]), op=ALU.mult),
                    reads=[("ps", b), "gmixT", "gmemT"], writes=[(tag, "dstT", tt)])
            P.barrier()

    rms_to_T(x, NT, gmixT, hT, "x")
    HT_KEYS = [("x", "dstT", tt) for tt in range(NT)]

    cp_toggle = [0]

    def evac(out_ap, in_ap, reads, writes, eng=None):
        if eng is None:
            eng = ("act", "dve")[cp_toggle[0] % 2]
            cp_toggle[0] += 1
        if eng == "act":
            return P.op("act", lambda e: e.activation(out=out_ap, in_=in_ap, func=AF.Copy), reads=reads, writes=writes)
        return P.op(eng, lambda e: e.tensor_copy(out=out_ap, in_=in_ap), reads=reads, writes=writes)

    NSB = 4
    with ExitStack() as sbs:
        qkT = sbs.enter_context(nc.sbuf_tensor("qkT", [128, 8, S], BF16))
        vsb = sbs.enter_context(nc.sbuf_tensor("vsb", [128, NT, 512], BF16))
        wsb_cm = nc.sbuf_tensor("wsb", [128, 8, 1536], BF16)
        wsb = wsb_cm.__enter__()
        P.dma("pool", lambda e: e.dma_start(out=wsb[:], in_=win_r[:, :, 0:1536]), "wsb", writes=["wsb"])
        WSB = ["wsb"]
        nb = 0
        for fc in range(8):
            for tg in range(4):
                b = nb % 4
                nb += 1
                for c in range(8):
                    P.op("pe", lambda e: e.matmul(psf(b), lhsT=wsb[:, c, fc * 128:(fc + 1) * 128],
                                                  rhs=hT[:, c, tg * 512:(tg + 1) * 512], start=(c == 0), stop=(c == 7)),
                         reads=WSB + HT_KEYS if c in (0, 7) else [], writes=[("ps", b)])
                evac(qkT[:, fc, tg * 512:(tg + 1) * 512], psf(b), [("ps", b)], [("qkT", fc, tg)])
        for tt in range(NT):
            b = nb % 4
            nb += 1
            for c in range(8):
                P.op("pe", lambda e: e.matmul(psf(b), lhsT=hT[:, c, tt * 128:(tt + 1) * 128],
                                              rhs=wsb[:, c, 1024:1536], start=(c == 0), stop=(c == 7)),
                     reads=WSB + HT_KEYS if c in (0, 7) else [], writes=[("ps", b)])
            evac(vsb[:, tt, :], psf(b), [("ps", b)], [("vsb", tt)])

        P.barrier()
        wsb_cm.__exit__(None, None, None)
        mt = sbs.enter_context(nc.sbuf_tensor("mt", [128, NSB, S + 8], F32))
        Abf = sbs.enter_context(nc.sbuf_tensor("Abf", [128, NSB, S + 8], BF16))
        ATs = sbs.enter_context(nc.sbuf_tensor("ATs", [128, NSB, S], BF16))
        st = {"zb": 0, "tb": 0}

        def sb_stage_a(h, qi, sl):
            ch = h // 2
            pb = (h % 2) * 64
            nk = (qi + 1) * 128
            nchunk = (nk + 511) // 512
            qkeys = [("qkT", ch, qi // 4)]
            for cc in range(nchunk):
                w = min(512, nk - cc * 512)
                b = st["zb"] % 4
                st["zb"] += 1
                P.op("pe", lambda e: e.matmul(psf(b)[:, 0:w], lhsT=qkT[pb:pb + 64, ch, qi * 128:(qi + 1) * 128],
                                              rhs=qkT[pb:pb + 64, 4 + ch, cc * 512:cc * 512 + w],
                                              start=True, stop=True),
                     reads=qkeys + [("qkT", 4 + ch, cc)], writes=[("ps", b)])
                P.op("act", lambda e: e.activation(out=mt[:, sl, cc * 512:cc * 512 + w], in_=psf(b)[:, 0:w],
                                                   func=AF.Sigmoid, scale=-0.125),
                     reads=[("ps", b)], writes=[("mt", sl)])
            P.op("pool", lambda e: e.affine_select(out=mt[:, sl, qi * 128:(qi + 1) * 128],
                                                   in_=mt[:, sl, qi * 128:(qi + 1) * 128],
                                                   pattern=[[-1, 128]], compare_op=ALU.is_gt, fill=fill_one,
                                                   base=0, channel_multiplier=1),
                 reads=[("mt", sl)], writes=[("mt", sl)])
            P.op("pool", lambda e: e.memset(Abf[:, sl, nk:nk + 1], 1.0), reads=[], writes=[("Abf", sl)])

        def sb_stage_a2(h, qi, sl):
            nk = (qi + 1) * 128
            rev = mt[:, sl, nk - 1::-1]
            P.op("dve", lambda e: e.tensor_tensor_scan(out=Abf[:, sl, nk - 1::-1], data0=rev, data1=rev, initial=1.0,
                                                       op0=ALU.mult, op1=ALU.bypass),
                 reads=[("mt", sl)], writes=[("Abf", sl)])

        def sb_stage_b(h, qi, sl):
            ch = h // 2
            pb = (h % 2) * 64
            for kb0 in range(0, qi + 1, 4):
                nblk = min(4, qi + 1 - kb0)
                b = 4 + st["tb"] % 2
                st["tb"] += 1
                for j in range(nblk):
                    kb = kb0 + j
                    P.op("pe", lambda e: e.matmul(psf(b)[:, j * 128:(j + 1) * 128],
                                                  lhsT=Abf[:, sl, kb * 128 + 1:(kb + 1) * 128 + 1], rhs=ident[:],
                                                  start=True, stop=False),
                         reads=[("Abf", sl), "ident"], writes=[("ps", b)])
                    P.op("pe", lambda e: e.matmul(psf(b)[:, j * 128:(j + 1) * 128],
                                                  lhsT=Abf[:, sl, kb * 128:(kb + 1) * 128], rhs=nident[:],
                                                  start=False, stop=True),
                         reads=[("Abf", sl), "ident"], writes=[("ps", b)])
                evac(ATs[:, sl, kb0 * 128:(kb0 + nblk) * 128], psf(b)[:, 0:nblk * 128],
                     [("ps", b)], [("ATs", sl, kb0)], eng="act")

        def sb_stage_c(h, qi, sl):
            ch = h // 2
            pb = (h % 2) * 64
            ab = 6 + (qi // 4) % 2
            for kb in range(qi + 1):
                P.op("pe", lambda e: e.matmul(psf(ab)[pb:pb + 64, (qi % 4) * 128:(qi % 4 + 1) * 128],
                                              lhsT=vsb[:, kb, h * 64:(h + 1) * 64],
                                              rhs=ATs[:, sl, kb * 128:(kb + 1) * 128],
                                              start=(kb == 0), stop=(kb == qi)),
                     reads=[("vsb", kb), ("ATs", sl, (kb // 4) * 4)], writes=[("ps", ab)])
            if qi % 4 == 3:
                evac(osbT[pb:pb + 64, ch, (qi - 3) * 128:(qi + 1) * 128], psf(ab)[pb:pb + 64, :],
                     [("ps", ab)], [("osbT", h, qi // 4)], eng="act")

        units = [(h, qi) for h in range(8) for qi in range(NT)]
        nun = len(units)
        for u in range(nun + 4):
            if u < nun:
                if units[u][1] == 0:
                    cvt_step(1)
                sb_stage_a(units[u][0], units[u][1], u % NSB)
            for lag, fn in ((2, sb_stage_a2), (3, sb_stage_b), (4, sb_stage_c)):
                v = u - lag
                if 0 <= v < nun:
                    fn(units[v][0], units[v][1], v % NSB)
        P.barrier()

    if debug == "sb":
        d1 = dbg_out("osbT", [128, 4 * S])
        d2 = dbg_out("hT", [128, 8 * S])
        P.dma("pool", lambda e: e.dma_start(out=d1, in_=osbT[:].rearrange("p c t -> p (c t)"), max_dma_last_dim=4096), "dbg1")
        P.dma("pool", lambda e: e.dma_start(out=d2, in_=hT[:].rearrange("p c t -> p (c t)"), max_dma_last_dim=4096), "dbg2")
        P.barrier(final=True)
        return nc, P

    def head_norm(b_in, b_ss, w, onesT, inv_n, gcol, dst_ap, sq, t1, tagk, dst_key):
        P.op("act", lambda e: e.activation(out=sq[:, 0:w], in_=psf(b_in)[:, 0:w], func=AF.Square),
             reads=[("ps", b_in)], writes=[(tagk, "sq")])
        P.op("pe", lambda e: e.matmul(psf(b_ss)[:, 0:w], lhsT=onesT, rhs=sq[:, 0:w], start=True, stop=True),
             reads=[(tagk, "sq"), "consts"], writes=[("ps", b_ss)])
        P.op("act", lambda e: e.activation(out=t1[:, 0:w], in_=psf(b_ss)[:, 0:w], func=AF.Ln, scale=inv_n,
                                           bias=epsc[:, 0:1]),
             reads=[("ps", b_ss), "consts"], writes=[(tagk, "t1")])
        P.op("act", lambda e: e.activation(out=t1[:, 0:w], in_=t1[:, 0:w], func=AF.Exp, scale=-0.5),
             reads=[(tagk, "t1")], writes=[(tagk, "t1")])
        P.op("dve", lambda e: e.scalar_tensor_tensor(out=dst_ap, in0=psf(b_in)[:, 0:w], scalar=gcol,
                                                     in1=t1[:, 0:w], op0=ALU.mult, op1=ALU.mult),
             reads=[("ps", b_in), (tagk, "t1"), "consts"], writes=[dst_key])

    DIL_R = (1, 4, 16)
    with nc.sbuf_tensor("dtab", [128, 6, 512], F32) as dtab, \
         nc.sbuf_tensor("gapi", [128, 256], I32) as gapi, \
         nc.sbuf_tensor("gapf", [128, 256], F32) as gapf, \
         nc.sbuf_tensor("maskf", [128, 256], F32) as maskf, \
         nc.sbuf_tensor("wd", [128, 8, 9, 128], BF16) as wd, \
         nc.sbuf_tensor("dq", [128, 3, S], BF16) as dq, \
         nc.sbuf_tensor("dk", [128, 3, S], BF16) as dk, \
         nc.sbuf_tensor("dv", [128, 3, 16, 128], BF16) as dv, \
         nc.sbuf_tensor("ND", [128, 2, S], F32) as ND, \
         nc.sbuf_tensor("dsq", [128, 512], BF16) as dsq, \
         nc.sbuf_tensor("dt1", [128, 512], F32) as dt1, \
         nc.sbuf_tensor("dE", [128, 3, 512], F32) as dE, \
         nc.sbuf_tensor("dPT", [128, 3, 512], BF16) as dPT:
        P.op("pool", lambda e: e.iota(gapi[:].rearrange("p (a q) -> p a q", a=2), pattern=[[128, 2], [1, 128]],
                                      base=0, channel_multiplier=-1), writes=["gapi"])
        P.op("dve", lambda e: e.tensor_copy(out=gapf[:], in_=gapi[:]), reads=["gapi"], writes=["gapf"])
        P.op("dve", lambda e: e.tensor_single_scalar(out=maskf[:, 0:128], in_=gapf[:, 0:128], scalar=0.0, op=ALU.is_ge),
             reads=["gapf"], writes=["maskf0"])
        P.op("dve", lambda e: e.tensor_single_scalar(out=maskf[:, 128:256], in_=gapf[:, 128:256], scalar=128.0,
                                                     op=ALU.is_le), reads=["gapf"], writes=["maskf1"])
        P.op("dve", lambda e: e.tensor_scalar_max(out=gapf[:, 0:128], in0=gapf[:, 0:128], scalar1=0.0),
             reads=["gapf", "maskf0"], writes=["gapf"])
        for g in range(3):
            for j in range(2):
                for e_ in range(2):
                    H = 4 * g + 2 * j + e_
                    slope = 2.0 ** (-8.0 * (H + 1) / 12.0)
                    for part in range(2):
                        col = (e_ * 2 + part) * 128
                        P.op("act", lambda e: e.activation(out=dtab[:, g * 2 + j, col:col + 128],
                                                           in_=gapf[:, part * 128:(part + 1) * 128], func=AF.Exp,
                                                           scale=-slope * DIL_R[g]),
                             reads=["gapf"], writes=[("dtab", g, j, part, e_)])
                        P.op("dve", lambda e: e.tensor_tensor(out=dtab[:, g * 2 + j, col:col + 128],
                                                              in0=dtab[:, g * 2 + j, col:col + 128],
                                                              in1=maskf[:, part * 128:(part + 1) * 128], op=ALU.mult),
                             reads=[("dtab", g, j, part, e_), "maskf0", "maskf1"], writes=[("dtab", g, j, part, e_)])
        DTAB = lambda g, j: [("dtab", g, j, pp, ee) for pp in range(2) for ee in range(2)]
        if debug == "dil_a":
            P.barrier(final=True)
            return nc, P

        for j in range(2):
            for t3 in range(3):
                for g in range(3):
                    col = 1536 + t3 * 768 + (2 * g + j) * 128
                    P.dma("pool", lambda e: e.dma_start(out=wd[:, :, t3 * 3 + g, :], in_=win_r[:, :, col:col + 128]),
                          ("wd", t3 * 3 + g), writes=[("wd", t3 * 3 + g)])
            cvt_step(4)
            nb = 0
            for t3 in range(2):
                dst = dq if t3 == 0 else dk
                for g in range(3):
                    for tg in range(4):
                        b = nb % 3
                        nb += 1
                        for c in range(8):
                            P.op("pe", lambda e: e.matmul(psf(b), lhsT=wd[:, c, t3 * 3 + g, :],
                                                          rhs=hT[:, c, tg * 512:(tg + 1) * 512],
                                                          start=(c == 0), stop=(c == 7)),
                                 reads=[("wd", t3 * 3 + g)] if c in (0, 7) else [], writes=[("ps", b)])
                        head_norm(b, 3, 512, blockones[:], 1.0 / 64, gdil[:, t3:t3 + 1],
                                  dst[:, g, tg * 512:(tg + 1) * 512], dsq, dt1, "dil", ("dqk", t3, g, tg))
            if debug == "dil_b":
                P.barrier(final=True)
                return nc, P
            for g in range(3):
                r = DIL_R[g]
                nbk = 16 // r
                for bi0 in range(0, 16, 4):
                    b = nb % 3
                    nb += 1
                    for jj in range(4):
                        bi = bi0 + jj
                        c_, n_ = bi // nbk, bi % nbk
                        st_ = c_ + r * 128 * n_
                        for c in range(8):
                            P.op("pe", lambda e: e.matmul(psf(b)[:, jj * 128:(jj + 1) * 128],
                                                          lhsT=hT[:, c, st_:st_ + 127 * r + 1:r],
                                                          rhs=wd[:, c, 6 + g, :], start=(c == 0), stop=(c == 7)),
                                 reads=[("wd", 6 + g)] if c in (0, 7) else [], writes=[("ps", b)])
                    evac(dv[:, g, bi0:bi0 + 4, :], psf(b).rearrange("p (a f) -> p a f", a=4), [("ps", b)],
                         [("dv", g, bi0)])
            if debug == "dil_c":
                P.barrier(final=True)
                return nc, P
            def dil_geom(g, bi):
                r = DIL_R[g]
                nbk = 16 // r
                c_, n_ = bi // nbk, bi % nbk
                st_ = c_ + r * 128 * n_
                tsl = slice(st_, st_ + 127 * r + 1, r)
                psl = slice(st_ - 128 * r, st_ - 128 * r + 127 * r + 1, r)
                return n_, tsl, psl

            def dil_stage_a(g, bi, un):
                n_, tsl, psl = dil_geom(g, bi)
                sl = un % 3
                sb_ = 2 * (un % 3)
                wp = 256 if n_ > 0 else 128
                rk = [("dqk", 0, g, tt_) for tt_ in range(4)] + [("dqk", 1, g, tt_) for tt_ in range(4)]
                for e_ in range(2):
                    for part in range(2 if n_ > 0 else 1):
                        ksl = tsl if part == 0 else psl
                        P.op("pe", lambda e: e.matmul(psf(sb_ + e_)[:, part * 128:(part + 1) * 128],
                                                      lhsT=dk[e_ * 64:(e_ + 1) * 64, g, ksl],
                                                      rhs=dq[e_ * 64:(e_ + 1) * 64, g, tsl], start=True, stop=True),
                             reads=rk, writes=[("ps", sb_ + e_)])
                P.op("act", lambda e: e.activation(out=dE[:, sl, :].rearrange("p (a q) -> p a q", a=2)[:, :, 0:wp],
                                                   in_=psall[:, sb_:sb_ + 2, 0:wp], func=AF.Exp, scale=0.125),
                     reads=[("ps", sb_), ("ps", sb_ + 1)], writes=[("dE", sl)])
                P.op("dve", lambda e: e.tensor_tensor(
                    out=dPT[:, sl, :].rearrange("p (a q) -> p a q", a=2)[:, :, 0:wp],
                    in0=dE[:, sl, :].rearrange("p (a q) -> p a q", a=2)[:, :, 0:wp],
                    in1=dtab[:, g * 2 + j, :].rearrange("p (a q) -> p a q", a=2)[:, :, 0:wp], op=ALU.mult),
                     reads=[("dE", sl)] + DTAB(g, j), writes=[("dPT", sl)])

            def dil_stage_b(g, bi, un):
                n_, tsl, psl = dil_geom(g, bi)
                sl = un % 3
                ob_ = 6 + un % 2
                for kind in range(2):
                    for e_ in range(2):
                        for part in range(2 if n_ > 0 else 1):
                            col = (e_ * 2 + part) * 128
                            if kind == 0:
                                lt = dv[:, g, bi - part, e_ * 64:(e_ + 1) * 64]
                            else:
                                lt = ones_bf[:, 0:64]
                            P.op("pe", lambda e: e.matmul(psf(ob_)[e_ * 64:(e_ + 1) * 64, kind * 128:(kind + 1) * 128],
                                                          lhsT=lt, rhs=dPT[:, sl, col:col + 128],
                                                          start=(part == 0), stop=(part == (1 if n_ > 0 else 0))),
                                 reads=[("dPT", sl), ("dv", g, (bi // 4) * 4), ("dv", g, ((bi - part) // 4) * 4), "consts"],
                                 writes=[("ps", ob_)])
                nd_out = ND[:, :, tsl]
                nd_in = psf(ob_)[:, 0:256].rearrange("p (a q) -> p a q", a=2)
                if g == 0:
                    P.op("dve", lambda e: e.tensor_copy(out=nd_out, in_=nd_in), reads=[("ps", ob_)], writes=["ND"])
                else:
                    P.op("dve", lambda e: e.tensor_tensor(out=nd_out, in0=nd_out, in1=nd_in, op=ALU.add),
                         reads=[("ps", ob_), "ND"], writes=["ND"])

            dunits = [(g, bi) for g in range(3) for bi in range(16)]
            DLOOK = 2
            for u in range(len(dunits) + DLOOK):
                if u < len(dunits):
                    dil_stage_a(dunits[u][0], dunits[u][1], u)
                if u >= DLOOK:
                    dil_stage_b(dunits[u - DLOOK][0], dunits[u - DLOOK][1], u - DLOOK)
            P.op("dve", lambda e: e.reciprocal(out=ND[:, 1, :], in_=ND[:, 1, :]), reads=["ND"], writes=["ND"])
            P.op("dve", lambda e: e.tensor_tensor(out=odilT[:, j, :], in0=ND[:, 0, :], in1=ND[:, 1, :], op=ALU.mult),
                 reads=["ND"], writes=[("odilT", j)])
        P.barrier()

    if debug == "dil":
        d1 = dbg_out("odilT", [128, 2 * S])
        P.dma("pool", lambda e: e.dma_start(out=d1, in_=odilT[:].rearrange("p c t -> p (c t)"), max_dma_last_dim=4096), "dbg1")
        P.barrier(final=True)
        return nc, P

    pre = ExitStack()
    wgate = pre.enter_context(nc.sbuf_tensor("wgate", [128, 8, 3072], BF16))
    wg_r = w_gate[0].rearrange("(c p) n -> p c n", p=128)
    for br in range(3):
        P.dma("pool", lambda e: e.dma_start(out=wgate[:, :, br * 1024:(br + 1) * 1024],
                                             in_=wg_r[:, :, br * 1024:(br + 1) * 1024]), ("wgate", br), writes=[("wgate", br)])

    with nc.sbuf_tensor("wkv", [128, 8, 1024], BF16) as wkv, \
         nc.sbuf_tensor("wmq", [128, 8, 512], BF16) as wmq, \
         nc.sbuf_tensor("memhT", [128, 8, MEM], BF16) as memhT, \
         nc.sbuf_tensor("kmT", [128, 4, MEM], BF16) as kmT, \
         nc.sbuf_tensor("vm", [128, 2, 512], BF16) as vm, \
         nc.sbuf_tensor("qmT", [128, 4, S], BF16) as qmT, \
         nc.sbuf_tensor("msq", [128, 512], BF16) as msq, \
         nc.sbuf_tensor("mt1", [128, 512], F32) as mt1, \
         nc.sbuf_tensor("mPT", [128, 2, 2, 512], BF16) as mPT, \
         nc.sbuf_tensor("mrd", [128, 2, 512], F32) as mrd:
        P.dma("pool", lambda e: e.dma_start(out=wkv[:], in_=w_mem_kv[0].rearrange("(c p) n -> p c n", p=128)),
              "wkv", writes=["wkv"])
        P.dma("pool", lambda e: e.dma_start(out=wmq[:], in_=win_r[:, :, 3840:4352]), "wmq", writes=["wmq"])
        rms_to_T(mem, 2, gmemT, memhT, "m")
        for hd in range(4):
            b = hd % 2
            for c in range(8):
                P.op("pe", lambda e: e.matmul(psf(b)[:, 0:MEM], lhsT=wkv[:, c, hd * 128:(hd + 1) * 128],
                                              rhs=memhT[:, c, :], start=(c == 0), stop=(c == 7)),
                     reads=["wkv"] if c in (0, 7) else [], writes=[("ps", b)])
            head_norm(b, 3, MEM, ones_bf[:], 1.0 / 128, gmemh[:, 1:2], kmT[:, hd, :], msq, mt1, "mem", ("kmT", hd))
        for blk in range(2):
            b = blk % 2
            for c in range(8):
                P.op("pe", lambda e: e.matmul(psf(b), lhsT=memhT[:, c, blk * 128:(blk + 1) * 128],
                                              rhs=wkv[:, c, 512:1024], start=(c == 0), stop=(c == 7)),
                     reads=["wkv"] if c in (0, 7) else [], writes=[("ps", b)])
            evac(vm[:, blk, :], psf(b), [("ps", b)], [("vm", blk)])
        nb = 0
        for hd in range(4):
            for tg in range(4):
                b = nb % 3
                nb += 1
                for c in range(8):
                    P.op("pe", lambda e: e.matmul(psf(b), lhsT=wmq[:, c, hd * 128:(hd + 1) * 128],
                                                  rhs=hT[:, c, tg * 512:(tg + 1) * 512], start=(c == 0), stop=(c == 7)),
                         reads=["wmq"] if c in (0, 7) else [], writes=[("ps", b)])
                head_norm(b, 3, 512, ones_bf[:], 1.0 / 128, gmemh[:, 0:1], qmT[:, hd, tg * 512:(tg + 1) * 512],
                          msq, mt1, "mem", ("qmT", hd, tg))
        un = 0
        for hd in range(4):
            for tg in range(4):
                sl = un % 2
                un += 1
                for blk in range(2):
                    sb_ = 4 + blk
                    P.op("pe", lambda e: e.matmul(psf(sb_), lhsT=kmT[:, hd, blk * 128:(blk + 1) * 128],
                                                  rhs=qmT[:, hd, tg * 512:(tg + 1) * 512], start=True, stop=True),
                         reads=[("kmT", hd), ("qmT", hd, tg)], writes=[("ps", sb_)])
                    P.op("act", lambda e: e.activation(out=mPT[:, sl, blk, :], in_=psf(sb_), func=AF.Exp,
                                                       scale=128.0 ** -0.5),
                         reads=[("ps", sb_)], writes=[("mPT", sl, blk)])
                for blk in range(2):
                    P.op("pe", lambda e: e.matmul(psf(6), lhsT=vm[:, blk, hd * 128:(hd + 1) * 128],
                                                  rhs=mPT[:, sl, blk, :], start=(blk == 0), stop=(blk == 1)),
                         reads=[("mPT", sl, blk), ("vm", blk)], writes=[("ps", 6)])
                for blk in range(2):
                    P.op("pe", lambda e: e.matmul(psf(7), lhsT=ones_bf[:], rhs=mPT[:, sl, blk, :],
                                                  start=(blk == 0), stop=(blk == 1)),
                         reads=[("mPT", sl, blk), "consts"], writes=[("ps", 7)])
                P.op("dve", lambda e: e.reciprocal(out=mrd[:, sl, :], in_=psf(7)), reads=[("ps", 7)], writes=[("mrd", sl)])
                P.op("dve", lambda e: e.tensor_tensor(out=omemT[:, hd, tg * 512:(tg + 1) * 512], in0=psf(6),
                                                      in1=mrd[:, sl, :], op=ALU.mult),
                     reads=[("ps", 6), ("mrd", sl)], writes=[("omemT", hd, tg)])
        P.barrier()

    if debug == "mem":
        d1 = dbg_out("omemT", [128, 4 * S])
        P.dma("pool", lambda e: e.dma_start(out=d1, in_=omemT[:].rearrange("p c t -> p (c t)"), max_dma_last_dim=4096), "dbg1")
        P.barrier(final=True)
        return nc, P

    if debug == "x1":
        x1d = dbg_out("x1", [S, D])
    else:
        x1d = nc.dram_tensor("x1d", [S, D], F32).ap()
    with nc.sbuf_tensor("wosb", [128, 4, D], BF16) as wosb, \
         nc.sbuf_tensor("wodil", [128, 2, D], BF16) as wodil, \
         nc.sbuf_tensor("womem", [128, 4, D], BF16) as womem, \
         nc.sbuf_tensor("wout", [128, 8, D], BF16) as wout, \
         nc.sbuf_tensor("bg", [128, 24], F32) as bg, \
         nc.sbuf_tensor("gsb", [128, 2, 3, 512], F32) as gsb, \
         nc.sbuf_tensor("macc", [128, 2, 512], F32) as macc, \
         nc.sbuf_tensor("mtmp", [128, 2, 512], F32) as mtmp, \
         nc.sbuf_tensor("mrgT", [128, 8, 512], BF16) as mrgT, \
         nc.sbuf_tensor("xr", [128, 2, D], F32) as xr, \
         nc.sbuf_tensor("x1s", [128, 2, D], F32) as x1s:
        P.dma("pool", lambda e: e.dma_start(out=wosb[:], in_=w_o_sb[0].rearrange("(c p) n -> p c n", p=128)), "wosb", writes=["wosb"])
        P.dma("pool", lambda e: e.dma_start(out=wodil[:], in_=w_o_dil[0].rearrange("(c p) n -> p c n", p=128)), "wodil", writes=["wodil"])
        P.dma("pool", lambda e: e.dma_start(out=womem[:], in_=w_o_mem[0].rearrange("(c p) n -> p c n", p=128)), "womem", writes=["womem"])
        P.dma("pool", lambda e: e.dma_start(out=wout[:], in_=w_out[0].rearrange("(c p) n -> p c n", p=128)), "wout", writes=["wout"])
        with nc.allow_non_contiguous_dma(reason="tiny bias vector"):
            P.dma("sp", lambda e: e.dma_start(out=bg[:], in_=b_gate.rearrange("o (c p) -> p (o c)", p=128)), "bg", writes=["bg"])
        branches = [(wosb, osbT, 4, "wosb"), (wodil, odilT, 2, "wodil"), (womem, omemT, 4, "womem")]
        un = 0
        xt = 0
        for tg in range(4):
            tsl = slice(tg * 512, (tg + 1) * 512)
            for dc in range(8):
                sl = un % 2
                un += 1
                for br in range(3):
                    gb = br
                    for c in range(8):
                        P.op("pe", lambda e: e.matmul(psf(gb), lhsT=wgate[:, c, br * 1024 + dc * 128:br * 1024 + (dc + 1) * 128],
                                                      rhs=hT[:, c, tsl], start=(c == 0), stop=(c == 7)),
                             reads=[("wgate", br)] if c in (0, 7) else [], writes=[("ps", gb)])
                    P.op("act", lambda e: e.activation(out=gsb[:, sl, br, :], in_=psf(gb), func=AF.Sigmoid,
                                                       bias=bg[:, br * 8 + dc:br * 8 + dc + 1], scale=1.0),
                         reads=[("ps", gb), "bg"], writes=[("gsb", sl, br)])
                for br in range(3):
                    wt, oT, nkc, wkey = branches[br]
                    yb = 3 + br
                    for kc in range(nkc):
                        P.op("pe", lambda e: e.matmul(psf(yb), lhsT=wt[:, kc, dc * 128:(dc + 1) * 128], rhs=oT[:, kc, tsl],
                                                      start=(kc == 0), stop=(kc == nkc - 1)),
                             reads=[wkey] if kc in (0, nkc - 1) else [], writes=[("ps", yb)])
                P.op("dve", lambda e: e.tensor_tensor(out=macc[:, sl, :], in0=psf(3), in1=gsb[:, sl, 0, :], op=ALU.mult),
                     reads=[("ps", 3), ("gsb", sl, 0)], writes=[("macc", sl)])
                P.op("dve", lambda e: e.tensor_tensor(out=mtmp[:, sl, :], in0=psf(4), in1=gsb[:, sl, 1, :], op=ALU.mult),
                     reads=[("ps", 4), ("gsb", sl, 1)], writes=[("mtmp", sl)])
                P.op("pool", lambda e: e.tensor_tensor(out=macc[:, sl, :], in0=macc[:, sl, :], in1=mtmp[:, sl, :], op=ALU.add),
                     reads=[("macc", sl), ("mtmp", sl)], writes=[("macc", sl)])
                P.op("dve", lambda e: e.tensor_tensor(out=mtmp[:, sl, :], in0=psf(5), in1=gsb[:, sl, 2, :], op=ALU.mult),
                     reads=[("ps", 5), ("gsb", sl, 2)], writes=[("mtmp", sl)])
                P.op("pool", lambda e: e.tensor_tensor(out=mrgT[:, dc, :], in0=macc[:, sl, :], in1=mtmp[:, sl, :], op=ALU.add),
                     reads=[("macc", sl), ("mtmp", sl)], writes=[("mrgT", dc)])
            for tl in range(4):
                tt = tg * 4 + tl
                xs_ = xt % 2
                xt += 1
                P.dma("sp", lambda e: e.dma_start(out=xr[:, xs_, :], in_=x[tt * 128:(tt + 1) * 128, :]), ("xr", xs_),
                      writes=[("xr", xs_)])
                for half in range(2):
                    ob = 6 + half
                    for dc in range(8):
                        P.op("pe", lambda e: e.matmul(psf(ob), lhsT=mrgT[:, dc, tl * 128:(tl + 1) * 128],
                                                      rhs=wout[:, dc, half * 512:(half + 1) * 512],
                                                      start=(dc == 0), stop=(dc == 7)),
                             reads=["wout"] + [("mrgT", d_) for d_ in range(8)] if dc in (0, 7) else [], writes=[("ps", ob)])
                    P.op("dve", lambda e: e.tensor_tensor(out=x1s[:, xs_, half * 512:(half + 1) * 512], in0=psf(ob),
                                                          in1=xr[:, xs_, half * 512:(half + 1) * 512], op=ALU.add),
                         reads=[("ps", ob), ("xr", xs_)], writes=[("x1s", xs_, half)])
                P.dma("sp", lambda e: e.dma_start(out=x1d[tt * 128:(tt + 1) * 128, :], in_=x1s[:, xs_, :]), ("x1o", xs_),
                      reads=[("x1s", xs_, 0), ("x1s", xs_, 1)], writes=[("x1d", tt)])
        P.barrier()

    if debug == "x1":
        P.barrier(final=True)
        return nc, P
    pre.close()
    es.close()

    cvt_step(16)
    NR = 20
    GS = 4
    FUSE_EVERY = 4
    pcnt = [0]
    jcnt = [0]
    with ExitStack() as pes:
        def sb(name, shape, dt):
            return pes.enter_context(nc.sbuf_tensor(name, shape, dt))
        wpq = sb("wpq", [128, 8, 2048], BF16)
        subkT = sb("subkT", [128, 16, 128], BF16)
        gffn_b = sb("gffn_b", [128, D], F32)
        iota16 = sb("iota16", [128, 16], F32)
        x1t = sb("x1t", [128, 2, D], F32)
        outt = sb("outt", [128, D], F32)
        h2 = sb("h2", [128, 2, D], F32)
        h2b = sb("h2b", [128, 2, D], BF16)
        prodr = sb("prodr", [128, 4, D], BF16)
        junkB = sb("junkB", [128, 2, D], mybir.dt.float8e4)
        h2T = sb("h2T", [128, 8, 128], BF16)
        pst = sb("pst", [128, 4], F32)
        qT = sb("qT", [128, 16, 128], BF16)
        sc = sb("sc", [128, 16, 128], F32)
        scw = sb("scw", [128, 16, 128], F32)
        top = sb("top", [128, 16, 16], F32)
        idx = sb("idx", [128, 16, 16], U32)
        idxf = sb("idxf", [128, 16, 16], F32)
        cand = sb("cand", [128, 8, 256], F32)
        cwk = scw[:].rearrange("p q k -> p (q k)").rearrange("p (h a) -> p h a", h=8)
        ctop = sb("ctop", [128, 8, 16], F32)
        cpos = sb("cpos", [128, 8, 16], U32)
        abu = sb("abu", [128, 2, 128], U32)
        abf = sb("abf", [128, 2, 128], F32)
        oh = sc[:].rearrange("p q k -> p (q k)").rearrange("p (h a) -> p h a", h=8)
        isel = sb("isel", [128, 2, 128], F32)
        eidxf = sb("eidxf", [128, 128], F32)
        eidx = sb("eidx", [128, 2, 128], U32)
        gate = sb("gate", [128, 2, 128], F32)
        gsum = sb("gsum", [128, 2, 8], F32)
        apre = sb("apre", [128, 2, 128], F32)
        wgt = sb("wgt", [128, 2, 128], F32)
        junkA = sb("junkA", [128, D], BF16)
        junkD = sb("junkD", [128, 1, D], BF16)
        diag = sb("diag", [128, 16, 128], BF16)

        P.dma("pool", lambda e: e.dma_start(out=wpq[:], in_=w_peer_q[0].rearrange("(c p) n -> p c n", p=128)),
              "wpq", writes=["wpq"])
        P.dma("sp", lambda e: e.dma_start(out=gffn_b[:], in_=g_ffn.broadcast_to([128, D])), "gffn", writes=["gffn"])
        P.op("pool", lambda e: e.iota(iot[:, 0:16], pattern=[[1, 16]], base=0, channel_multiplier=0), writes=["iot"])
        P.op("dve", lambda e: e.tensor_copy(out=iota16[:], in_=iot[:, 0:16]), reads=["iot"], writes=["iota16"])
        with nc.sbuf_tensor("subk", [128, 16, 128], BF16) as subk:
            P.dma("pool", lambda e: e.dma_start(out=subk[:], in_=peer_subkeys[0].rearrange("h t k d -> k (h t) d")),
                  "subk", writes=["subk"])
            for qg in range(4):
                b = qg % 2
                for jj in range(4):
                    qc = qg * 4 + jj
                    P.op("pe", lambda e: e.transpose(out=psh(b)[:, jj * 128:(jj + 1) * 128], in_=subk[:, qc, :], identity=ident[:]),
                         reads=["subk", "ident"], writes=[("ps", b)])
                evac(subkT[:, qg * 4:(qg + 1) * 4, :], psh(b)[:, 0:512].rearrange("p (a k) -> p a k", a=4), [("ps", b)], ["subkT"])
            P.barrier()
        UV = sb("UV", [128, NR, 2048], BF16)

        def topk_ops(tt):
            p_ = tt % 2
            K = lambda n: (n, p_)
            P.dma("sp", lambda e: e.dma_start(out=x1t[:, p_, :], in_=x1d[tt * 128:(tt + 1) * 128, :]), ("x1t", p_), writes=[K("x1t")])
            P.op("act", lambda e: e.activation(out=junkA[:], in_=x1t[:, p_, :], func=AF.Square, accum_out=pst[:, 0:1]),
                 reads=[K("x1t")], writes=["junkA", "pst"])
            P.op("dve", lambda e: e.tensor_scalar(out=pst[:, 1:2], in0=pst[:, 0:1], scalar1=1.0 / D, scalar2=EPS,
                                                  op0=ALU.mult, op1=ALU.add), reads=["pst"], writes=["pst"])
            P.op("act", lambda e: e.activation(out=pst[:, 2:3], in_=pst[:, 1:2], func=AF.Sqrt), reads=["pst"], writes=["pst"])
            P.op("dve", lambda e: e.reciprocal(out=pst[:, 3:4], in_=pst[:, 2:3]), reads=["pst"], writes=["pst"])
            P.op("dve", lambda e: e.scalar_tensor_tensor(out=h2[:, p_, :], in0=x1t[:, p_, :], scalar=pst[:, 3:4], in1=gffn_b[:],
                                                         op0=ALU.mult, op1=ALU.mult),
                 reads=[K("x1t"), "pst", "gffn"], writes=[K("h2")])
            P.op("act", lambda e: e.activation(out=h2b[:, p_, :], in_=h2[:, p_, :], func=AF.Copy), reads=[K("h2")], writes=[K("h2b")])
            for c in range(8):
                P.op("pe", lambda e: e.transpose(out=psh(0)[:, c * 128:(c + 1) * 128], in_=h2b[:, p_, c * 128:(c + 1) * 128],
                                                 identity=ident[:]), reads=[K("h2b"), "ident"], writes=[("ps", 0)])
            evac(h2T[:].rearrange("p c t -> p (c t)"), psh(0), [("ps", 0)], ["h2T"], eng="act")
            yield
            for qg in range(4):
                b = 1 + qg % 2
                for jj in range(4):
                    qc = qg * 4 + jj
                    for c in range(8):
                        P.op("pe", lambda e: e.matmul(psf(b)[:, jj * 128:(jj + 1) * 128], lhsT=wpq[:, c, qc * 128:(qc + 1) * 128],
                                                      rhs=h2T[:, c, :], start=(c == 0), stop=(c == 7)),
                             reads=["wpq", "h2T"] if c in (0, 7) else [], writes=[("ps", b)])
                evac(qT[:, qg * 4:(qg + 1) * 4, :], psf(b).rearrange("p (a k) -> p a k", a=4), [("ps", b)], [("qT", qg)],
                     eng="act")
            for qg in range(4):
                b = 3
                for jj in range(4):
                    qc = qg * 4 + jj
                    P.op("pe", lambda e: e.matmul(psf(b)[:, jj * 128:(jj + 1) * 128], lhsT=qT[:, qc, :], rhs=subkT[:, qc, :],
                                                  start=True, stop=True),
                         reads=[("qT", qg), "subkT"], writes=[("ps", b)])
                evac(sc[:, qg * 4:(qg + 1) * 4, :], psf(b).rearrange("p (a k) -> p a k", a=4), [("ps", b)], [("sc", qg)],
                     eng="act")
            yield
            for qc in range(16):
                k_sc = ("sc", qc // 4)
                P.op("dve", lambda e: e.max(out=top[:, qc, 0:8], in_=sc[:, qc, :]), reads=[k_sc], writes=[("top", qc)])
                P.op("dve", lambda e: e.max_index(out=idx[:, qc, 0:8], in_max=top[:, qc, 0:8], in_values=sc[:, qc, :]),
                     reads=[k_sc, ("top", qc)], writes=[("idx", qc)])
                P.op("dve", lambda e: e.match_replace(out=scw[:, qc, :], in_to_replace=top[:, qc, 0:8], in_values=sc[:, qc, :],
                                                      imm_value=-1e30), reads=[k_sc, ("top", qc)], writes=[("scw", qc)])
                P.op("dve", lambda e: e.max(out=top[:, qc, 8:16], in_=scw[:, qc, :]), reads=[("scw", qc)], writes=[("top", qc)])
                P.op("dve", lambda e: e.max_index(out=idx[:, qc, 8:16], in_max=top[:, qc, 8:16], in_values=scw[:, qc, :]),
                     reads=[("scw", qc), ("top", qc)], writes=[("idx", qc)])
                if qc % 2 == 1:
                    yield
            TOPK = [("top", qc) for qc in range(16)]
            IDXK = [("idx", qc) for qc in range(16)]
            top_v = top[:].rearrange("p (h t) k -> p h t k", t=2)
            P.op("dve", lambda e: e.tensor_tensor(
                out=cand[:].rearrange("p h (a b) -> p h a b", a=16),
                in0=top_v[:, :, 0, :].unsqueeze(3).broadcast_to([128, 8, 16, 16]),
                in1=top_v[:, :, 1, :].unsqueeze(2).broadcast_to([128, 8, 16, 16]), op=ALU.add),
                reads=TOPK, writes=["cand"])
            for hd in range(8):
                P.op("dve", lambda e: e.max(out=ctop[:, hd, 0:8], in_=cand[:, hd, :]), reads=["cand"], writes=[("ctop", hd)])
                P.op("dve", lambda e: e.max_index(out=cpos[:, hd, 0:8], in_max=ctop[:, hd, 0:8], in_values=cand[:, hd, :]),
                     reads=["cand", ("ctop", hd)], writes=[("cpos", hd)])
                P.op("dve", lambda e: e.match_replace(out=cwk[:, hd, :], in_to_replace=ctop[:, hd, 0:8], in_values=cand[:, hd, :],
                                                      imm_value=-1e30), reads=["cand", ("ctop", hd)], writes=[("scw", 2 * hd), ("scw", 2 * hd + 1)])
                P.op("dve", lambda e: e.max(out=ctop[:, hd, 8:16], in_=cwk[:, hd, :]), reads=[("scw", 2 * hd), ("scw", 2 * hd + 1)], writes=[("ctop", hd)])
                P.op("dve", lambda e: e.max_index(out=cpos[:, hd, 8:16], in_max=ctop[:, hd, 8:16], in_values=cwk[:, hd, :]),
                     reads=[("scw", 2 * hd), ("scw", 2 * hd + 1), ("ctop", hd)], writes=[("cpos", hd)])
                if hd % 2 == 1:
                    yield
            CTOP = [("ctop", hd) for hd in range(8)]
            CPOS = [("cpos", hd) for hd in range(8)]
            gate_v = gate[:, p_, :].rearrange("p (h k) -> p h k", h=8)
            P.op("dve", lambda e: e.tensor_tensor(out=gate_v, in0=ctop[:], in1=ctop[:, :, 0:1].broadcast_to([128, 8, 16]),
                                                  op=ALU.subtract), reads=CTOP, writes=[K("gate")])
            P.op("act", lambda e: e.activation(out=gate[:, p_, :], in_=gate[:, p_, :], func=AF.Exp), reads=[K("gate")], writes=[K("gate")])
            P.op("dve", lambda e: e.tensor_reduce(out=gsum[:, 0, :], in_=gate_v, axis=AX.X, op=ALU.add),
                 reads=[K("gate")], writes=["gsum"])
            P.op("dve", lambda e: e.reciprocal(out=gsum[:, 1, :], in_=gsum[:, 0, :]), reads=["gsum"], writes=["gsum"])
            P.op("dve", lambda e: e.tensor_tensor(out=gate_v, in0=gate_v,
                                                  in1=gsum[:, 1, :].unsqueeze(2).broadcast_to([128, 8, 16]), op=ALU.mult),
                 reads=[K("gate"), "gsum"], writes=[K("gate")])
            yield
            cpf = cpos[:].rearrange("p h k -> p (h k)")
            P.op("dve", lambda e: e.tensor_single_scalar(out=abu[:, 0, :], in_=cpf, scalar=4, op=ALU.logical_shift_right),
                 reads=CPOS, writes=["abu0"])
            P.op("dve", lambda e: e.tensor_single_scalar(out=abu[:, 1, :], in_=cpf, scalar=15, op=ALU.bitwise_and),
                 reads=CPOS, writes=["abu1"])
            P.op("dve", lambda e: e.tensor_copy(out=abf[:], in_=abu[:]), reads=["abu0", "abu1"], writes=["abf"])
            P.op("dve", lambda e: e.tensor_copy(out=idxf[:], in_=idx[:]), reads=IDXK, writes=["idxf"])
            idxf_v = idxf[:].rearrange("p (h t) k -> p h t k", t=2)
            oh_v = oh[:].rearrange("p h (k a) -> p h k a", k=16)
            for t_ in range(2):
                P.op("dve", lambda e: e.tensor_tensor(
                    out=oh_v, in0=abf[:, t_, :].rearrange("p (h k) -> p h k", h=8).unsqueeze(3).broadcast_to([128, 8, 16, 16]),
                    in1=iota16[:, :].unsqueeze(1).unsqueeze(1).broadcast_to([128, 8, 16, 16]), op=ALU.is_equal),
                    reads=["abf", "iota16"], writes=[("sc", q_) for q_ in range(4)])
                P.op("dve", lambda e: e.tensor_tensor(
                    out=oh_v, in0=oh_v, in1=idxf_v[:, :, t_, :].unsqueeze(2).broadcast_to([128, 8, 16, 16]), op=ALU.mult),
                    reads=[("sc", q_) for q_ in range(4)] + ["idxf"], writes=[("sc", q_) for q_ in range(4)])
                P.op("dve", lambda e: e.tensor_reduce(out=isel[:, t_, :].rearrange("p (h k) -> p h k", h=8), in_=oh_v,
                                                      axis=AX.X, op=ALU.add), reads=[("sc", q_) for q_ in range(4)], writes=[("isel", t_)])
                yield
            P.op("dve", lambda e: e.scalar_tensor_tensor(out=eidxf[:], in0=isel[:, 0, :], scalar=128.0, in1=isel[:, 1, :],
                                                         op0=ALU.mult, op1=ALU.add),
                 reads=[("isel", 0), ("isel", 1)], writes=["eidxf"])
            P.op("dve", lambda e: e.tensor_copy(out=eidx[:, p_, :], in_=eidxf[:]), reads=["eidxf"], writes=[K("eidx")])

        cvt_h = cvt_state["h"]
        for _ in topk_ops(0):
            pass
        if debug == "peer_idx":
            d1 = dbg_out("eidx", [128, 128])
            d2 = dbg_out("gate", [128, 128])
            P.dma("sp", lambda e: e.dma_start(out=d1, in_=eidxf[:]), "dbg1", reads=["eidxf"])
            P.dma("sp", lambda e: e.dma_start(out=d2, in_=gate[:, 0, :]), "dbg2", reads=[("gate", 0)])
            P.barrier(final=True)
            return nc, P
        NG = 128 // GS
        TOT = NT * NG
        DLY = 1
        ginfo = {}
        gen = {"g": None}

        def peer_gather(G):
            tt, grp = G // NG, G % NG
            p_ = tt % 2
            slots = []
            for k_ in range(GS):
                s_ = grp * GS + k_
                r = (G * GS + k_) % NR
                slots.append((s_, r))
                P.dma("pool", lambda e: e.indirect_dma_start(
                    out=UV[:, r, :], out_offset=None, in_=uvb,
                    in_offset=bass.IndirectOffsetOnAxis(ap=eidx[:, p_, s_:s_ + 1], axis=0)),
                    ("UV", r), reads=[("eidx", p_)], writes=[("UV", r)], extra=[cvt_h])
            ginfo[G] = [slots, None, None]

        def peer_dots(G, k0, k1):
            tt, grp = G // NG, G % NG
            p_ = tt % 2
            info = ginfo[G]
            for (s_, r) in info[0][k0:k1]:
                if FUSE_EVERY and s_ % FUSE_EVERY == 0:
                    jd = 0
                    jcnt[0] += 1
                    info[1] = P.op("dve", lambda e: e.scalar_tensor_tensor(out=junkD[:, jd, :], in0=UV[:, r, 0:1024], scalar=1.0,
                                                                           in1=h2[:, p_, :], op0=ALU.mult, op1=ALU.mult,
                                                                           accum_out=apre[:, p_, s_:s_ + 1]),
                                   reads=[("UV", r), ("h2", p_)], writes=[("junkD", jd), ("apre", p_, s_)])
                else:
                    pr = pcnt[0] % 4
                    jb = pcnt[0] % 2
                    pcnt[0] += 1
                    P.op("dve", lambda e: e.tensor_tensor(out=prodr[:, pr, :], in0=UV[:, r, 0:1024], in1=h2b[:, p_, :],
                                                          op=ALU.mult),
                         reads=[("UV", r), ("h2b", p_)], writes=[("prodr", pr)])
                    info[2] = P.op("act", lambda e: e.activation(out=junkB[:, jb, :], in_=prodr[:, pr, :], func=AF.Copy,
                                                                 accum_out=apre[:, p_, s_:s_ + 1], saturate=False),
                                   reads=[("prodr", pr)], writes=[("junkB", jb), ("apre", p_, s_)])

        def peer_gelu(G):
            tt, grp = G // NG, G % NG
            p_ = tt % 2
            gsl = slice(grp * GS, (grp + 1) * GS)
            P.op("act", lambda e: e.activation(out=wgt[:, p_, gsl], in_=apre[:, p_, gsl], func=AF.Gelu),
                 extra=[ginfo[G][1], ginfo[G][2]], reads=[("apre", p_, s_) for s_ in range(grp * GS, (grp + 1) * GS)],
                 writes=[("wgt", p_, grp)])

        def peer_back(G):
            tt, grp = G // NG, G % NG
            p_ = tt % 2
            ab = 4 + 2 * p_
            slots = ginfo.pop(G)[0]
            gsl = slice(grp * GS, (grp + 1) * GS)
            P.op("dve", lambda e: e.tensor_tensor(out=wgt[:, p_, gsl], in0=wgt[:, p_, gsl], in1=gate[:, p_, gsl], op=ALU.mult),
                 reads=[("wgt", p_, grp), ("gate", p_)], writes=[("wgt", p_, grp)])
            for (s_, r) in slots:
                dsl = s_ % 16
                P.op("act", lambda e: e.activation(out=diag[:, dsl, :], in_=ident[:], func=AF.Copy,
                                                   scale=wgt[:, p_, s_:s_ + 1]),
                     reads=[("wgt", p_, grp), "ident"], writes=[("diag", dsl)])
                for half in range(2):
                    P.op("pe", lambda e: e.matmul(psf(ab + half), lhsT=diag[:, dsl, :],
                                                  rhs=UV[:, r, 1024 + half * 512:1024 + (half + 1) * 512],
                                                  start=(s_ == 0), stop=(s_ == 127)),
                         reads=[("diag", dsl), ("UV", r)], writes=[("ps", ab + half)])
            if grp == NG - 1:
                for half in range(2):
                    P.op("dve", lambda e: e.tensor_tensor(out=outt[:, half * 512:(half + 1) * 512], in0=psf(ab + half),
                                                          in1=x1t[:, p_, half * 512:(half + 1) * 512], op=ALU.add),
                         reads=[("ps", ab + half), ("x1t", p_)], writes=[("outt", half)])
                P.dma("sp", lambda e: e.dma_start(out=out[tt * 128:(tt + 1) * 128, :], in_=outt[:]), "outd",
                      reads=[("outt", 0), ("outt", 1)], writes=[("out", tt)])

        HALF = GS // 2
        if debug == "peer_gatheronly":
            peer_dots = lambda *a: None
            peer_gelu = lambda *a: None
            peer_back = lambda G: ginfo.pop(G)
        for G in range(TOT + 1):
            tt, grp = G // NG, G % NG
            if G < TOT:
                if grp == 1 and tt + 1 < NT:
                    gen["g"] = topk_ops(tt + 1)
                peer_gather(G)
                peer_dots(G, 0, HALF)
            if G >= 1:
                peer_back(G - 1)
            if G < TOT:
                peer_dots(G, HALF, GS)
                peer_gelu(G)
                if gen["g"] is not None and grp >= 1:
                    if grp == NG - 1:
                        for _ in gen["g"]:
                            pass
                        gen["g"] = None
                    else:
                        next(gen["g"], None)
        P.barrier(final=True)

    P.barrier(final=True)
    return nc, P


_CACHE = {}


def kernel(**inputs):
    if "nc" not in _CACHE:
        _CACHE["nc"] = build_program()[0]
    nc = _CACHE["nc"]
    n = 8
    in_maps = []
    for b in range(n):
        m = {}
        for k, v in inputs.items():
            a = np.asarray(v)
            if k in ("x", "mem"):
                a = a[b]
            m[k] = np.ascontiguousarray(a, dtype=np.float32)
        in_maps.append(m)
    res = run_bass_kernel_spmd(nc, in_maps, core_ids=list(range(n)))
    return np.stack([np.asarray(r["out"], dtype=np.float32) for r in res.results], axis=0)
```

```python
import numpy as np
from contextlib import ExitStack
import concourse.bass as bass
import concourse.mybir as mybir
from concourse.alu_op_type import AluOpType as ALU
from concourse.bass_utils import run_bass_kernel_spmd

AF = mybir.ActivationFunctionType
AX = mybir.AxisListType
F32 = mybir.dt.float32
BF16 = mybir.dt.bfloat16
I32 = mybir.dt.int32
U32 = mybir.dt.uint32

S = 2048
D = 1024
NT = 16
EPS = 1e-6
MEM = 256


class Prog:
    ENG = ("pe", "act", "dve", "pool", "sp")

    def __init__(self, nc):
        self.nc = nc
        self.eng = {"pe": nc.tensor, "act": nc.scalar, "dve": nc.vector,
                    "pool": nc.gpsimd, "sp": nc.sync}
        self.sem = {e: nc.alloc_semaphore("s_" + e) for e in self.ENG}
        self.cnt = {e: 0 for e in self.ENG}
        self.seen = {e: {} for e in self.ENG}
        self.last_w = {}
        self.readers = {}
        self.dsem = {}
        self.dcnt = {}
        self.ninst = 0
        self.background = set()

    def _semobj(self, sk):
        return self.sem[sk] if sk in self.sem else self.dsem[sk]

    def _wait(self, eng, deps):
        need = {}
        for d in deps:
            if d is None:
                continue
            sk, v = d
            if need.get(sk, 0) < v:
                need[sk] = v
        seen = self.seen[eng]
        for sk, v in need.items():
            if seen.get(sk, 0) < v:
                self.eng[eng].wait_ge(self._semobj(sk), v)
                seen[sk] = v

    def _deps(self, reads, writes):
        deps = []
        for k in reads:
            if k in self.last_w:
                deps.append(self.last_w[k])
        for k in writes:
            if k in self.last_w:
                deps.append(self.last_w[k])
            deps.extend(self.readers.get(k, {}).items())
        return deps

    def _commit(self, h, reads, writes):
        for k in writes:
            self.last_w[k] = h
            self.readers[k] = {}
        for k in reads:
            r = self.readers.setdefault(k, {})
            if r.get(h[0], 0) < h[1]:
                r[h[0]] = h[1]

    def op(self, eng, fn, reads=(), writes=(), extra=()):
        deps = self._deps(reads, writes) + list(extra)
        if eng == "pe":
            deps = [d for d in deps if d is not None and d[0] != "pe"]
        self._wait(eng, deps)
        inst = fn(self.eng[eng])
        self.cnt[eng] += 1
        inst.then_inc(self.sem[eng], 1)
        h = (eng, self.cnt[eng])
        self._commit(h, reads, writes)
        self.ninst += 1
        return h

    def dma(self, q, fn, semname, reads=(), writes=(), extra=()):
        if semname not in self.dsem:
            self.dsem[semname] = self.nc.alloc_semaphore("d_" + str(semname))
            self.dcnt[semname] = 0
        deps = self._deps(reads, writes) + list(extra)
        deps = [d for d in deps if d is not None and d[0] != semname]
        self._wait(q, deps)
        inst = fn(self.eng[q])
        self.dcnt[semname] += 16
        inst.then_inc(self.dsem[semname], 16)
        h = (semname, self.dcnt[semname])
        self._commit(h, reads, writes)
        self.ninst += 1
        return h

    def barrier(self, final=False):
        allh = [(e, self.cnt[e]) for e in self.ENG if self.cnt[e] > 0]
        allh += [(s, c) for s, c in self.dcnt.items() if c > 0 and (final or s not in self.background)]
        for e in self.ENG:
            self._wait(e, allh)
        self.last_w = {}
        self.readers = {}


def build_program(debug=None):
    nc = bass.Bass("TRN2", target_bir_lowering=False)
    P = Prog(nc)

    def din(name, shape, dt=F32):
        return nc.dram_tensor(name, shape, dt, kind="ExternalInput").ap()

    x = din("x", [S, D])
    mem = din("mem", [MEM, D])
    g_mix = din("g_mix", [1, D])
    g_mem = din("g_mem", [1, D])
    w_in = din("w_in", [1, D, 4352])
    w_mem_kv = din("w_mem_kv", [1, D, 1024])
    g_q_dil = din("g_q_dil", [1, 64])
    g_k_dil = din("g_k_dil", [1, 64])
    g_q_mem = din("g_q_mem", [1, 128])
    g_k_mem = din("g_k_mem", [1, 128])
    w_o_sb = din("w_o_sb", [1, 512, D])
    w_o_dil = din("w_o_dil", [1, 256, D])
    w_o_mem = din("w_o_mem", [1, 512, D])
    w_gate = din("w_gate", [1, D, 3072])
    b_gate = din("b_gate", [1, 3072])
    w_out = din("w_out", [1, D, D])
    g_ffn = din("g_ffn", [1, D])
    w_peer_q = din("w_peer_q", [1, D, 2048])
    peer_subkeys = din("peer_subkeys", [1, 8, 2, 128, 128])
    peer_u = din("peer_u", [1, 16384, D])
    peer_v = din("peer_v", [1, 16384, D])
    out = nc.dram_tensor("out", [S, D], F32, kind="ExternalOutput").ap()
    dbg = {}

    def dbg_out(name, shape):
        dbg[name] = nc.dram_tensor("dbg_" + name, shape, F32, kind="ExternalOutput").ap()
        return dbg[name]

    win_r = w_in[0].rearrange("(c p) n -> p c n", p=128)

    psall = nc.alloc_psum_tensor("psall", [128, 8, 512], F32)

    def psf(b):
        return psall[:, b, :]

    def psh(b):
        return psall[:, b, :].bitcast(BF16)

    ident = nc.alloc_sbuf_tensor("ident", [128, 128], BF16)
    iot = nc.alloc_sbuf_tensor("iot", [128, 128], I32)
    nident = nc.alloc_sbuf_tensor("nident", [128, 128], BF16)
    umask = nc.alloc_sbuf_tensor("umask", [128, 128], U32)
    onesf = nc.alloc_sbuf_tensor("onesf", [128, 128], F32)
    gmixT = nc.alloc_sbuf_tensor("gmixT", [128, 8], F32)
    gmemT = nc.alloc_sbuf_tensor("gmemT", [128, 8], F32)
    epsc = nc.alloc_sbuf_tensor("epsc", [128, 1], F32)
    blockones = nc.alloc_sbuf_tensor("blockones", [128, 128], BF16)
    ones_bf = nc.alloc_sbuf_tensor("ones_bf", [128, 128], BF16)
    gdil = nc.alloc_sbuf_tensor("gdil", [128, 2], F32)
    gmemh = nc.alloc_sbuf_tensor("gmemh", [128, 2], F32)
    es = ExitStack()
    hT = es.enter_context(nc.sbuf_tensor("hT", [128, 8, S], BF16))
    osbT = es.enter_context(nc.sbuf_tensor("osbT", [128, 4, S], BF16))

    odilT = es.enter_context(nc.sbuf_tensor("odilT", [128, 2, S], BF16))
    omemT = es.enter_context(nc.sbuf_tensor("omemT", [128, 4, S], BF16))
    P.op("dve", lambda e: e.memset(epsc[:], EPS), writes=["consts"])
    P.op("dve", lambda e: e.memset(blockones[:], 0.0), writes=["consts"])
    P.op("dve", lambda e: e.memset(blockones[0:64, 0:64], 1.0), writes=["consts"])
    P.op("dve", lambda e: e.memset(blockones[64:128, 64:128], 1.0), writes=["consts"])
    P.op("dve", lambda e: e.memset(ones_bf[:], 1.0), writes=["consts"])
    with nc.allow_non_contiguous_dma(reason="tiny gain vectors"):
        P.dma("sp", lambda e: e.dma_start(out=gmemT[:], in_=g_mem.rearrange("o (c p) -> p (o c)", p=128)),
              "gmemT", writes=["gmemT"])
        for half in range(2):
            P.dma("sp", lambda e: e.dma_start(out=gdil[half * 64:(half + 1) * 64, 0:1], in_=g_q_dil.rearrange("o d -> d o")),
                  "gsm", writes=["consts"])
            P.dma("sp", lambda e: e.dma_start(out=gdil[half * 64:(half + 1) * 64, 1:2], in_=g_k_dil.rearrange("o d -> d o")),
                  "gsm", writes=["consts"])
        P.dma("sp", lambda e: e.dma_start(out=gmemh[:, 0:1], in_=g_q_mem.rearrange("o d -> d o")), "gsm", writes=["consts"])
        P.dma("sp", lambda e: e.dma_start(out=gmemh[:, 1:2], in_=g_k_mem.rearrange("o d -> d o")), "gsm", writes=["consts"])

    P.op("pool", lambda e: e.iota(iot[:], pattern=[[1, 128]], base=0, channel_multiplier=-1), writes=["iot"])
    P.op("dve", lambda e: e.tensor_single_scalar(out=ident[:], in_=iot[:], scalar=0.0, op=ALU.is_equal),
         reads=["iot"], writes=["ident"])
    P.op("dve", lambda e: e.tensor_scalar(out=nident[:], in0=iot[:], scalar1=0.0, scalar2=-1.0, op0=ALU.is_equal, op1=ALU.mult),
         reads=["iot"], writes=["ident"])
    P.op("dve", lambda e: e.tensor_single_scalar(out=umask[:], in_=iot[:], scalar=0.0, op=ALU.is_ge),
         reads=["iot"], writes=["umask"])
    P.op("dve", lambda e: e.memset(onesf[:], 1.0), writes=["umask"])
    with nc.allow_non_contiguous_dma(reason="tiny gain vectors"):
        P.dma("sp", lambda e: e.dma_start(out=gmixT[:], in_=g_mix.rearrange("o (c p) -> p (o c)", p=128)),
              "gmixT", writes=["gmixT"])

    fill_one = nc.gpsimd.to_reg(1.0)

    uvb = nc.dram_tensor("uvb", [16384, 2048], BF16).ap()
    P.background.add("cvt")
    cvt_state = {"k": 0, "h": None}

    def cvt_step(n):
        for _ in range(n):
            k = cvt_state["k"]
            if k >= 16:
                return
            cvt_state["k"] += 1
            src = (peer_u, peer_v)[k % 2][0]
            rows = slice((k // 2) * 2048, (k // 2 + 1) * 2048)
            cvt_state["h"] = P.dma("pool", lambda e: e.dma_start(out=uvb[rows, (k % 2) * 1024:(k % 2 + 1) * 1024],
                                                                 in_=src[rows, :]), "cvt")

    def rms_to_T(src, ntiles, gT, dstT, tag):
        with nc.sbuf_tensor(tag + "_xs", [128, 2, D], F32) as xs, \
             nc.sbuf_tensor(tag + "_xn", [128, 2, D], BF16) as xn, \
             nc.sbuf_tensor(tag + "_junk", [128, D], BF16) as junk, \
             nc.sbuf_tensor(tag + "_st", [128, 4, ntiles], F32) as st:
            for tt in range(ntiles):
                sl = tt % 2
                P.dma("sp", lambda e: e.dma_start(out=xs[:, sl, :], in_=src[tt * 128:(tt + 1) * 128, :]),
                      (tag, "xs", sl), writes=[(tag, "xs", sl)])
                P.op("act", lambda e: e.activation(out=junk[:], in_=xs[:, sl, :], func=AF.Square,
                                                   accum_out=st[:, 0, tt:tt + 1]),
                     reads=[(tag, "xs", sl)], writes=[(tag, "junk"), (tag, "st", tt)])
                P.op("act", lambda e: e.activation(out=st[:, 1, tt:tt + 1], in_=st[:, 0, tt:tt + 1], func=AF.Ln,
                                                   scale=1.0 / D, bias=epsc[:, 0:1]),
                     reads=[(tag, "st", tt), "consts"], writes=[(tag, "st", tt)])
                P.op("act", lambda e: e.activation(out=st[:, 3, tt:tt + 1], in_=st[:, 1, tt:tt + 1], func=AF.Exp, scale=-0.5),
                     reads=[(tag, "st", tt)], writes=[(tag, "st", tt)])
                P.op("act", lambda e: e.activation(out=xn[:, sl, :], in_=xs[:, sl, :], func=AF.Copy,
                                                   scale=st[:, 3, tt:tt + 1]),
                     reads=[(tag, "xs", sl), (tag, "st", tt)], writes=[(tag, "xn", sl)])
                b = tt % 2
                for c in range(8):
                    P.op("pe", lambda e: e.transpose(out=psh(b)[:, c * 128:(c + 1) * 128],
                                                     in_=xn[:, sl, c * 128:(c + 1) * 128], identity=ident[:]),
                         reads=[(tag, "xn", sl), "ident"], writes=[("ps", b)])
                P.op("dve", lambda e: e.tensor_tensor(
                    out=dstT[:, :, tt * 128:(tt + 1) * 128],
                    in0=psh(b).rearrange("p (c t) -> p c t", c=8),
                    in1=gT[:, :].unsqueeze(2).broadcast_to([128, 8, 128]), op=ALU.mult),
                    reads=[("ps", b), "gmixT", "gmemT"], writes=[(tag, "dstT", tt)])
            P.barrier()

    rms_to_T(x, NT, gmixT, hT, "x")
    HT_KEYS = [("x", "dstT", tt) for tt in range(NT)]

    cp_toggle = [0]

    def evac(out_ap, in_ap, reads, writes, eng=None):
        if eng is None:
            eng = ("act", "dve")[cp_toggle[0] % 2]
            cp_toggle[0] += 1
        if eng == "act":
            return P.op("act", lambda e: e.activation(out=out_ap, in_=in_ap, func=AF.Copy), reads=reads, writes=writes)
        return P.op(eng, lambda e: e.tensor_copy(out=out_ap, in_=in_ap), reads=reads, writes=writes)

    NSB = 4
    with ExitStack() as sbs:
        qkT = sbs.enter_context(nc.sbuf_tensor("qkT", [128, 8, S], BF16))
        vsb = sbs.enter_context(nc.sbuf_tensor("vsb", [128, NT, 512], BF16))
        wsb_cm = nc.sbuf_tensor("wsb", [128, 8, 1536], BF16)
        wsb = wsb_cm.__enter__()
        P.dma("pool", lambda e: e.dma_start(out=wsb[:], in_=win_r[:, :, 0:1536]), "wsb", writes=["wsb"])
        WSB = ["wsb"]
        nb = 0
        for fc in range(8):
            for tg in range(4):
                b = nb % 4
                nb += 1
                for c in range(8):
                    P.op("pe", lambda e: e.matmul(psf(b), lhsT=wsb[:, c, fc * 128:(fc + 1) * 128],
                                                  rhs=hT[:, c, tg * 512:(tg + 1) * 512], start=(c == 0), stop=(c == 7)),
                         reads=WSB + HT_KEYS if c in (0, 7) else [], writes=[("ps", b)])
                evac(qkT[:, fc, tg * 512:(tg + 1) * 512], psf(b), [("ps", b)], [("qkT", fc, tg)])
        for tt in range(NT):
            b = nb % 4
            nb += 1
            for c in range(8):
                P.op("pe", lambda e: e.matmul(psf(b), lhsT=hT[:, c, tt * 128:(tt + 1) * 128],
                                              rhs=wsb[:, c, 1024:1536], start=(c == 0), stop=(c == 7)),
                     reads=WSB + HT_KEYS if c in (0, 7) else [], writes=[("ps", b)])
            evac(vsb[:, tt, :], psf(b), [("ps", b)], [("vsb", tt)])

        P.barrier()
        wsb_cm.__exit__(None, None, None)
        mt = sbs.enter_context(nc.sbuf_tensor("mt", [128, NSB, S + 8], F32))
        Abf = sbs.enter_context(nc.sbuf_tensor("Abf", [128, NSB, S + 8], BF16))
        ATs = sbs.enter_context(nc.sbuf_tensor("ATs", [128, NSB, S], BF16))
        st = {"zb": 0, "tb": 0}

        def sb_stage_a(h, qi, sl):
            ch = h // 2
            pb = (h % 2) * 64
            nk = (qi + 1) * 128
            nchunk = (nk + 511) // 512
            qkeys = [("qkT", ch, qi // 4)]
            for cc in range(nchunk):
                w = min(512, nk - cc * 512)
                b = st["zb"] % 4
                st["zb"] += 1
                P.op("pe", lambda e: e.matmul(psf(b)[:, 0:w], lhsT=qkT[pb:pb + 64, ch, qi * 128:(qi + 1) * 128],
                                              rhs=qkT[pb:pb + 64, 4 + ch, cc * 512:cc * 512 + w],
                                              start=True, stop=True),
                     reads=qkeys + [("qkT", 4 + ch, cc)], writes=[("ps", b)])
                P.op("act", lambda e: e.activation(out=mt[:, sl, cc * 512:cc * 512 + w], in_=psf(b)[:, 0:w],
                                                   func=AF.Sigmoid, scale=-0.125),
                     reads=[("ps", b)], writes=[("mt", sl)])
            P.op("pool", lambda e: e.affine_select(out=mt[:, sl, qi * 128:(qi + 1) * 128],
                                                   in_=mt[:, sl, qi * 128:(qi + 1) * 128],
                                                   pattern=[[-1, 128]], compare_op=ALU.is_gt, fill=fill_one,
                                                   base=0, channel_multiplier=1),
                 reads=[("mt", sl)], writes=[("mt", sl)])
            P.op("pool", lambda e: e.memset(Abf[:, sl, nk:nk + 1], 1.0), reads=[], writes=[("Abf", sl)])

        def sb_stage_a2(h, qi, sl):
            nk = (qi + 1) * 128
            rev = mt[:, sl, nk - 1::-1]
            P.op("dve", lambda e: e.tensor_tensor_scan(out=Abf[:, sl, nk - 1::-1], data0=rev, data1=rev, initial=1.0,
                                                       op0=ALU.mult, op1=ALU.bypass),
                 reads=[("mt", sl)], writes=[("Abf", sl)])

        def sb_stage_b(h, qi, sl):
            ch = h // 2
            pb = (h % 2) * 64
            for kb0 in range(0, qi + 1, 4):
                nblk = min(4, qi + 1 - kb0)
                b = 4 + st["tb"] % 2
                st["tb"] += 1
                for j in range(nblk):
                    kb = kb0 + j
                    P.op("pe", lambda e: e.matmul(psf(b)[:, j * 128:(j + 1) * 128],
                                                  lhsT=Abf[:, sl, kb * 128 + 1:(kb + 1) * 128 + 1], rhs=ident[:],
                                                  start=True, stop=False),
                         reads=[("Abf", sl), "ident"], writes=[("ps", b)])
                    P.op("pe", lambda e: e.matmul(psf(b)[:, j * 128:(j + 1) * 128],
                                                  lhsT=Abf[:, sl, kb * 128:(kb + 1) * 128], rhs=nident[:],
                                                  start=False, stop=True),
                         reads=[("Abf", sl), "ident"], writes=[("ps", b)])
                evac(ATs[:, sl, kb0 * 128:(kb0 + nblk) * 128], psf(b)[:, 0:nblk * 128],
                     [("ps", b)], [("ATs", sl, kb0)], eng="act")

        def sb_stage_c(h, qi, sl):
            ch = h // 2
            pb = (h % 2) * 64
            ab = 6 + (qi // 4) % 2
            for kb in range(qi + 1):
                P.op("pe", lambda e: e.matmul(psf(ab)[pb:pb + 64, (qi % 4) * 128:(qi % 4 + 1) * 128],
                                              lhsT=vsb[:, kb, h * 64:(h + 1) * 64],
                                              rhs=ATs[:, sl, kb * 128:(kb + 1) * 128],
                                              start=(kb == 0), stop=(kb == qi)),
                     reads=[("vsb", kb), ("ATs", sl, (kb // 4) * 4)], writes=[("ps", ab)])
            if qi % 4 == 3:
                evac(osbT[pb:pb + 64, ch, (qi - 3) * 128:(qi + 1) * 128], psf(ab)[pb:pb + 64, :],
                     [("ps", ab)], [("osbT", h, qi // 4)], eng="act")

        units = [(h, qi) for h in range(8) for qi in range(NT)]
        nun = len(units)
        for u in range(nun + 4):
            if u < nun:
                if units[u][1] == 0 and units[u][0] % 2 == 0:
                    cvt_step(1)
                sb_stage_a(units[u][0], units[u][1], u % NSB)
            for lag, fn in ((2, sb_stage_a2), (3, sb_stage_b), (4, sb_stage_c)):
                v = u - lag
                if 0 <= v < nun:
                    fn(units[v][0], units[v][1], v % NSB)
        P.barrier()

    if debug == "sb":
        d1 = dbg_out("osbT", [128, 4 * S])
        d2 = dbg_out("hT", [128, 8 * S])
        P.dma("pool", lambda e: e.dma_start(out=d1, in_=osbT[:].rearrange("p c t -> p (c t)"), max_dma_last_dim=4096), "dbg1")
        P.dma("pool", lambda e: e.dma_start(out=d2, in_=hT[:].rearrange("p c t -> p (c t)"), max_dma_last_dim=4096), "dbg2")
        P.barrier(final=True)
        return nc, P

    def head_norm(b_in, b_ss, w, onesT, inv_n, gcol, dst_ap, sq, t1, tagk, dst_key):
        P.op("act", lambda e: e.activation(out=sq[:, 0:w], in_=psf(b_in)[:, 0:w], func=AF.Square),
             reads=[("ps", b_in)], writes=[(tagk, "sq")])
        P.op("pe", lambda e: e.matmul(psf(b_ss)[:, 0:w], lhsT=onesT, rhs=sq[:, 0:w], start=True, stop=True),
             reads=[(tagk, "sq"), "consts"], writes=[("ps", b_ss)])
        P.op("act", lambda e: e.activation(out=t1[:, 0:w], in_=psf(b_ss)[:, 0:w], func=AF.Ln, scale=inv_n,
                                           bias=epsc[:, 0:1]),
             reads=[("ps", b_ss), "consts"], writes=[(tagk, "t1")])
        P.op("act", lambda e: e.activation(out=t1[:, 0:w], in_=t1[:, 0:w], func=AF.Exp, scale=-0.5),
             reads=[(tagk, "t1")], writes=[(tagk, "t1")])
        P.op("dve", lambda e: e.scalar_tensor_tensor(out=dst_ap, in0=psf(b_in)[:, 0:w], scalar=gcol,
                                                     in1=t1[:, 0:w], op0=ALU.mult, op1=ALU.mult),
             reads=[("ps", b_in), (tagk, "t1"), "consts"], writes=[dst_key])

    DIL_R = (1, 4, 16)
    with nc.sbuf_tensor("dtab", [128, 6, 512], F32) as dtab, \
         nc.sbuf_tensor("gapi", [128, 256], I32) as gapi, \
         nc.sbuf_tensor("gapf", [128, 256], F32) as gapf, \
         nc.sbuf_tensor("maskf", [128, 256], F32) as maskf, \
         nc.sbuf_tensor("wd", [128, 8, 9, 128], BF16) as wd, \
         nc.sbuf_tensor("dq", [128, 3, S], BF16) as dq, \
         nc.sbuf_tensor("dk", [128, 3, S], BF16) as dk, \
         nc.sbuf_tensor("dv", [128, 3, 16, 128], BF16) as dv, \
         nc.sbuf_tensor("ND", [128, 2, S], F32) as ND, \
         nc.sbuf_tensor("dsq", [128, 512], BF16) as dsq, \
         nc.sbuf_tensor("dt1", [128, 512], F32) as dt1, \
         nc.sbuf_tensor("dE", [128, 3, 512], F32) as dE, \
         nc.sbuf_tensor("dPT", [128, 3, 512], BF16) as dPT:
        P.op("pool", lambda e: e.iota(gapi[:].rearrange("p (a q) -> p a q", a=2), pattern=[[128, 2], [1, 128]],
                                      base=0, channel_multiplier=-1), writes=["gapi"])
        P.op("dve", lambda e: e.tensor_copy(out=gapf[:], in_=gapi[:]), reads=["gapi"], writes=["gapf"])
        P.op("dve", lambda e: e.tensor_single_scalar(out=maskf[:, 0:128], in_=gapf[:, 0:128], scalar=0.0, op=ALU.is_ge),
             reads=["gapf"], writes=["maskf0"])
        P.op("dve", lambda e: e.tensor_single_scalar(out=maskf[:, 128:256], in_=gapf[:, 128:256], scalar=128.0,
                                                     op=ALU.is_le), reads=["gapf"], writes=["maskf1"])
        P.op("dve", lambda e: e.tensor_scalar_max(out=gapf[:, 0:128], in0=gapf[:, 0:128], scalar1=0.0),
             reads=["gapf", "maskf0"], writes=["gapf"])
        for g in range(3):
            for j in range(2):
                for e_ in range(2):
                    H = 4 * g + 2 * j + e_
                    slope = 2.0 ** (-8.0 * (H + 1) / 12.0)
                    for part in range(2):
                        col = (e_ * 2 + part) * 128
                        P.op("act", lambda e: e.activation(out=dtab[:, g * 2 + j, col:col + 128],
                                                           in_=gapf[:, part * 128:(part + 1) * 128], func=AF.Exp,
                                                           scale=-slope * DIL_R[g]),
                             reads=["gapf"], writes=[("dtab", g, j, part, e_)])
                        P.op("dve", lambda e: e.tensor_tensor(out=dtab[:, g * 2 + j, col:col + 128],
                                                              in0=dtab[:, g * 2 + j, col:col + 128],
                                                              in1=maskf[:, part * 128:(part + 1) * 128], op=ALU.mult),
                             reads=[("dtab", g, j, part, e_), "maskf0", "maskf1"], writes=[("dtab", g, j, part, e_)])
        DTAB = lambda g, j: [("dtab", g, j, pp, ee) for pp in range(2) for ee in range(2)]
        if debug == "dil_a":
            P.barrier(final=True)
            return nc, P

        for j in range(2):
            for t3 in range(3):
                for g in range(3):
                    col = 1536 + t3 * 768 + (2 * g + j) * 128
                    P.dma("pool", lambda e: e.dma_start(out=wd[:, :, t3 * 3 + g, :], in_=win_r[:, :, col:col + 128]),
                          ("wd", t3 * 3 + g), writes=[("wd", t3 * 3 + g)])
            cvt_step(6)
            nb = 0
            for t3 in range(2):
                dst = dq if t3 == 0 else dk
                for g in range(3):
                    for tg in range(4):
                        b = nb % 3
                        nb += 1
                        for c in range(8):
                            P.op("pe", lambda e: e.matmul(psf(b), lhsT=wd[:, c, t3 * 3 + g, :],
                                                          rhs=hT[:, c, tg * 512:(tg + 1) * 512],
                                                          start=(c == 0), stop=(c == 7)),
                                 reads=[("wd", t3 * 3 + g)] if c in (0, 7) else [], writes=[("ps", b)])
                        head_norm(b, 3, 512, blockones[:], 1.0 / 64, gdil[:, t3:t3 + 1],
                                  dst[:, g, tg * 512:(tg + 1) * 512], dsq, dt1, "dil", ("dqk", t3, g, tg))
            if debug == "dil_b":
                P.barrier(final=True)
                return nc, P
            for g in range(3):
                r = DIL_R[g]
                nbk = 16 // r
                for bi0 in range(0, 16, 4):
                    b = nb % 3
                    nb += 1
                    for jj in range(4):
                        bi = bi0 + jj
                        c_, n_ = bi // nbk, bi % nbk
                        st_ = c_ + r * 128 * n_
                        for c in range(8):
                            P.op("pe", lambda e: e.matmul(psf(b)[:, jj * 128:(jj + 1) * 128],
                                                          lhsT=hT[:, c, st_:st_ + 127 * r + 1:r],
                                                          rhs=wd[:, c, 6 + g, :], start=(c == 0), stop=(c == 7)),
                                 reads=[("wd", 6 + g)] if c in (0, 7) else [], writes=[("ps", b)])
                    evac(dv[:, g, bi0:bi0 + 4, :], psf(b).rearrange("p (a f) -> p a f", a=4), [("ps", b)],
                         [("dv", g, bi0)])
            if debug == "dil_c":
                P.barrier(final=True)
                return nc, P
            def dil_geom(g, bi):
                r = DIL_R[g]
                nbk = 16 // r
                c_, n_ = bi // nbk, bi % nbk
                st_ = c_ + r * 128 * n_
                tsl = slice(st_, st_ + 127 * r + 1, r)
                psl = slice(st_ - 128 * r, st_ - 128 * r + 127 * r + 1, r)
                return n_, tsl, psl

            def dil_stage_a(g, bi, un):
                n_, tsl, psl = dil_geom(g, bi)
                sl = un % 3
                sb_ = 2 * (un % 3)
                wp = 256 if n_ > 0 else 128
                rk = [("dqk", 0, g, tt_) for tt_ in range(4)] + [("dqk", 1, g, tt_) for tt_ in range(4)]
                for e_ in range(2):
                    for part in range(2 if n_ > 0 else 1):
                        ksl = tsl if part == 0 else psl
                        P.op("pe", lambda e: e.matmul(psf(sb_ + e_)[:, part * 128:(part + 1) * 128],
                                                      lhsT=dk[e_ * 64:(e_ + 1) * 64, g, ksl],
                                                      rhs=dq[e_ * 64:(e_ + 1) * 64, g, tsl], start=True, stop=True),
                             reads=rk, writes=[("ps", sb_ + e_)])
                P.op("act", lambda e: e.activation(out=dE[:, sl, :].rearrange("p (a q) -> p a q", a=2)[:, :, 0:wp],
                                                   in_=psall[:, sb_:sb_ + 2, 0:wp], func=AF.Exp, scale=0.125),
                     reads=[("ps", sb_), ("ps", sb_ + 1)], writes=[("dE", sl)])
                P.op("dve", lambda e: e.tensor_tensor(
                    out=dPT[:, sl, :].rearrange("p (a q) -> p a q", a=2)[:, :, 0:wp],
                    in0=dE[:, sl, :].rearrange("p (a q) -> p a q", a=2)[:, :, 0:wp],
                    in1=dtab[:, g * 2 + j, :].rearrange("p (a q) -> p a q", a=2)[:, :, 0:wp], op=ALU.mult),
                     reads=[("dE", sl)] + DTAB(g, j), writes=[("dPT", sl)])

            def dil_stage_b(g, bi, un):
                n_, tsl, psl = dil_geom(g, bi)
                sl = un % 3
                ob_ = 6 + un % 2
                for kind in range(2):
                    for e_ in range(2):
                        for part in range(2 if n_ > 0 else 1):
                            col = (e_ * 2 + part) * 128
                            if kind == 0:
                                lt = dv[:, g, bi - part, e_ * 64:(e_ + 1) * 64]
                            else:
                                lt = ones_bf[:, 0:64]
                            P.op("pe", lambda e: e.matmul(psf(ob_)[e_ * 64:(e_ + 1) * 64, kind * 128:(kind + 1) * 128],
                                                          lhsT=lt, rhs=dPT[:, sl, col:col + 128],
                                                          start=(part == 0), stop=(part == (1 if n_ > 0 else 0))),
                                 reads=[("dPT", sl), ("dv", g, (bi // 4) * 4), ("dv", g, ((bi - part) // 4) * 4), "consts"],
                                 writes=[("ps", ob_)])
                nd_out = ND[:, :, tsl]
                nd_in = psf(ob_)[:, 0:256].rearrange("p (a q) -> p a q", a=2)
                if g == 0:
                    P.op("dve", lambda e: e.tensor_copy(out=nd_out, in_=nd_in), reads=[("ps", ob_)], writes=["ND"])
                else:
                    P.op("dve", lambda e: e.tensor_tensor(out=nd_out, in0=nd_out, in1=nd_in, op=ALU.add),
                         reads=[("ps", ob_), "ND"], writes=["ND"])

            dunits = [(g, bi) for g in range(3) for bi in range(16)]
            DLOOK = 2
            for u in range(len(dunits) + DLOOK):
                if u < len(dunits):
                    dil_stage_a(dunits[u][0], dunits[u][1], u)
                if u >= DLOOK:
                    dil_stage_b(dunits[u - DLOOK][0], dunits[u - DLOOK][1], u - DLOOK)
            P.op("dve", lambda e: e.reciprocal(out=ND[:, 1, :], in_=ND[:, 1, :]), reads=["ND"], writes=["ND"])
            P.op("dve", lambda e: e.tensor_tensor(out=odilT[:, j, :], in0=ND[:, 0, :], in1=ND[:, 1, :], op=ALU.mult),
                 reads=["ND"], writes=[("odilT", j)])
        P.barrier()

    if debug == "dil":
        d1 = dbg_out("odilT", [128, 2 * S])
        P.dma("pool", lambda e: e.dma_start(out=d1, in_=odilT[:].rearrange("p c t -> p (c t)"), max_dma_last_dim=4096), "dbg1")
        P.barrier(final=True)
        return nc, P

    pre = ExitStack()
    wgate = pre.enter_context(nc.sbuf_tensor("wgate", [128, 8, 3072], BF16))
    wg_r = w_gate[0].rearrange("(c p) n -> p c n", p=128)
    for br in range(3):
        P.dma("pool", lambda e: e.dma_start(out=wgate[:, :, br * 1024:(br + 1) * 1024],
                                             in_=wg_r[:, :, br * 1024:(br + 1) * 1024]), ("wgate", br), writes=[("wgate", br)])

    with nc.sbuf_tensor("wkv", [128, 8, 1024], BF16) as wkv, \
         nc.sbuf_tensor("wmq", [128, 8, 512], BF16) as wmq, \
         nc.sbuf_tensor("memhT", [128, 8, MEM], BF16) as memhT, \
         nc.sbuf_tensor("kmT", [128, 4, MEM], BF16) as kmT, \
         nc.sbuf_tensor("vm", [128, 2, 512], BF16) as vm, \
         nc.sbuf_tensor("qmT", [128, 4, S], BF16) as qmT, \
         nc.sbuf_tensor("msq", [128, 512], BF16) as msq, \
         nc.sbuf_tensor("mt1", [128, 512], F32) as mt1, \
         nc.sbuf_tensor("mPT", [128, 2, 2, 512], BF16) as mPT, \
         nc.sbuf_tensor("mrd", [128, 2, 512], F32) as mrd:
        P.dma("pool", lambda e: e.dma_start(out=wkv[:], in_=w_mem_kv[0].rearrange("(c p) n -> p c n", p=128)),
              "wkv", writes=["wkv"])
        P.dma("pool", lambda e: e.dma_start(out=wmq[:], in_=win_r[:, :, 3840:4352]), "wmq", writes=["wmq"])
        rms_to_T(mem, 2, gmemT, memhT, "m")
        for hd in range(4):
            b = hd % 2
            for c in range(8):
                P.op("pe", lambda e: e.matmul(psf(b)[:, 0:MEM], lhsT=wkv[:, c, hd * 128:(hd + 1) * 128],
                                              rhs=memhT[:, c, :], start=(c == 0), stop=(c == 7)),
                     reads=["wkv"] if c in (0, 7) else [], writes=[("ps", b)])
            head_norm(b, 3, MEM, ones_bf[:], 1.0 / 128, gmemh[:, 1:2], kmT[:, hd, :], msq, mt1, "mem", ("kmT", hd))
        for blk in range(2):
            b = blk % 2
            for c in range(8):
                P.op("pe", lambda e: e.matmul(psf(b), lhsT=memhT[:, c, blk * 128:(blk + 1) * 128],
                                              rhs=wkv[:, c, 512:1024], start=(c == 0), stop=(c == 7)),
                     reads=["wkv"] if c in (0, 7) else [], writes=[("ps", b)])
            evac(vm[:, blk, :], psf(b), [("ps", b)], [("vm", blk)])
        nb = 0
        for hd in range(4):
            for tg in range(4):
                b = nb % 3
                nb += 1
                for c in range(8):
                    P.op("pe", lambda e: e.matmul(psf(b), lhsT=wmq[:, c, hd * 128:(hd + 1) * 128],
                                                  rhs=hT[:, c, tg * 512:(tg + 1) * 512], start=(c == 0), stop=(c == 7)),
                         reads=["wmq"] if c in (0, 7) else [], writes=[("ps", b)])
                head_norm(b, 3, 512, ones_bf[:], 1.0 / 128, gmemh[:, 0:1], qmT[:, hd, tg * 512:(tg + 1) * 512],
                          msq, mt1, "mem", ("qmT", hd, tg))
        un = 0
        for hd in range(4):
            for tg in range(4):
                sl = un % 2
                un += 1
                for blk in range(2):
                    sb_ = 4 + blk
                    P.op("pe", lambda e: e.matmul(psf(sb_), lhsT=kmT[:, hd, blk * 128:(blk + 1) * 128],
                                                  rhs=qmT[:, hd, tg * 512:(tg + 1) * 512], start=True, stop=True),
                         reads=[("kmT", hd), ("qmT", hd, tg)], writes=[("ps", sb_)])
                    P.op("act", lambda e: e.activation(out=mPT[:, sl, blk, :], in_=psf(sb_), func=AF.Exp,
                                                       scale=128.0 ** -0.5),
                         reads=[("ps", sb_)], writes=[("mPT", sl, blk)])
                for blk in range(2):
                    P.op("pe", lambda e: e.matmul(psf(6), lhsT=vm[:, blk, hd * 128:(hd + 1) * 128],
                                                  rhs=mPT[:, sl, blk, :], start=(blk == 0), stop=(blk == 1)),
                         reads=[("mPT", sl, blk), ("vm", blk)], writes=[("ps", 6)])
                for blk in range(2):
                    P.op("pe", lambda e: e.matmul(psf(7), lhsT=ones_bf[:], rhs=mPT[:, sl, blk, :],
                                                  start=(blk == 0), stop=(blk == 1)),
                         reads=[("mPT", sl, blk), "consts"], writes=[("ps", 7)])
                P.op("dve", lambda e: e.reciprocal(out=mrd[:, sl, :], in_=psf(7)), reads=[("ps", 7)], writes=[("mrd", sl)])
                P.op("dve", lambda e: e.tensor_tensor(out=omemT[:, hd, tg * 512:(tg + 1) * 512], in0=psf(6),
                                                      in1=mrd[:, sl, :], op=ALU.mult),
                     reads=[("ps", 6), ("mrd", sl)], writes=[("omemT", hd, tg)])
        P.barrier()

    if debug == "mem":
        d1 = dbg_out("omemT", [128, 4 * S])
        P.dma("pool", lambda e: e.dma_start(out=d1, in_=omemT[:].rearrange("p c t -> p (c t)"), max_dma_last_dim=4096), "dbg1")
        P.barrier(final=True)
        return nc, P

    if debug == "x1":
        x1d = dbg_out("x1", [S, D])
    else:
        x1d = nc.dram_tensor("x1d", [S, D], F32).ap()
    with nc.sbuf_tensor("wosb", [128, 4, D], BF16) as wosb, \
         nc.sbuf_tensor("wodil", [128, 2, D], BF16) as wodil, \
         nc.sbuf_tensor("womem", [128, 4, D], BF16) as womem, \
         nc.sbuf_tensor("wout", [128, 8, D], BF16) as wout, \
         nc.sbuf_tensor("bg", [128, 24], F32) as bg, \
         nc.sbuf_tensor("gsb", [128, 2, 3, 512], F32) as gsb, \
         nc.sbuf_tensor("macc", [128, 2, 512], F32) as macc, \
         nc.sbuf_tensor("mtmp", [128, 2, 512], F32) as mtmp, \
         nc.sbuf_tensor("mrgT", [128, 8, 512], BF16) as mrgT, \
         nc.sbuf_tensor("xr", [128, 2, D], F32) as xr, \
         nc.sbuf_tensor("x1s", [128, 2, D], F32) as x1s:
        P.dma("pool", lambda e: e.dma_start(out=wosb[:], in_=w_o_sb[0].rearrange("(c p) n -> p c n", p=128)), "wosb", writes=["wosb"])
        P.dma("pool", lambda e: e.dma_start(out=wodil[:], in_=w_o_dil[0].rearrange("(c p) n -> p c n", p=128)), "wodil", writes=["wodil"])
        P.dma("pool", lambda e: e.dma_start(out=womem[:], in_=w_o_mem[0].rearrange("(c p) n -> p c n", p=128)), "womem", writes=["womem"])
        P.dma("pool", lambda e: e.dma_start(out=wout[:], in_=w_out[0].rearrange("(c p) n -> p c n", p=128)), "wout", writes=["wout"])
        with nc.allow_non_contiguous_dma(reason="tiny bias vector"):
            P.dma("sp", lambda e: e.dma_start(out=bg[:], in_=b_gate.rearrange("o (c p) -> p (o c)", p=128)), "bg", writes=["bg"])
        branches = [(wosb, osbT, 4, "wosb"), (wodil, odilT, 2, "wodil"), (womem, omemT, 4, "womem")]
        un = 0
        xt = 0
        for tg in range(4):
            tsl = slice(tg * 512, (tg + 1) * 512)
            for dc in range(8):
                sl = un % 2
                un += 1
                for br in range(3):
                    gb = br
                    for c in range(8):
                        P.op("pe", lambda e: e.matmul(psf(gb), lhsT=wgate[:, c, br * 1024 + dc * 128:br * 1024 + (dc + 1) * 128],
                                                      rhs=hT[:, c, tsl], start=(c == 0), stop=(c == 7)),
                             reads=[("wgate", br)] if c in (0, 7) else [], writes=[("ps", gb)])
                    P.op("act", lambda e: e.activation(out=gsb[:, sl, br, :], in_=psf(gb), func=AF.Sigmoid,
                                                       bias=bg[:, br * 8 + dc:br * 8 + dc + 1], scale=1.0),
                         reads=[("ps", gb), "bg"], writes=[("gsb", sl, br)])
                for br in range(3):
                    wt, oT, nkc, wkey = branches[br]
                    yb = 3 + br
                    for kc in range(nkc):
                        P.op("pe", lambda e: e.matmul(psf(yb), lhsT=wt[:, kc, dc * 128:(dc + 1) * 128], rhs=oT[:, kc, tsl],
                                                      start=(kc == 0), stop=(kc == nkc - 1)),
                             reads=[wkey] if kc in (0, nkc - 1) else [], writes=[("ps", yb)])
                P.op("dve", lambda e: e.tensor_tensor(out=macc[:, sl, :], in0=psf(3), in1=gsb[:, sl, 0, :], op=ALU.mult),
                     reads=[("ps", 3), ("gsb", sl, 0)], writes=[("macc", sl)])
                P.op("dve", lambda e: e.tensor_tensor(out=mtmp[:, sl, :], in0=psf(4), in1=gsb[:, sl, 1, :], op=ALU.mult),
                     reads=[("ps", 4), ("gsb", sl, 1)], writes=[("mtmp", sl)])
                P.op("pool", lambda e: e.tensor_tensor(out=macc[:, sl, :], in0=macc[:, sl, :], in1=mtmp[:, sl, :], op=ALU.add),
                     reads=[("macc", sl), ("mtmp", sl)], writes=[("macc", sl)])
                P.op("dve", lambda e: e.tensor_tensor(out=mtmp[:, sl, :], in0=psf(5), in1=gsb[:, sl, 2, :], op=ALU.mult),
                     reads=[("ps", 5), ("gsb", sl, 2)], writes=[("mtmp", sl)])
                P.op("pool", lambda e: e.tensor_tensor(out=mrgT[:, dc, :], in0=macc[:, sl, :], in1=mtmp[:, sl, :], op=ALU.add),
                     reads=[("macc", sl), ("mtmp", sl)], writes=[("mrgT", dc)])
            for tl in range(4):
                tt = tg * 4 + tl
                xs_ = xt % 2
                xt += 1
                P.dma("sp", lambda e: e.dma_start(out=xr[:, xs_, :], in_=x[tt * 128:(tt + 1) * 128, :]), ("xr", xs_),
                      writes=[("xr", xs_)])
                for half in range(2):
                    ob = 6 + half
                    for dc in range(8):
                        P.op("pe", lambda e: e.matmul(psf(ob), lhsT=mrgT[:, dc, tl * 128:(tl + 1) * 128],
                                                      rhs=wout[:, dc, half * 512:(half + 1) * 512],
                                                      start=(dc == 0), stop=(dc == 7)),
                             reads=["wout"] + [("mrgT", d_) for d_ in range(8)] if dc in (0, 7) else [], writes=[("ps", ob)])
                    P.op("dve", lambda e: e.tensor_tensor(out=x1s[:, xs_, half * 512:(half + 1) * 512], in0=psf(ob),
                                                          in1=xr[:, xs_, half * 512:(half + 1) * 512], op=ALU.add),
                         reads=[("ps", ob), ("xr", xs_)], writes=[("x1s", xs_, half)])
                P.dma("sp", lambda e: e.dma_start(out=x1d[tt * 128:(tt + 1) * 128, :], in_=x1s[:, xs_, :]), ("x1o", xs_),
                      reads=[("x1s", xs_, 0), ("x1s", xs_, 1)], writes=[("x1d", tt)])
        P.barrier()

    if debug == "x1":
        P.barrier(final=True)
        return nc, P
    pre.close()
    es.close()

    cvt_step(16)
    NR = 20
    GS = 4
    FUSE_EVERY = 4
    pcnt = [0]
    jcnt = [0]
    with ExitStack() as pes:
        def sb(name, shape, dt):
            return pes.enter_context(nc.sbuf_tensor(name, shape, dt))
        wpq = sb("wpq", [128, 8, 2048], BF16)
        subkT = sb("subkT", [128, 16, 128], BF16)
        gffn_b = sb("gffn_b", [128, D], F32)
        iota16 = sb("iota16", [128, 16], F32)
        x1t = sb("x1t", [128, 2, D], F32)
        outt = sb("outt", [128, D], F32)
        h2 = sb("h2", [128, 2, D], F32)
        h2b = sb("h2b", [128, 2, D], BF16)
        prodr = sb("prodr", [128, 4, D], BF16)
        junkB = sb("junkB", [128, 2, D], mybir.dt.float8e4)
        h2T = sb("h2T", [128, 8, 128], BF16)
        pst = sb("pst", [128, 4], F32)
        qT = sb("qT", [128, 16, 128], BF16)
        sc = sb("sc", [128, 16, 128], F32)
        scw = sb("scw", [128, 16, 128], F32)
        top = sb("top", [128, 16, 16], F32)
        idx = sb("idx", [128, 16, 16], U32)
        idxf = sb("idxf", [128, 16, 16], F32)
        cand = sb("cand", [128, 8, 256], F32)
        cwk = scw[:].rearrange("p q k -> p (q k)").rearrange("p (h a) -> p h a", h=8)
        ctop = sb("ctop", [128, 8, 16], F32)
        cpos = sb("cpos", [128, 8, 16], U32)
        abu = sb("abu", [128, 2, 128], U32)
        abf = sb("abf", [128, 2, 128], F32)
        oh = sc[:].rearrange("p q k -> p (q k)").rearrange("p (h a) -> p h a", h=8)
        isel = sb("isel", [128, 2, 128], F32)
        eidxf = sb("eidxf", [128, 128], F32)
        eidx = sb("eidx", [128, 2, 128], U32)
        gate = sb("gate", [128, 2, 128], F32)
        gsum = sb("gsum", [128, 2, 8], F32)
        apre = sb("apre", [128, 2, 128], F32)
        wgt = sb("wgt", [128, 2, 128], F32)
        junkA = sb("junkA", [128, D], BF16)
        junkD = sb("junkD", [128, 1, D], BF16)
        diag = sb("diag", [128, 16, 128], BF16)

        P.dma("pool", lambda e: e.dma_start(out=wpq[:], in_=w_peer_q[0].rearrange("(c p) n -> p c n", p=128)),
              "wpq", writes=["wpq"])
        P.dma("sp", lambda e: e.dma_start(out=gffn_b[:], in_=g_ffn.broadcast_to([128, D])), "gffn", writes=["gffn"])
        P.op("pool", lambda e: e.iota(iot[:, 0:16], pattern=[[1, 16]], base=0, channel_multiplier=0), writes=["iot"])
        P.op("dve", lambda e: e.tensor_copy(out=iota16[:], in_=iot[:, 0:16]), reads=["iot"], writes=["iota16"])
        with nc.sbuf_tensor("subk", [128, 16, 128], BF16) as subk:
            P.dma("pool", lambda e: e.dma_start(out=subk[:], in_=peer_subkeys[0].rearrange("h t k d -> k (h t) d")),
                  "subk", writes=["subk"])
            for qg in range(4):
                b = qg % 2
                for jj in range(4):
                    qc = qg * 4 + jj
                    P.op("pe", lambda e: e.transpose(out=psh(b)[:, jj * 128:(jj + 1) * 128], in_=subk[:, qc, :], identity=ident[:]),
                         reads=["subk", "ident"], writes=[("ps", b)])
                evac(subkT[:, qg * 4:(qg + 1) * 4, :], psh(b)[:, 0:512].rearrange("p (a k) -> p a k", a=4), [("ps", b)], ["subkT"])
            P.barrier()
        UV = sb("UV", [128, NR, 2048], BF16)

        def topk_ops(tt):
            p_ = tt % 2
            K = lambda n: (n, p_)
            P.dma("sp", lambda e: e.dma_start(out=x1t[:, p_, :], in_=x1d[tt * 128:(tt + 1) * 128, :]), ("x1t", p_), writes=[K("x1t")])
            P.op("act", lambda e: e.activation(out=junkA[:], in_=x1t[:, p_, :], func=AF.Square, accum_out=pst[:, 0:1]),
                 reads=[K("x1t")], writes=["junkA", "pst"])
            P.op("dve", lambda e: e.tensor_scalar(out=pst[:, 1:2], in0=pst[:, 0:1], scalar1=1.0 / D, scalar2=EPS,
                                                  op0=ALU.mult, op1=ALU.add), reads=["pst"], writes=["pst"])
            P.op("act", lambda e: e.activation(out=pst[:, 2:3], in_=pst[:, 1:2], func=AF.Sqrt), reads=["pst"], writes=["pst"])
            P.op("dve", lambda e: e.reciprocal(out=pst[:, 3:4], in_=pst[:, 2:3]), reads=["pst"], writes=["pst"])
            P.op("dve", lambda e: e.scalar_tensor_tensor(out=h2[:, p_, :], in0=x1t[:, p_, :], scalar=pst[:, 3:4], in1=gffn_b[:],
                                                         op0=ALU.mult, op1=ALU.mult),
                 reads=[K("x1t"), "pst", "gffn"], writes=[K("h2")])
            P.op("act", lambda e: e.activation(out=h2b[:, p_, :], in_=h2[:, p_, :], func=AF.Copy), reads=[K("h2")], writes=[K("h2b")])
            for c in range(8):
                P.op("pe", lambda e: e.transpose(out=psh(0)[:, c * 128:(c + 1) * 128], in_=h2b[:, p_, c * 128:(c + 1) * 128],
                                                 identity=ident[:]), reads=[K("h2b"), "ident"], writes=[("ps", 0)])
            evac(h2T[:].rearrange("p c t -> p (c t)"), psh(0), [("ps", 0)], ["h2T"], eng="act")
            yield
            for qg in range(4):
                b = 1 + qg % 2
                for jj in range(4):
                    qc = qg * 4 + jj
                    for c in range(8):
                        P.op("pe", lambda e: e.matmul(psf(b)[:, jj * 128:(jj + 1) * 128], lhsT=wpq[:, c, qc * 128:(qc + 1) * 128],
                                                      rhs=h2T[:, c, :], start=(c == 0), stop=(c == 7)),
                             reads=["wpq", "h2T"] if c in (0, 7) else [], writes=[("ps", b)])
                evac(qT[:, qg * 4:(qg + 1) * 4, :], psf(b).rearrange("p (a k) -> p a k", a=4), [("ps", b)], [("qT", qg)],
                     eng="act")
            for qg in range(4):
                b = 3
                for jj in range(4):
                    qc = qg * 4 + jj
                    P.op("pe", lambda e: e.matmul(psf(b)[:, jj * 128:(jj + 1) * 128], lhsT=qT[:, qc, :], rhs=subkT[:, qc, :],
                                                  start=True, stop=True),
                         reads=[("qT", qg), "subkT"], writes=[("ps", b)])
                evac(sc[:, qg * 4:(qg + 1) * 4, :], psf(b).rearrange("p (a k) -> p a k", a=4), [("ps", b)], [("sc", qg)],
                     eng="act")
            yield
            for qc in range(16):
                k_sc = ("sc", qc // 4)
                P.op("dve", lambda e: e.max(out=top[:, qc, 0:8], in_=sc[:, qc, :]), reads=[k_sc], writes=[("top", qc)])
                P.op("dve", lambda e: e.max_index(out=idx[:, qc, 0:8], in_max=top[:, qc, 0:8], in_values=sc[:, qc, :]),
                     reads=[k_sc, ("top", qc)], writes=[("idx", qc)])
                P.op("dve", lambda e: e.match_replace(out=scw[:, qc, :], in_to_replace=top[:, qc, 0:8], in_values=sc[:, qc, :],
                                                      imm_value=-1e30), reads=[k_sc, ("top", qc)], writes=[("scw", qc)])
                P.op("dve", lambda e: e.max(out=top[:, qc, 8:16], in_=scw[:, qc, :]), reads=[("scw", qc)], writes=[("top", qc)])
                P.op("dve", lambda e: e.max_index(out=idx[:, qc, 8:16], in_max=top[:, qc, 8:16], in_values=scw[:, qc, :]),
                     reads=[("scw", qc), ("top", qc)], writes=[("idx", qc)])
                if qc % 2 == 1:
                    yield
            TOPK = [("top", qc) for qc in range(16)]
            IDXK = [("idx", qc) for qc in range(16)]
            top_v = top[:].rearrange("p (h t) k -> p h t k", t=2)
            P.op("dve", lambda e: e.tensor_tensor(
                out=cand[:].rearrange("p h (a b) -> p h a b", a=16),
                in0=top_v[:, :, 0, :].unsqueeze(3).broadcast_to([128, 8, 16, 16]),
                in1=top_v[:, :, 1, :].unsqueeze(2).broadcast_to([128, 8, 16, 16]), op=ALU.add),
                reads=TOPK, writes=["cand"])
            for hd in range(8):
                P.op("dve", lambda e: e.max(out=ctop[:, hd, 0:8], in_=cand[:, hd, :]), reads=["cand"], writes=[("ctop", hd)])
                P.op("dve", lambda e: e.max_index(out=cpos[:, hd, 0:8], in_max=ctop[:, hd, 0:8], in_values=cand[:, hd, :]),
                     reads=["cand", ("ctop", hd)], writes=[("cpos", hd)])
                P.op("dve", lambda e: e.match_replace(out=cwk[:, hd, :], in_to_replace=ctop[:, hd, 0:8], in_values=cand[:, hd, :],
                                                      imm_value=-1e30), reads=["cand", ("ctop", hd)], writes=[("scw", 2 * hd), ("scw", 2 * hd + 1)])
                P.op("dve", lambda e: e.max(out=ctop[:, hd, 8:16], in_=cwk[:, hd, :]), reads=[("scw", 2 * hd), ("scw", 2 * hd + 1)], writes=[("ctop", hd)])
                P.op("dve", lambda e: e.max_index(out=cpos[:, hd, 8:16], in_max=ctop[:, hd, 8:16], in_values=cwk[:, hd, :]),
                     reads=[("scw", 2 * hd), ("scw", 2 * hd + 1), ("ctop", hd)], writes=[("cpos", hd)])
                if hd % 2 == 1:
                    yield
            CTOP = [("ctop", hd) for hd in range(8)]
            CPOS = [("cpos", hd) for hd in range(8)]
            gate_v = gate[:, p_, :].rearrange("p (h k) -> p h k", h=8)
            P.op("dve", lambda e: e.tensor_tensor(out=gate_v, in0=ctop[:], in1=ctop[:, :, 0:1].broadcast_to([128, 8, 16]),
                                                  op=ALU.subtract), reads=CTOP, writes=[K("gate")])
            P.op("act", lambda e: e.activation(out=gate[:, p_, :], in_=gate[:, p_, :], func=AF.Exp), reads=[K("gate")], writes=[K("gate")])
            P.op("dve", lambda e: e.tensor_reduce(out=gsum[:, 0, :], in_=gate_v, axis=AX.X, op=ALU.add),
                 reads=[K("gate")], writes=["gsum"])
            P.op("dve", lambda e: e.reciprocal(out=gsum[:, 1, :], in_=gsum[:, 0, :]), reads=["gsum"], writes=["gsum"])
            P.op("dve", lambda e: e.tensor_tensor(out=gate_v, in0=gate_v,
                                                  in1=gsum[:, 1, :].unsqueeze(2).broadcast_to([128, 8, 16]), op=ALU.mult),
                 reads=[K("gate"), "gsum"], writes=[K("gate")])
            yield
            cpf = cpos[:].rearrange("p h k -> p (h k)")
            P.op("dve", lambda e: e.tensor_single_scalar(out=abu[:, 0, :], in_=cpf, scalar=4, op=ALU.logical_shift_right),
                 reads=CPOS, writes=["abu0"])
            P.op("dve", lambda e: e.tensor_single_scalar(out=abu[:, 1, :], in_=cpf, scalar=15, op=ALU.bitwise_and),
                 reads=CPOS, writes=["abu1"])
            P.op("dve", lambda e: e.tensor_copy(out=abf[:], in_=abu[:]), reads=["abu0", "abu1"], writes=["abf"])
            P.op("dve", lambda e: e.tensor_copy(out=idxf[:], in_=idx[:]), reads=IDXK, writes=["idxf"])
            idxf_v = idxf[:].rearrange("p (h t) k -> p h t k", t=2)
            oh_v = oh[:].rearrange("p h (k a) -> p h k a", k=16)
            for t_ in range(2):
                P.op("dve", lambda e: e.tensor_tensor(
                    out=oh_v, in0=abf[:, t_, :].rearrange("p (h k) -> p h k", h=8).unsqueeze(3).broadcast_to([128, 8, 16, 16]),
                    in1=iota16[:, :].unsqueeze(1).unsqueeze(1).broadcast_to([128, 8, 16, 16]), op=ALU.is_equal),
                    reads=["abf", "iota16"], writes=[("sc", q_) for q_ in range(4)])
                P.op("dve", lambda e: e.tensor_tensor(
                    out=oh_v, in0=oh_v, in1=idxf_v[:, :, t_, :].unsqueeze(2).broadcast_to([128, 8, 16, 16]), op=ALU.mult),
                    reads=[("sc", q_) for q_ in range(4)] + ["idxf"], writes=[("sc", q_) for q_ in range(4)])
                P.op("dve", lambda e: e.tensor_reduce(out=isel[:, t_, :].rearrange("p (h k) -> p h k", h=8), in_=oh_v,
                                                      axis=AX.X, op=ALU.add), reads=[("sc", q_) for q_ in range(4)], writes=[("isel", t_)])
                yield
            P.op("dve", lambda e: e.scalar_tensor_tensor(out=eidxf[:], in0=isel[:, 0, :], scalar=128.0, in1=isel[:, 1, :],
                                                         op0=ALU.mult, op1=ALU.add),
                 reads=[("isel", 0), ("isel", 1)], writes=["eidxf"])
            P.op("dve", lambda e: e.tensor_copy(out=eidx[:, p_, :], in_=eidxf[:]), reads=["eidxf"], writes=[K("eidx")])

        cvt_h = cvt_state["h"]
        for _ in topk_ops(0):
            pass
        if debug == "peer_idx":
            d1 = dbg_out("eidx", [128, 128])
            d2 = dbg_out("gate", [128, 128])
            P.dma("sp", lambda e: e.dma_start(out=d1, in_=eidxf[:]), "dbg1", reads=["eidxf"])
            P.dma("sp", lambda e: e.dma_start(out=d2, in_=gate[:, 0, :]), "dbg2", reads=[("gate", 0)])
            P.barrier(final=True)
            return nc, P
        NG = 128 // GS
        TOT = NT * NG
        DLY = 1
        ginfo = {}
        gen = {"g": None}

        def peer_gather(G):
            tt, grp = G // NG, G % NG
            p_ = tt % 2
            slots = []
            for k_ in range(GS):
                s_ = grp * GS + k_
                r = (G * GS + k_) % NR
                slots.append((s_, r))
                P.dma("pool", lambda e: e.indirect_dma_start(
                    out=UV[:, r, :], out_offset=None, in_=uvb,
                    in_offset=bass.IndirectOffsetOnAxis(ap=eidx[:, p_, s_:s_ + 1], axis=0)),
                    ("UV", r), reads=[("eidx", p_)], writes=[("UV", r)], extra=[cvt_h])
            ginfo[G] = [slots, None, None]

        def peer_dots(G, k0, k1):
            tt, grp = G // NG, G % NG
            p_ = tt % 2
            info = ginfo[G]
            for (s_, r) in info[0][k0:k1]:
                if FUSE_EVERY and s_ % FUSE_EVERY == 0:
                    jd = 0
                    jcnt[0] += 1
                    info[1] = P.op("dve", lambda e: e.scalar_tensor_tensor(out=junkD[:, jd, :], in0=UV[:, r, 0:1024], scalar=1.0,
                                                                           in1=h2[:, p_, :], op0=ALU.mult, op1=ALU.mult,
                                                                           accum_out=apre[:, p_, s_:s_ + 1]),
                                   reads=[("UV", r), ("h2", p_)], writes=[("junkD", jd), ("apre", p_, s_)])
                else:
                    pr = pcnt[0] % 4
                    jb = pcnt[0] % 2
                    pcnt[0] += 1
                    P.op("dve", lambda e: e.tensor_tensor(out=prodr[:, pr, :], in0=UV[:, r, 0:1024], in1=h2b[:, p_, :],
                                                          op=ALU.mult),
                         reads=[("UV", r), ("h2b", p_)], writes=[("prodr", pr)])
                    info[2] = P.op("act", lambda e: e.activation(out=junkB[:, jb, :], in_=prodr[:, pr, :], func=AF.Copy,
                                                                 accum_out=apre[:, p_, s_:s_ + 1], saturate=False),
                                   reads=[("prodr", pr)], writes=[("junkB", jb), ("apre", p_, s_)])

        def peer_gelu(G):
            tt, grp = G // NG, G % NG
            p_ = tt % 2
            gsl = slice(grp * GS, (grp + 1) * GS)
            P.op("act", lambda e: e.activation(out=wgt[:, p_, gsl], in_=apre[:, p_, gsl], func=AF.Gelu),
                 extra=[ginfo[G][1], ginfo[G][2]], reads=[("apre", p_, s_) for s_ in range(grp * GS, (grp + 1) * GS)],
                 writes=[("wgt", p_, grp)])

        def peer_back(G):
            tt, grp = G // NG, G % NG
            p_ = tt % 2
            ab = 4 + 2 * p_
            slots = ginfo.pop(G)[0]
            gsl = slice(grp * GS, (grp + 1) * GS)
            P.op("dve", lambda e: e.tensor_tensor(out=wgt[:, p_, gsl], in0=wgt[:, p_, gsl], in1=gate[:, p_, gsl], op=ALU.mult),
                 reads=[("wgt", p_, grp), ("gate", p_)], writes=[("wgt", p_, grp)])
            for (s_, r) in slots:
                dsl = s_ % 16
                P.op("act", lambda e: e.activation(out=diag[:, dsl, :], in_=ident[:], func=AF.Copy,
                                                   scale=wgt[:, p_, s_:s_ + 1]),
                     reads=[("wgt", p_, grp), "ident"], writes=[("diag", dsl)])
                for half in range(2):
                    P.op("pe", lambda e: e.matmul(psf(ab + half), lhsT=diag[:, dsl, :],
                                                  rhs=UV[:, r, 1024 + half * 512:1024 + (half + 1) * 512],
                                                  start=(s_ == 0), stop=(s_ == 127)),
                         reads=[("diag", dsl), ("UV", r)], writes=[("ps", ab + half)])
            if grp == NG - 1:
                for half in range(2):
                    P.op("dve", lambda e: e.tensor_tensor(out=outt[:, half * 512:(half + 1) * 512], in0=psf(ab + half),
                                                          in1=x1t[:, p_, half * 512:(half + 1) * 512], op=ALU.add),
                         reads=[("ps", ab + half), ("x1t", p_)], writes=[("outt", half)])
                P.dma("sp", lambda e: e.dma_start(out=out[tt * 128:(tt + 1) * 128, :], in_=outt[:]), "outd",
                      reads=[("outt", 0), ("outt", 1)], writes=[("out", tt)])

        HALF = GS // 2
        if debug == "peer_gatheronly":
            peer_dots = lambda *a: None
            peer_gelu = lambda *a: None
            peer_back = lambda G: ginfo.pop(G)
        for G in range(TOT + 1):
            tt, grp = G // NG, G % NG
            if G < TOT:
                if grp == 1 and tt + 1 < NT:
                    gen["g"] = topk_ops(tt + 1)
                peer_gather(G)
                peer_dots(G, 0, HALF)
            if G >= 1:
                peer_back(G - 1)
            if G < TOT:
                peer_dots(G, HALF, GS)
                peer_gelu(G)
                if gen["g"] is not None and grp >= 1:
                    if grp == NG - 1:
                        for _ in gen["g"]:
                            pass
                        gen["g"] = None
                    else:
                        next(gen["g"], None)
        P.barrier(final=True)

    P.barrier(final=True)
    return nc, P


_CACHE = {}


def kernel(**inputs):
    if "nc" not in _CACHE:
        _CACHE["nc"] = build_program()[0]
    nc = _CACHE["nc"]
    n = 8
    in_maps = []
    for b in range(n):
        m = {}
        for k, v in inputs.items():
            a = np.asarray(v)
            if k in ("x", "mem"):
                a = a[b]
            m[k] = np.ascontiguousarray(a, dtype=np.float32)
        in_maps.append(m)
    res = run_bass_kernel_spmd(nc, in_maps, core_ids=list(range(n)))
    return np.stack([np.asarray(r["out"], dtype=np.float32) for r in res.results], axis=0)
```
